# Optimizing a Trainium2 kernel written in Bass

```python
import math
import jax
import jax.numpy as jnp
from jax import lax
import numpy as np

D_MODEL = 2048
BATCH = 4
SEQ = 4096
DEPTH = 4

D_MIX = D_MODEL
A_WIDTH = D_MIX // 4
A_GROUPS = 4
A_GROUP_DIM = A_WIDTH // A_GROUPS
SGU_CHUNK = 128
R_WIDTH = D_MIX // 4
R_HEADS = 4
R_HEAD_DIM = R_WIDTH // R_HEADS
RET_CHUNK = 128
C_WIDTH = D_MIX // 2
MLA_HEADS = 8
MLA_V_DIM = C_WIDTH // MLA_HEADS
MLA_NOPE = 128
MLA_ROPE = 64
Q_LORA = D_MODEL // 4
KV_LORA = D_MODEL // 8
Q_BLOCK = 128
MEM_LEN = 256
XA_HEADS = 4
XA_HEAD_DIM = D_MODEL // XA_HEADS
D_FF = ((8 * D_MODEL // 3 + 255) // 256) * 256
ROPE_BASE = 10000.0
EPS = 1e-6
IN_SIZES = (A_WIDTH, A_WIDTH,
            R_WIDTH, R_WIDTH, R_WIDTH, R_WIDTH,
            Q_LORA, KV_LORA, MLA_ROPE)
IN_COLS = sum(IN_SIZES)

kernel_name = 'hybrid_gmlp_retention_mla_macaron'


def _split_points():
    pts, acc = [], 0
    for s in IN_SIZES[:-1]:
        acc += s
        pts.append(acc)
    return pts


def rmsnorm(x, g):
    xf = x.astype(jnp.float32)
    y = xf * lax.rsqrt(jnp.mean(xf * xf, axis=-1, keepdims=True) + EPS)
    return (y * g.astype(jnp.float32)).astype(x.dtype)


def swiglu(x, w_gate, w_up, w_down):
    return (jax.nn.silu(x @ w_gate) * (x @ w_up)) @ w_down


def rope(x):
    S, dim = x.shape[1], x.shape[-1]
    half = dim // 2
    pos = jnp.arange(S, dtype=jnp.float32)
    inv_freq = ROPE_BASE ** (-jnp.arange(half, dtype=jnp.float32) * 2.0 / dim)
    ang = pos[:, None] * inv_freq[None, :]
    shape = (1, S) + (1,) * (x.ndim - 3) + (half,)
    cos = jnp.cos(ang).reshape(shape)
    sin = jnp.sin(ang).reshape(shape)
    xf = x.astype(jnp.float32)
    x1, x2 = xf[..., :half], xf[..., half:]
    return jnp.concatenate([x1 * cos - x2 * sin, x2 * cos + x1 * sin], axis=-1).astype(x.dtype)


def sgu_mixer(u, v, norm_g, w_s, b):
    B, S, _ = u.shape
    N = S // SGU_CHUNK
    u = jax.nn.gelu(u)
    v = rmsnorm(jax.nn.gelu(v), norm_g)
    v = v.reshape(B, N, SGU_CHUNK, A_GROUPS, A_GROUP_DIM)
    w = w_s * jnp.tril(jnp.ones((SGU_CHUNK, SGU_CHUNK), w_s.dtype))[None]
    z = jnp.einsum('gts,bnsgc->bntgc', w, v) + b.T[:, :, None]
    return u * z.reshape(B, S, A_WIDTH)


def retention_chunkwise(q, k, v):
    B, S, H, dk = q.shape
    dv = v.shape[-1]
    C = RET_CHUNK
    N = S // C
    log_g = jnp.log1p(-jnp.exp2(-5.0 - jnp.arange(H, dtype=jnp.float32)))
    i = jnp.arange(C, dtype=jnp.float32)
    diff = i[:, None] - i[None, :]
    dmask = jnp.where(diff[None] >= 0,
                      jnp.exp(jnp.maximum(diff, 0.0)[None] * log_g[:, None, None]),
                      0.0).astype(q.dtype)
    zeta = jnp.exp((C - 1 - i)[None, :] * log_g[:, None]).T.astype(q.dtype)
    xi = jnp.exp((i + 1)[None, :] * log_g[:, None]).T.astype(q.dtype)
    g_chunk = jnp.exp(C * log_g).astype(q.dtype)

    q = q.reshape(B, N, C, H, dk)
    k = k.reshape(B, N, C, H, dk) * (dk ** -0.5)
    v = v.reshape(B, N, C, H, dv)

    scores = jnp.einsum('bnihd,bnjhd->bnhij', q, k) * dmask
    inner = jnp.einsum('bnhij,bnjhe->bnihe', scores, v)

    kv = jnp.einsum('bnjhd,bnjhe->nbhde', k * zeta[:, :, None], v)

    def step(state, kv_n):
        return g_chunk[None, :, None, None] * state + kv_n, state

    _, prev = lax.scan(step, jnp.zeros((B, H, dk, dv), kv.dtype), kv)
    cross = jnp.einsum('bnihd,nbhde->bnihe', q * xi[:, :, None], prev)
    return (inner + cross).reshape(B, S, H, dv)


def retention_mixer(q, k, v, g, gn):
    B, S, _ = q.shape
    q = rope(q.reshape(B, S, R_HEADS, R_HEAD_DIM))
    k = rope(k.reshape(B, S, R_HEADS, R_HEAD_DIM))
    v = v.reshape(B, S, R_HEADS, R_HEAD_DIM)
    y = retention_chunkwise(q, k, v).astype(jnp.float32)
    mu = jnp.mean(y, axis=-1, keepdims=True)
    var = jnp.mean(jnp.square(y - mu), axis=-1, keepdims=True)
    y = ((y - mu) * lax.rsqrt(var + EPS)).reshape(B, S, R_WIDTH) * gn.astype(jnp.float32)
    return (jax.nn.silu(g.astype(jnp.float32)) * y).astype(g.dtype)


def causal_mla_attention(q_nope, q_rope, k_nope, k_rope, v):
    B, S, H, _ = q_nope.shape
    nb = S // Q_BLOCK
    scale = (MLA_NOPE + MLA_ROPE) ** -0.5
    qn_b = q_nope.reshape(B, nb, Q_BLOCK, H, -1).transpose(1, 0, 2, 3, 4)
    qr_b = q_rope.reshape(B, nb, Q_BLOCK, H, -1).transpose(1, 0, 2, 3, 4)
    kpos = jnp.arange(S)

    def one_block(args):
        qn, qr, bi = args
        qpos = bi * Q_BLOCK + jnp.arange(Q_BLOCK)
        s = (jnp.einsum('bqhd,bkhd->bhqk', qn, k_nope)
             + jnp.einsum('bqhr,bkr->bhqk', qr, k_rope)).astype(jnp.float32) * scale
        s = jnp.where(kpos[None, :] <= qpos[:, None], s, jnp.float32(-1e30))
        p = jax.nn.softmax(s, axis=-1).astype(v.dtype)
        return jnp.einsum('bhqk,bkhd->bqhd', p, v)

    out = lax.map(one_block, (qn_b, qr_b, jnp.arange(nb)))
    return out.transpose(1, 0, 2, 3, 4).reshape(B, S, H, -1)


def mla_mixer(c_q, c_kv, k_rope, q_norm, w_uq, kv_norm, w_ukv):
    B, S, _ = c_q.shape
    q = (rmsnorm(c_q, q_norm) @ w_uq).reshape(B, S, MLA_HEADS, MLA_NOPE + MLA_ROPE)
    q_nope, q_rope = q[..., :MLA_NOPE], rope(q[..., MLA_NOPE:])
    kv = (rmsnorm(c_kv, kv_norm) @ w_ukv).reshape(B, S, MLA_HEADS, MLA_NOPE + MLA_V_DIM)
    k_nope, v = kv[..., :MLA_NOPE], kv[..., MLA_NOPE:]
    o = causal_mla_attention(q_nope, q_rope, k_nope, rope(k_rope), v)
    return o.reshape(B, S, C_WIDTH)


def memory_cross_attention(n, m, wq, wkv, wo):
    B, S, _ = n.shape
    M = m.shape[1]
    q = (n @ wq).reshape(B, S, XA_HEADS, XA_HEAD_DIM)
    kv = (m @ wkv).reshape(B, M, 2, XA_HEADS, XA_HEAD_DIM)
    s = jnp.einsum('bshd,bmhd->bhsm', q, kv[:, :, 0]).astype(jnp.float32) * (XA_HEAD_DIM ** -0.5)
    p = jax.nn.softmax(s, axis=-1).astype(n.dtype)
    o = jnp.einsum('bhsm,bmhd->bshd', p, kv[:, :, 1]).reshape(B, S, D_MODEL)
    return o @ wo


def setup_inputs(seed: int = 0) -> dict:
    key = jax.random.key(seed)
    ks = jax.random.split(key, 27)
    f32 = jnp.float32
    L = DEPTH

    def w(k, shape, fan_in):
        return jax.random.normal(k, shape, f32) * (fan_in ** -0.5)

    def gain(k, shape):
        return 1.0 + 0.02 * jax.random.normal(k, shape, f32)

    return {
        'x': jax.random.normal(ks[0], (BATCH, SEQ, D_MODEL), f32),
        'mem': jax.random.normal(ks[1], (BATCH, MEM_LEN, D_MODEL), f32),
        'ffn1_norm': gain(ks[2], (L, D_MODEL)),
        'ffn1_w_gate': w(ks[3], (L, D_MODEL, D_FF), D_MODEL),
        'ffn1_w_up': w(ks[4], (L, D_MODEL, D_FF), D_MODEL),
        'ffn1_w_down': w(ks[5], (L, D_FF, D_MODEL), D_FF),
        'mix_norm': gain(ks[6], (L, D_MODEL)),
        'w_in': w(ks[7], (L, D_MODEL, IN_COLS), D_MODEL),
        'sgu_norm': gain(ks[8], (L, A_WIDTH)),
        'sgu_w_s': w(ks[9], (L, A_GROUPS, SGU_CHUNK, SGU_CHUNK), SGU_CHUNK),
        'sgu_b': gain(ks[10], (L, A_GROUPS, SGU_CHUNK)),
        'ret_gn': gain(ks[11], (L, R_WIDTH)),
        'q_norm': gain(ks[12], (L, Q_LORA)),
        'w_uq': w(ks[13], (L, Q_LORA, MLA_HEADS * (MLA_NOPE + MLA_ROPE)), Q_LORA),
        'kv_norm': gain(ks[14], (L, KV_LORA)),
        'w_ukv': w(ks[15], (L, KV_LORA, MLA_HEADS * (MLA_NOPE + MLA_V_DIM)), KV_LORA),
        'w_out': w(ks[16], (L, D_MIX, D_MODEL), D_MIX),
        'xa_norm': gain(ks[17], (L, D_MODEL)),
        'mem_norm': gain(ks[18], (L, D_MODEL)),
        'xa_wq': w(ks[19], (L, D_MODEL, D_MODEL), D_MODEL),
        'xa_wkv': w(ks[20], (L, D_MODEL, 2 * D_MODEL), D_MODEL),
        'xa_wo': w(ks[21], (L, D_MODEL, D_MODEL), D_MODEL),
        'ffn2_norm': gain(ks[22], (L, D_MODEL)),
        'ffn2_w_gate': w(ks[23], (L, D_MODEL, D_FF), D_MODEL),
        'ffn2_w_up': w(ks[24], (L, D_MODEL, D_FF), D_MODEL),
        'ffn2_w_down': w(ks[25], (L, D_FF, D_MODEL), D_FF),
        'final_norm': gain(ks[26], (D_MODEL,)),
    }


def reference(x, mem, ffn1_norm, ffn1_w_gate, ffn1_w_up, ffn1_w_down, mix_norm, w_in,
              sgu_norm, sgu_w_s, sgu_b, ret_gn, q_norm, w_uq, kv_norm, w_ukv, w_out,
              xa_norm, mem_norm, xa_wq, xa_wkv, xa_wo, ffn2_norm, ffn2_w_gate, ffn2_w_up,
              ffn2_w_down, final_norm):
    pts = _split_points()
    h = x
    for l in range(DEPTH):
        h = h + 0.5 * swiglu(rmsnorm(h, ffn1_norm[l]), ffn1_w_gate[l], ffn1_w_up[l], ffn1_w_down[l])
        n = rmsnorm(h, mix_norm[l])
        proj = n @ w_in[l]
        a_u, a_v, r_q, r_k, r_v, r_g, c_q, c_kv, c_kr = jnp.split(proj, pts, axis=-1)
        y_a = sgu_mixer(a_u, a_v, sgu_norm[l], sgu_w_s[l], sgu_b[l])
        y_r = retention_mixer(r_q, r_k, r_v, r_g, ret_gn[l])
        y_c = mla_mixer(c_q, c_kv, c_kr, q_norm[l], w_uq[l], kv_norm[l], w_ukv[l])
        h = h + jnp.concatenate([y_a, y_r, y_c], axis=-1) @ w_out[l]
        h = h + memory_cross_attention(rmsnorm(h, xa_norm[l]), rmsnorm(mem, mem_norm[l]),
                                       xa_wq[l], xa_wkv[l], xa_wo[l])
        h = h + 0.5 * swiglu(rmsnorm(h, ffn2_norm[l]), ffn2_w_gate[l], ffn2_w_up[l], ffn2_w_down[l])
    return rmsnorm(h, final_norm)
```

```python
import math
import numpy as np
import ml_dtypes
import concourse.bass as bass
import concourse.mybir as mybir
from concourse.bass_utils import run_bass_kernel_spmd

F32 = mybir.dt.float32
BF16 = mybir.dt.bfloat16
AF = mybir.ActivationFunctionType
ALU = mybir.AluOpType

D_MODEL = 2048
SEQ = 4096
BATCH = 4
DEPTH = 4
D_FF = 5632
T = 1024
EPS = 1e-6
NV = 106

ENGS = ["pe", "act", "dve", "pool", "sp"]
EPOCH = 20000


def I(method, **kw):
    return lambda e: getattr(e, method)(**kw)


class Buf:
    __slots__ = ("name", "last_w", "readers")

    def __init__(self, name, hazard=None):
        self.name = name
        self.last_w = None
        self.readers = dict(hazard) if hazard else {}


class Op:
    __slots__ = ("eng", "fn", "waits", "ord", "needs_inc", "sem_i", "val", "is_dma", "dkey")


class Sched:
    def __init__(self):
        self.ops = {e: [] for e in ENGS}
        self.seen = {e: {} for e in ENGS}
        self.dma_n = {}
        self.units = {}
        self.hazard = {}

    def tk(self, unit, *key):
        d = self.units.setdefault(unit, {})
        b = d.get(key)
        if b is None:
            b = Buf(f"{unit}{key}", self.hazard.get(unit))
            d[key] = b
        return b

    def reset_unit(self, unit):
        hz = dict(self.hazard.get(unit, {}))
        for b in self.units.get(unit, {}).values():
            cands = list(b.readers.values())
            if b.last_w is not None:
                cands.append(b.last_w)
            for op in cands:
                k = ("dma", op.dkey) if op.is_dma else op.eng
                if k not in hz or hz[k].ord < op.ord:
                    hz[k] = op
        self.hazard[unit] = hz
        self.units[unit] = {}

    def add(self, eng, fn, reads=(), writes=(), dma_key=None):
        op = Op()
        op.eng = eng
        op.fn = fn
        op.needs_inc = False
        op.is_dma = dma_key is not None
        op.dkey = dma_key
        op.sem_i = 0
        op.val = 0
        if op.is_dma:
            n = self.dma_n.get(dma_key, 0) + 1
            self.dma_n[dma_key] = n
            op.ord = n
        else:
            op.ord = len(self.ops[eng])
        deps = []
        for b in reads:
            if b.last_w is not None:
                d = b.last_w
                if d.is_dma or op.is_dma or not (d.eng == eng and eng == "pe"):
                    deps.append(d)
        for b in writes:
            cands = list(b.readers.values())
            if b.last_w is not None:
                cands.append(b.last_w)
            for d in cands:
                if d.is_dma or op.is_dma or d.eng != eng:
                    deps.append(d)
        waits = {}
        seen = self.seen[eng]
        for d in deps:
            if d is op:
                continue
            key = ("dma", d.dkey) if d.is_dma else ("eng", d.eng)
            if seen.get(key, -1) >= d.ord:
                continue
            if key not in waits or waits[key].ord < d.ord:
                waits[key] = d
        for key, d in waits.items():
            seen[key] = d.ord
            d.needs_inc = True
        op.waits = list(waits.values())
        rk = ("dma", dma_key) if op.is_dma else eng
        for b in reads:
            b.readers[rk] = op
        for b in writes:
            b.last_w = op
            b.readers = {}
        self.ops[eng].append(op)
        return op

    def emit(self, nc, final_waits=()):
        nsem = {}
        for e in ENGS:
            c = 0
            for op in self.ops[e]:
                if op.is_dma or not op.needs_inc:
                    continue
                c += 1
                op.sem_i = (c - 1) // EPOCH
                op.val = (c - 1) % EPOCH + 1
            nsem[e] = (c + EPOCH - 1) // EPOCH if c else 0
        sems = {}
        for e in ENGS:
            for i in range(max(nsem[e], 1)):
                sems[("eng", e, i)] = nc.alloc_semaphore(name=f"s_{e}_{i}")
        for k in self.dma_n:
            sems[("dma", k)] = nc.alloc_semaphore(name=f"d_{k}")

        def sem_of(d):
            if d.is_dma:
                return sems[("dma", d.dkey)], 16 * d.ord
            return sems[("eng", d.eng, d.sem_i)], d.val

        def run(e, engine):
            for op in self.ops[e]:
                for d in op.waits:
                    s, v = sem_of(d)
                    engine.wait_ge(s, v)
                ins = op.fn(engine)
                if op.is_dma:
                    ins.then_inc(sems[("dma", op.dkey)], 16)
                elif op.needs_inc:
                    ins.then_inc(sems[("eng", e, op.sem_i)], 1)
            if e == "sp":
                for d in final_waits:
                    s, v = sem_of(d)
                    engine.wait_ge(s, v)

        with nc.Block() as block:
            @block.tensor
            def _(eng):
                run("pe", eng)

            @block.scalar
            def _(eng):
                run("act", eng)

            @block.vector
            def _(eng):
                run("dve", eng)

            @block.gpsimd
            def _(eng):
                run("pool", eng)

            @block.sync
            def _(eng):
                run("sp", eng)
        return {e: len(self.ops[e]) for e in ENGS}


V_FFN1, V_MIX, V_XA, V_MEM, V_FFN2, V_QN, V_KVN, V_GN, V_FIN = 0, 16, 32, 48, 64, 80, 84, 86, 90


def build(depth=DEPTH, npass=SEQ // T, use_gelu_tanh_lut=True, phases="fabcxg"):
    nc = bass.Bass("TRN2", target_bir_lowering=False)
    S = Sched()
    L = depth
    NTOK = npass * T

    def din(name, shape, dt=F32):
        return nc.dram_tensor(name, list(shape), dt, kind="ExternalInput").ap()

    xT = din("xT", [D_MODEL, NTOK])
    memT = din("memT", [D_MODEL, 256])
    yT = nc.dram_tensor("yT", [D_MODEL, NTOK], F32, kind="ExternalOutput").ap()
    w_g1 = din("w_g1", [L, D_MODEL, D_FF]); w_u1 = din("w_u1", [L, D_MODEL, D_FF]); w_d1 = din("w_d1", [L, D_FF, D_MODEL])
    w_g2 = din("w_g2", [L, D_MODEL, D_FF]); w_u2 = din("w_u2", [L, D_MODEL, D_FF]); w_d2 = din("w_d2", [L, D_FF, D_MODEL])
    w_in = din("w_in", [L, D_MODEL, 5120])
    w_uq = din("w_uq", [L, 512, 2048])
    w_ukv = din("w_ukv", [L, 256, 2048])
    w_out = din("w_out", [L, D_MODEL, D_MODEL])
    xa_wq = din("xa_wq", [L, D_MODEL, D_MODEL])
    xa_wkv = din("xa_wkv", [L, D_MODEL, 4096])
    xa_wo = din("xa_wo", [L, D_MODEL, D_MODEL])
    vecs_d = din("vecs", [128, L, NV])
    sgu_rep_d = din("sgu_rep", [L, 128, 512])
    sgu_wsT_d = din("sgu_wsT", [L, 128, 4, 128])
    sgu_b_d = din("sgu_b", [L, 1, 512])
    c_ident = din("c_ident", [128, 128]); c_ones = din("c_ones", [128, 128]); c_tri = din("c_tri", [128, 128])
    c_dmask = din("c_dmask", [128, 4, 128]); c_xi = din("c_xi", [128, 4, 128]); c_zeta = din("c_zeta", [128, 4])
    c_cosR = din("c_cosR", [128, SEQ]); c_sinR = din("c_sinR", [128, SEQ])
    c_cosM = din("c_cosM", [128, SEQ]); c_sinM = din("c_sinM", [128, SEQ])
    lat_cache = nc.dram_tensor("lat_cache", [L, 128, 3, NTOK], BF16, kind="Internal").ap()
    st_cache = nc.dram_tensor("st_cache", [L, 128, 4, 128], F32, kind="Internal").ap()

    def sb(name, shape, dt):
        return nc.alloc_sbuf_tensor("sb_" + name, list(shape), dt).ap()

    hT = sb("hT", [128, 16, T], F32)
    nT = sb("nT", [128, 16, T], BF16)
    bx = sb("bx", [128, 6, 4096], BF16)
    ring = sb("ring", [128, 2, 4096], BF16)
    wkvb = sb("wkvb", [128, 4096], BF16)
    pt = sb("pt", [128, 4, 512], BF16)
    tab = sb("tab", [128, 2, 512], F32)
    sq = sb("sq", [128, 2, 2, 512], BF16)
    rstd = sb("rstd", [128, T], F32)
    fs = sb("fs", [128, 4, 512], F32)
    vecs = sb("vecs", [128, L, NV], F32)
    ident = sb("ident", [128, 128], BF16); ones = sb("ones", [128, 128], BF16); tri = sb("tri", [128, 128], F32)
    trib = sb("trib", [128, 128], BF16)
    dmask = sb("dmask", [128, 4, 128], F32); xi = sb("xi", [128, 4, 128], F32); zeta = sb("zeta", [128, 4], F32)
    sgurep = sb("sgurep", [128, 512], F32)
    wsm = sb("wsm", [128, 4, 128], BF16)
    brow = sb("brow", [1, 512], BF16)
    st32 = sb("st32", [128, 4, 128], F32); st16 = sb("st16", [128, 4, 128], BF16)
    small = sb("small", [128, 16], F32)
    banks = [nc.alloc_psum_tensor(f"bank{i}", [128, 512], F32).ap() for i in range(8)]

    state = {"bank": 0, "slot": 0, "ev": 0, "sq": 0, "fsr": 0, "ptr": 0}
    bank_tok = [S.tk("bank", i) for i in range(8)]

    def next_bank(lo=0, hi=8):
        state["bank"] += 1
        i = lo + state["bank"] % (hi - lo)
        return banks[i], bank_tok[i]

    def units_all():
        return [("ring", 0), ("ring", 1)] + [("bx", i) for i in range(6)]

    def slot_ap(u):
        return (ring if u[0] == "ring" else bx)[:, u[1], :]

    def slot_tok(u):
        return S.tk(u[0], "slot", u[1])

    slotsets = {"all": units_all(), "two": units_all()[:2]}

    def load_slot(src_ap, view_fn, mode):
        us = slotsets[mode]
        state["slot"] += 1
        u = us[state["slot"] % len(us)]
        ap = view_fn(slot_ap(u))
        tok = slot_tok(u)
        S.add("pool", I("dma_start", out=ap, in_=src_ap), writes=[tok], dma_key=f"w_{u[0]}{u[1]}")
        return ap, tok

    def v16(a):
        return a.rearrange("p (k n) -> p k n", k=16)

    def v4(a):
        return a.rearrange("p (k n) -> p k n", k=4)

    def v2(a):
        return a.rearrange("p (k n) -> p k n", k=2)

    def colslab(W, l, c0, ncols=256):
        return W[l].rearrange("(k p) n -> p k n", p=128)[:, :, c0:c0 + ncols]

    def rowslab(W, l, r0, nk, c0, ncols):
        return W[l][r0:r0 + nk * 128, c0:c0 + ncols].rearrange("(k p) n -> p k n", p=128)

    def evac_engine():
        state["ev"] += 1
        return "act" if state["ev"] % 2 else "dve"

    def copy_out(dst, src, reads, writes, eng=None, scale=None):
        eng = eng or evac_engine()
        if eng == "act":
            if scale is None:
                S.add("act", I("activation", out=dst, in_=src, func=AF.Copy), reads=reads, writes=writes)
            else:
                S.add("act", I("activation", out=dst, in_=src, func=AF.Copy, scale=scale), reads=reads, writes=writes)
        else:
            if scale is None:
                S.add("dve", I("tensor_copy", out=dst, in_=src), reads=reads, writes=writes)
            else:
                S.add("dve", I("tensor_scalar", out=dst, in0=src, scalar1=scale, scalar2=None, op0=ALU.mult), reads=reads, writes=writes)

    def mm_group(out_ap, out_tok, pairs, reads, start=True, stop=True):
        n = len(pairs)

        def fn(e):
            ins = None
            for i, (l_, r_) in enumerate(pairs):
                ins = e.matmul(out_ap, lhsT=l_, rhs=r_, start=(start and i == 0), stop=(stop and i == n - 1))
            return ins
        return S.add("pe", fn, reads=reads, writes=[out_tok])

    def rsqrt_to(dst, src, src_toks, dst_toks, scale, n):
        S.add("act", I("activation", out=dst, in_=src, func=AF.Sqrt, scale=scale, bias=eps_col[:, 0:1]),
              reads=src_toks, writes=dst_toks)
        S.add("dve", I("reciprocal", out=dst, in_=dst), reads=dst_toks, writes=dst_toks)

    def gelu_to(dst, src, reads, writes):
        n = src.shape[-1]
        if use_gelu_tanh_lut:
            S.add("act", I("activation", out=dst, in_=src, func=AF.Gelu_apprx_tanh), reads=reads, writes=writes)
            return
        state["fsr"] += 1
        r = state["fsr"] % 4
        tmp = fs[:, r, 0:n]
        tt = [S.tk("fs", r)]
        S.add("act", I("activation", out=tmp, in_=src, func=AF.Square), reads=reads, writes=tt)
        S.add("dve", I("tensor_scalar", out=tmp, in0=tmp, scalar1=0.044715, scalar2=1.0, op0=ALU.mult, op1=ALU.add), reads=tt, writes=tt)
        S.add("dve", I("tensor_tensor", out=tmp, in0=tmp, in1=src, op=ALU.mult), reads=tt + list(reads), writes=tt)
        S.add("act", I("activation", out=tmp, in_=tmp, func=AF.Sigmoid, scale=1.5957691216057308), reads=tt, writes=tt)
        S.add("dve", I("tensor_tensor", out=dst, in0=tmp, in1=src, op=ALU.mult), reads=tt + list(reads), writes=writes)

    ctok = S.tk("const", 0)
    eps_col = small[:, 0:1]

    def cload(dst, src, q="sp", key="c0"):
        S.add(q, I("dma_start", out=dst, in_=src), writes=[ctok], dma_key=key)

    cload(vecs, vecs_d, "sp", "c0")
    cload(tri, c_tri, "sp", "c0")
    cload(dmask, c_dmask, "sp", "c0")
    cload(xi, c_xi, "sp", "c0")
    cload(zeta, c_zeta, "sp", "c0")
    cload(ident, c_ident, "pool", "c1")
    cload(ones, c_ones, "pool", "c1")
    cload(trib, c_tri, "pool", "c1")
    S.add("dve", I("memset", ap=small[:, 0:1], constant=EPS), writes=[ctok])
    CT = [ctok]

    def norm_fm(src, src_tok_fn, nch, dtrue, ncols_list, gcol_fn, dst_fn, dst_tok_fn, in_place_f32=False):
        for (c0, n) in ncols_list:
            ssb, sst = next_bank()
            for cp in range(0, nch, 2):
                k = min(2, nch - cp)
                state["sq"] += 1
                r = state["sq"] % 2
                sqt = S.tk("sq", r)
                rd = [src_tok_fn(c, c0) for c in range(cp, cp + k)]
                S.add("act", I("activation", out=sq[:, r, 0:k, 0:n], in_=src[:, cp:cp + k, c0:c0 + n], func=AF.Square),
                      reads=rd, writes=[sqt])
                mm_group(ssb[:, 0:n], sst, [(ones[:, :], sq[:, r, j, 0:n]) for j in range(k)], reads=[sqt] + CT,
                         start=(cp == 0), stop=(cp + k >= nch))
            rt = S.tk("rstd", c0)
            rsqrt_to(rstd[:, c0:c0 + n], ssb[:, 0:n], [sst] + CT, [rt], 1.0 / dtrue, n)
            for c in range(nch):
                S.add("dve", I("scalar_tensor_tensor", out=dst_fn(c, c0, n), in0=src[:, c, c0:c0 + n], scalar=gcol_fn(c),
                                                                   in1=rstd[:, c0:c0 + n], op0=ALU.mult, op1=ALU.mult),
                      reads=[src_tok_fn(c, c0), rt] + CT, writes=[dst_tok_fn(c, c0)])

    TBS = [(0, 512), (512, 512)]

    def h_tok(c, c0):
        return S.tk("hT", c, c0 // 512)

    def n_tok(c, c0):
        return S.tk("nT", c, c0 // 512)

    def norm_h_to_n(l, vofs):
        S.reset_unit("nT")
        norm_fm(hT, h_tok, 16, D_MODEL, TBS, lambda c: vecs[:, l, vofs + c:vofs + c + 1],
                lambda c, c0, n: nT[:, c, c0:c0 + n], n_tok)

    def n_reads(tb):
        return [S.tk("nT", c, tb) for c in range(16)]

    def h_accum(bank, btok, oc, tb, scale, extra_reads=()):
        ht = S.tk("hT", oc, tb)
        dst = hT[:, oc, tb * 512:(tb + 1) * 512]
        S.add("dve", I("scalar_tensor_tensor", out=dst, in0=bank, scalar=scale, in1=dst, op0=ALU.mult, op1=ALU.add),
              reads=[btok, ht], writes=[ht])

    def ffn(l, Wg, Wu, Wd, vofs):
        norm_h_to_n(l, vofs)
        S.reset_unit("bx")
        S.reset_unit("pt")
        for s in range(D_FF // 256):
            gv, gt = load_slot(colslab(Wg, l, s * 256), v16, "all")
            uv, ut = load_slot(colslab(Wu, l, s * 256), v16, "all")
            dv, dt_ = load_slot(rowslab(Wd, l, s * 256, 2, 0, D_MODEL), v2, "all")
            for tb in range(2):
                tsl = slice(tb * 512, (tb + 1) * 512)
                state["ptr"] += 1
                pr = state["ptr"] % 2
                for j in range(2):
                    gb, gbt = next_bank()
                    mm_group(gb, gbt, [(gv[:, kc, j * 128:(j + 1) * 128], nT[:, kc, tsl]) for kc in range(16)], reads=[gt] + n_reads(tb))
                    ub, ubt = next_bank()
                    mm_group(ub, ubt, [(uv[:, kc, j * 128:(j + 1) * 128], nT[:, kc, tsl]) for kc in range(16)], reads=[ut] + n_reads(tb))
                    state["fsr"] += 1
                    r = state["fsr"] % 4
                    ft = S.tk("fs", r)
                    S.add("act", I("activation", out=fs[:, r, :], in_=gb, func=AF.Silu), reads=[gbt], writes=[ft])
                    at = S.tk("pt", pr, j)
                    S.add("dve", I("tensor_tensor", out=pt[:, pr * 2 + j, :], in0=fs[:, r, :], in1=ub, op=ALU.mult),
                          reads=[ft, ubt], writes=[at])
                for oc in range(16):
                    ob, obt = next_bank()
                    mm_group(ob, obt, [(dv[:, j, oc * 128:(oc + 1) * 128], pt[:, pr * 2 + j, :]) for j in range(2)],
                             reads=[dt_, S.tk("pt", pr, 0), S.tk("pt", pr, 1)])
                    h_accum(ob, obt, oc, tb, 0.5)

    def proj_fm(wv, wt, j, tb, src=None, src_reads=None, nk=16):
        b, bt = next_bank()
        tsl = slice(tb * 512, (tb + 1) * 512)
        src = nT if src is None else src
        rd = n_reads(tb) if src_reads is None else src_reads
        mm_group(b, bt, [(wv[:, kc, j * 128:(j + 1) * 128], src[:, kc, tsl]) for kc in range(nk)], reads=[wt] + rd)
        return b, bt

    def out_proj(l, W, r0, src_fn, src_reads_fn):
        for ch in range(2):
            wv, wt = load_slot(rowslab(W, l, r0, 4, ch * 1024, 1024), v4, "two")
            for tb in range(2):
                for o8 in range(8):
                    b, bt = next_bank()
                    mm_group(b, bt, [(wv[:, c, o8 * 128:(o8 + 1) * 128], src_fn(c, tb)) for c in range(4)],
                             reads=[wt] + src_reads_fn(tb))
                    h_accum(b, bt, ch * 8 + o8, tb, 1.0)

    X = [bx[:, i, :] for i in range(6)]

    def load_tab(cosd, sind, pos0, tb):
        tt = S.tk("tab", 0)
        S.add("sp", I("dma_start", out=tab[:, 0, :], in_=cosd[:, pos0 + tb * 512: pos0 + (tb + 1) * 512]), writes=[tt], dma_key="tab")
        S.add("sp", I("dma_start", out=tab[:, 1, :], in_=sind[:, pos0 + tb * 512: pos0 + (tb + 1) * 512]), writes=[tt], dma_key="tab")
        return tt

    def rope_out(dst, a, at, b, bt, tt, wtoks):
        state["fsr"] += 1
        r0 = state["fsr"] % 4
        state["fsr"] += 1
        r1 = state["fsr"] % 4
        t0 = S.tk("fs", r0)
        t1 = S.tk("fs", r1)
        S.add("dve", I("tensor_tensor", out=fs[:, r0, :], in0=a, in1=tab[:, 0, :], op=ALU.mult), reads=[at, tt], writes=[t0])
        S.add("dve", I("tensor_tensor", out=fs[:, r1, :], in0=b, in1=tab[:, 1, :], op=ALU.mult), reads=[bt, tt], writes=[t1])
        S.add("dve", I("tensor_tensor", out=dst, in0=fs[:, r0, :], in1=fs[:, r1, :], op=ALU.add), reads=[t0, t1], writes=wtoks)

    def mixer_a(l, p):
        S.reset_unit("bx")
        S.reset_unit("pt")
        uT = v4(X[0]); vt = X[1].rearrange("p (t n) -> p t n", t=8); yaT = v4(X[2])
        lt = S.tk("sgu_l", 0)
        S.add("sp", I("dma_start", out=sgurep, in_=sgu_rep_d[l]), writes=[lt], dma_key="sgul")
        S.add("sp", I("dma_start", out=fs[:, 3, :].rearrange("p (g t) -> p g t", g=4), in_=sgu_wsT_d[l]), writes=[S.tk("fs", 3)], dma_key="sgul2")
        S.add("pool", I("dma_start", out=brow, in_=sgu_b_d[l]), writes=[lt], dma_key="sgub")
        wt_ = S.tk("wsm", 0)
        for g in range(4):
            S.add("dve", I("tensor_tensor", out=wsm[:, g, :], in0=fs[:, 3, g * 128:(g + 1) * 128], in1=tri, op=ALU.mult),
                  reads=[S.tk("fs", 3)] + CT, writes=[wt_])
        for s in range(2):
            wv, wt = load_slot(colslab(w_in, l, s * 256), v16, "two")
            for j in range(2):
                for tb in range(2):
                    b, bt = proj_fm(wv, wt, j, tb)
                    gelu_to(uT[:, s * 2 + j, tb * 512:(tb + 1) * 512], b, [bt], [S.tk("bx", 0, s * 2 + j, tb)])
        wv2, wt2 = load_slot(colslab(w_in, l, 512), v16, "two")
        wv3, wt3 = load_slot(colslab(w_in, l, 768), v16, "two")
        for tt_ in range(8):
            b, bt = next_bank()
            mm_group(b[:, 0:256], bt, [(nT[:, kc, tt_ * 128:(tt_ + 1) * 128], wv2[:, kc, :]) for kc in range(16)], reads=[wt2] + n_reads(tt_ // 4))
            mm_group(b[:, 256:512], bt, [(nT[:, kc, tt_ * 128:(tt_ + 1) * 128], wv3[:, kc, :]) for kc in range(16)], reads=[wt3] + n_reads(tt_ // 4))
            r = tt_ % 2
            gt = S.tk("fs", r)
            gelu_to(fs[:, r, :], b, [bt], [gt])
            sst = S.tk("small", 1 + r)
            S.add("dve", I("memset", ap=small[:, 1 + r:2 + r], constant=0.0), writes=[sst])
            S.add("dve", I("scalar_tensor_tensor", out=fs[:, 2 + r, :], in0=fs[:, r, :], scalar=1.0, in1=fs[:, r, :], op0=ALU.mult, op1=ALU.mult,
                                                               accum_out=small[:, 1 + r:2 + r]), reads=[gt], writes=[sst, S.tk("fs", 2 + r)])
            rsqrt_to(small[:, 1 + r:2 + r], small[:, 1 + r:2 + r], [sst] + CT, [sst], 1.0 / 512, 1)
            S.add("dve", I("scalar_tensor_tensor", out=vt[:, tt_, :], in0=fs[:, r, :], scalar=small[:, 1 + r:2 + r], in1=sgurep,
                                                                        op0=ALU.mult, op1=ALU.mult),
                  reads=[gt, sst, lt], writes=[S.tk("bx", 1, tt_)])
        for g in range(4):
            for tb in range(2):
                b, bt = next_bank()
                for n4 in range(4):
                    n = tb * 4 + n4
                    mm_group(b[:, n4 * 128:(n4 + 1) * 128], bt,
                             [(vt[:, n, g * 128:(g + 1) * 128], wsm[:, g, :]), (ones[0:1, :], brow[0:1, g * 128:(g + 1) * 128])],
                             reads=[S.tk("bx", 1, n), wt_, lt] + CT)
                S.add("dve", I("tensor_tensor", out=yaT[:, g, tb * 512:(tb + 1) * 512], in0=b, in1=uT[:, g, tb * 512:(tb + 1) * 512], op=ALU.mult),
                      reads=[bt, S.tk("bx", 0, g, tb)], writes=[S.tk("bx", 2, g, tb)])
        out_proj(l, w_out, 0, lambda c, tb: yaT[:, c, tb * 512:(tb + 1) * 512], lambda tb: [S.tk("bx", 2, c, tb) for c in range(4)])

    def mixer_b(l, p):
        S.reset_unit("bx")
        S.reset_unit("pt")
        qT = v4(X[0]); kT = v4(X[1])
        kz = X[2].rearrange("p (n h d) -> p n h d", n=8, h=4)
        vtm = X[3].rearrange("p (t n) -> p t n", t=8)
        sgT = v4(X[4]); yrT = v4(X[5])
        gam = [1.0 - 2.0 ** (-5.0 - h) for h in range(4)]
        S.reset_unit("st")
        stt = S.tk("st", "init")
        if p == 0:
            S.add("dve", I("memset", ap=st32[:], constant=0.0), writes=[stt])
            S.add("dve", I("memset", ap=st16[:], constant=0.0), writes=[stt])
        else:
            S.add("sp", I("dma_start", out=st32[:], in_=st_cache[l]), reads=[S.tk("stc", l)], writes=[stt], dma_key="stl")
            S.add("dve", I("tensor_copy", out=st16[:], in_=st32[:]), reads=[stt], writes=[stt])
        for which, dstT, un in ((0, qT, 0), (1, kT, 1)):
            c_a = 8 + which * 8
            for hp in range(2):
                wa, wat = load_slot(colslab(w_in, l, (c_a + hp * 2) * 128), v16, "two")
                wb, wbt = load_slot(colslab(w_in, l, (c_a + 4 + hp * 2) * 128), v16, "two")
                for tb in range(2):
                    tt = load_tab(c_cosR, c_sinR, p * T, tb)
                    for j in range(2):
                        a, at = proj_fm(wa, wat, j, tb)
                        b, bt = proj_fm(wb, wbt, j, tb)
                        h = hp * 2 + j
                        rope_out(dstT[:, h, tb * 512:(tb + 1) * 512], a, at, b, bt, tt, [S.tk("bx", un, h, tb)])
        for h in range(4):
            for n2 in range(0, 8, 4):
                b, bt = next_bank()
                bb = b.bitcast(BF16)
                for n4 in range(4):
                    n = n2 + n4
                    S.add("pe", I("transpose", out=bb[:, n4 * 128:(n4 + 1) * 128], in_=kT[:, h, n * 128:(n + 1) * 128], identity=ident),
                          reads=[S.tk("bx", 1, h, n // 4)] + CT, writes=[bt])
                for n4 in range(4):
                    n = n2 + n4
                    S.add("dve", I("tensor_scalar", out=kz[:, n, h, :], in0=bb[:, n4 * 128:(n4 + 1) * 128], scalar1=zeta[:, h:h + 1], scalar2=None, op0=ALU.mult),
                          reads=[bt] + CT, writes=[S.tk("bx", 2, n, h)])
        for s in range(2):
            wv, wt = load_slot(colslab(w_in, l, (24 + s * 2) * 128), v16, "two")
            for tt_ in range(8):
                b, bt = next_bank()
                mm_group(b[:, 0:256], bt, [(nT[:, kc, tt_ * 128:(tt_ + 1) * 128], wv[:, kc, :]) for kc in range(16)], reads=[wt] + n_reads(tt_ // 4))
                copy_out(vtm[:, tt_, s * 256:(s + 1) * 256], b[:, 0:256], [bt], [S.tk("bx", 3, tt_, s)])
        for s in range(2):
            wv, wt = load_slot(colslab(w_in, l, (28 + s * 2) * 128), v16, "two")
            for j in range(2):
                for tb in range(2):
                    b, bt = proj_fm(wv, wt, j, tb)
                    S.add("act", I("activation", out=sgT[:, s * 2 + j, tb * 512:(tb + 1) * 512], in_=b, func=AF.Silu),
                          reads=[bt], writes=[S.tk("bx", 4, s * 2 + j, tb)])
        for tb in range(2):
            ybanks = []
            for h in range(4):
                yb, ybt = next_bank(4, 8)
                ybanks.append((yb, ybt))
            for n4 in range(4):
                n = tb * 4 + n4
                nsl = slice(n * 128, (n + 1) * 128)
                for h in range(4):
                    yb, ybt = ybanks[h]
                    sb_, sbt = next_bank(0, 4)
                    mm_group(sb_[:, 0:128], sbt, [(kT[:, h, nsl], qT[:, h, nsl])], reads=[S.tk("bx", 1, h, tb), S.tk("bx", 0, h, tb)])
                    state["ptr"] += 1
                    pr = state["ptr"] % 8
                    sm = pt.rearrange("p a (b c) -> p (a b) c", b=4)
                    smt = S.tk("pt", "b16", pr)
                    S.add("dve", I("tensor_tensor", out=sm[:, pr, :], in0=sb_[:, 0:128], in1=dmask[:, h, :], op=ALU.mult),
                          reads=[sbt] + CT, writes=[smt])
                    qxt = S.tk("pt", "b16", 8 + pr)
                    S.add("pool", I("tensor_tensor", out=sm[:, 8 + pr, :], in0=qT[:, h, nsl], in1=xi[:, h, :], op=ALU.mult),
                          reads=[S.tk("bx", 0, h, tb)] + CT, writes=[qxt])
                    mm_group(yb[:, n4 * 128:(n4 + 1) * 128], ybt,
                             [(vtm[:, n, h * 128:(h + 1) * 128], sm[:, pr, :]), (st16[:, h, :], sm[:, 8 + pr, :])],
                             reads=[S.tk("bx", 3, n, h // 2), smt, qxt, S.tk("st", 16, h), stt])
                    kb, kbt = next_bank(0, 4)
                    mm_group(kb[:, 0:128], kbt, [(kz[:, n, h, :], vtm[:, n, h * 128:(h + 1) * 128])], reads=[S.tk("bx", 2, n, h), S.tk("bx", 3, n, h // 2)])
                    s32t = S.tk("st", 32, h)
                    S.add("dve", I("scalar_tensor_tensor", out=st32[:, h, :], in0=st32[:, h, :], scalar=gam[h] ** 128, in1=kb[:, 0:128], op0=ALU.mult, op1=ALU.add),
                          reads=[kbt, s32t, stt], writes=[s32t])
                    S.add("act", I("activation", out=st16[:, h, :], in_=st32[:, h, :], func=AF.Copy), reads=[s32t], writes=[S.tk("st", 16, h)])
            for h in range(4):
                yb, ybt = ybanks[h]
                state["sq"] += 1
                r = state["sq"] % 2
                sqt = S.tk("sq", r)
                S.add("act", I("activation", out=sq[:, r, 0, :], in_=yb, func=AF.Copy), reads=[ybt], writes=[sqt])
                S.add("act", I("activation", out=sq[:, r, 1, :], in_=yb, func=AF.Square), reads=[ybt], writes=[sqt])
                s1, s1t = next_bank(0, 4)
                mm_group(s1, s1t, [(ones[:, :], sq[:, r, 0, :])], reads=[sqt] + CT)
                s2, s2t = next_bank(0, 4)
                mm_group(s2, s2t, [(ones[:, :], sq[:, r, 1, :])], reads=[sqt] + CT)
                f0, f1, f2 = S.tk("fs", 0), S.tk("fs", 1), S.tk("fs", 2)
                S.add("dve", I("tensor_scalar", out=fs[:, 0, :], in0=s1, scalar1=1.0 / 128, scalar2=None, op0=ALU.mult), reads=[s1t], writes=[f0])
                S.add("dve", I("tensor_tensor", out=fs[:, 1, :], in0=fs[:, 0, :], in1=fs[:, 0, :], op=ALU.mult), reads=[f0], writes=[f1])
                S.add("dve", I("scalar_tensor_tensor", out=fs[:, 1, :], in0=s2, scalar=1.0 / 128, in1=fs[:, 1, :], op0=ALU.mult, op1=ALU.subtract),
                      reads=[s2t, f1], writes=[f1])
                rsqrt_to(fs[:, 1, :], fs[:, 1, :], [f1] + CT, [f1], 1.0, 512)
                S.add("dve", I("tensor_tensor", out=fs[:, 2, :], in0=yb, in1=fs[:, 0, :], op=ALU.subtract), reads=[ybt, f0], writes=[f2])
                S.add("dve", I("scalar_tensor_tensor", out=fs[:, 2, :], in0=fs[:, 2, :], scalar=vecs[:, l, V_GN + h:V_GN + h + 1], in1=fs[:, 1, :],
                                                                   op0=ALU.mult, op1=ALU.mult), reads=[f2, f1] + CT, writes=[f2])
                S.add("dve", I("tensor_tensor", out=yrT[:, h, tb * 512:(tb + 1) * 512], in0=fs[:, 2, :], in1=sgT[:, h, tb * 512:(tb + 1) * 512], op=ALU.mult),
                      reads=[f2, S.tk("bx", 4, h, tb)], writes=[S.tk("bx", 5, h, tb)])
        S.add("sp", I("dma_start", out=st_cache[l], in_=st32[:]), reads=[S.tk("st", 32, h) for h in range(4)] + [stt], writes=[S.tk("stc", l)], dma_key="sts")
        out_proj(l, w_out, 512, lambda c, tb: yrT[:, c, tb * 512:(tb + 1) * 512], lambda tb: [S.tk("bx", 5, c, tb) for c in range(4)])

    def mixer_c(l, p):
        S.reset_unit("bx")
        S.reset_unit("pt")
        cqn = v4(X[0]); ycT = v4(X[0])
        latall = bx[:, 1:4, :]
        knT = X[4]
        vh = X[5].rearrange("p (t d) -> p t d", d=128)
        nkeys = (p + 1) * T
        pos0 = p * T
        S.reset_unit("wkvb")
        wkt = S.tk("wkvb", 0)
        S.add("pool", I("dma_start", out=v2(wkvb), in_=rowslab(w_ukv, l, 0, 2, 0, 2048)), writes=[wkt], dma_key="wukv")
        wkv2 = v2(wkvb)
        if p > 0:
            S.add("sp", I("dma_start", out=latall[:, :, 0:pos0], in_=lat_cache[l][:, :, 0:pos0]), reads=[S.tk("latc", l)],
                  writes=[S.tk("bx", "lat", "prior")], dma_key="latl")
        wq0, wq0t = load_slot(colslab(w_in, l, 32 * 128), v16, "two")
        wq1, wq1t = load_slot(colslab(w_in, l, 34 * 128), v16, "two")
        for tb in range(2):
            for j in range(4):
                wv, wt = (wq0, wq0t) if j < 2 else (wq1, wq1t)
                b, bt = proj_fm(wv, wt, j % 2, tb)
                copy_out(fs[:, j, :], b, [bt], [S.tk("fs", j)])
            norm_fm(fs, lambda c, c0: S.tk("fs", c), 4, 512, [(0, 512)], lambda c: vecs[:, l, V_QN + c:V_QN + c + 1],
                    lambda c, c0, n, tb=tb: cqn[:, c, tb * 512:(tb + 1) * 512], lambda c, c0, tb=tb: S.tk("bx", 0, c, tb))
        wkv_, wkvt = load_slot(colslab(w_in, l, 36 * 128), v16, "two")
        wkr, wkrt = load_slot(colslab(w_in, l, 38 * 128), v16, "two")
        for tb in range(2):
            for j in range(2):
                b, bt = proj_fm(wkv_, wkvt, j, tb)
                copy_out(fs[:, j, :], b, [bt], [S.tk("fs", j)])
            norm_fm(fs, lambda c, c0: S.tk("fs", c), 2, 256, [(0, 512)], lambda c: vecs[:, l, V_KVN + c:V_KVN + c + 1],
                    lambda c, c0, n, tb=tb: latall[:, c, pos0 + tb * 512: pos0 + (tb + 1) * 512], lambda c, c0, tb=tb: S.tk("bx", "lat", "own", c, tb))
            tt = load_tab(c_cosM, c_sinM, pos0, tb)
            a, at = proj_fm(wkr, wkrt, 0, tb)
            b, bt = proj_fm(wkr, wkrt, 1, tb)
            rope_out(latall[:, 2, pos0 + tb * 512: pos0 + (tb + 1) * 512], a, at, b, bt, tt, [S.tk("bx", "lat", "own", 2, tb)])
        own_lat = [S.tk("bx", "lat", "own", c, tb) for c in range(3) for tb in range(2)]
        if p + 1 < npass:
            S.add("sp", I("dma_start", out=lat_cache[l][:, :, pos0:pos0 + T], in_=latall[:, :, pos0:pos0 + T]), reads=own_lat,
                  writes=[S.tk("latc", l)], dma_key="lats")
        lat_reads = own_lat + ([S.tk("bx", "lat", "prior")] if p > 0 else [])
        S.reset_unit("nT")
        wn, wnt = load_slot(rowslab(w_uq, l, 0, 4, 0, 1024), v4, "two")
        wr, wrt = load_slot(rowslab(w_uq, l, 0, 4, 1024, 1024), v4, "two")
        for tb in range(2):
            cq_reads = [S.tk("bx", 0, c, tb) for c in range(4)]
            for hq in range(8):
                b, bt = proj_fm(wn, wnt, hq, tb, src=cqn, src_reads=cq_reads, nk=4)
                copy_out(nT[:, hq, tb * 512:(tb + 1) * 512], b, [bt], [S.tk("nT", hq, tb)])
            tt = load_tab(c_cosM, c_sinM, pos0, tb)
            for i in range(4):
                a, at = proj_fm(wr, wrt, i, tb, src=cqn, src_reads=cq_reads, nk=4)
                b, bt = proj_fm(wr, wrt, 4 + i, tb, src=cqn, src_reads=cq_reads, nk=4)
                rope_out(nT[:, 8 + i, tb * 512:(tb + 1) * 512], a, at, b, bt, tt, [S.tk("nT", 8 + i, tb)])
        sc = 192.0 ** -0.5
        for hg in range(2):
            for h4 in range(4):
                h = hg * 4 + h4
                half = h % 2
                pair = h // 2
                kt_ = S.tk("bx", 4, "k")
                vt_ = S.tk("bx", 5, "v")
                for kb in range(nkeys // 512):
                    b, bt = next_bank(0, 4)
                    mm_group(b, bt, [(wkv2[:, c, h * 128:(h + 1) * 128], latall[:, c, kb * 512:(kb + 1) * 512]) for c in range(2)], reads=[wkt] + lat_reads)
                    copy_out(knT[:, kb * 512:(kb + 1) * 512], b, [bt], [kt_])
                    b, bt = next_bank(0, 4)
                    for t4 in range(4):
                        t_ = kb * 4 + t4
                        mm_group(b[:, t4 * 128:(t4 + 1) * 128], bt,
                                 [(latall[:, c, t_ * 128:(t_ + 1) * 128], wkv2[:, c, 1024 + h * 128: 1024 + (h + 1) * 128]) for c in range(2)], reads=[wkt] + lat_reads)
                    copy_out(X[5][:, kb * 512:(kb + 1) * 512], b, [bt], [vt_])
                for qb in range(2):
                    state["acc"] = state.get("acc", 0) + 1
                    ob, obt = banks[4 + state["acc"] % 2], bank_tok[4 + state["acc"] % 2]
                    sb_, sbt = banks[6 + state["acc"] % 2], bank_tok[6 + state["acc"] % 2]
                    qsl0 = qb * 512
                    gmax = 8 * p + 4 * qb + 4
                    for g in range(gmax):
                        jd = g - (8 * p + 4 * qb)
                        c0 = 128 * jd if jd > 0 else 0
                        cols = slice(c0, 512)
                        qcols = slice(qsl0 + c0, qsl0 + 512)
                        ksl = slice(g * 128, (g + 1) * 128)
                        st_, stt_ = next_bank(0, 4)
                        mm_group(st_[:, cols], stt_,
                                 [(knT[:, ksl], nT[:, h, qcols]),
                                  (latall[64 * half:64 * half + 64, 2, ksl], nT[64 * half:64 * half + 64, 8 + pair, qcols])],
                                 reads=[kt_, S.tk("nT", h, qb), S.tk("nT", 8 + pair, qb)] + lat_reads)
                        state["ptr"] += 1
                        pr = state["ptr"] % 4
                        ptt = S.tk("pt", pr)
                        S.add("act", I("activation", out=pt[:, pr, cols], in_=st_[:, cols], func=AF.Exp, scale=sc),
                              reads=[stt_], writes=[ptt])
                        if jd >= 0:
                            S.add("dve", I("tensor_tensor", out=pt[:, pr, c0:c0 + 128], in0=pt[:, pr, c0:c0 + 128], in1=trib, op=ALU.mult),
                                  reads=[ptt] + CT, writes=[ptt])
                        mm_group(ob[:, cols], obt, [(vh[:, g, :], pt[:, pr, cols])], reads=[vt_, ptt], start=(g == 0), stop=(g == gmax - 1))
                        mm_group(sb_[:, cols], sbt, [(ones[:, :], pt[:, pr, cols])], reads=[ptt] + CT, start=(g == 0), stop=(g == gmax - 1))
                    state["fsr"] += 1
                    r = state["fsr"] % 4
                    ft = S.tk("fs", r)
                    S.add("dve", I("reciprocal", out=fs[:, r, :], in_=sb_), reads=[sbt], writes=[ft])
                    S.add("dve", I("tensor_tensor", out=ycT[:, h4, qb * 512:(qb + 1) * 512], in0=ob, in1=fs[:, r, :], op=ALU.mult),
                          reads=[obt, ft], writes=[S.tk("bx", 0, h4, qb)])
            out_proj(l, w_out, 1024 + hg * 512, lambda c, tb: ycT[:, c, tb * 512:(tb + 1) * 512], lambda tb: [S.tk("bx", 0, c, tb) for c in range(4)])

    def xattn(l, p):
        norm_h_to_n(l, V_XA)
        S.reset_unit("bx")
        S.reset_unit("pt")
        S.reset_unit("wkvb")
        memn = v16(wkvb)
        xq = bx[:, 0:4, :].rearrange("p u (c t) -> p (u c) t", c=4)
        kTm = X[4].rearrange("p (c m) -> p c m", m=256)
        vm = X[5].rearrange("p (t n) -> p t n", t=2)
        memr = memT.rearrange("(c p) m -> p c m", p=128)
        fsv = fs.rearrange("p a b -> p (a b)").rearrange("p (c m) -> p c m", m=256)
        fst = [S.tk("fs", i) for i in range(4)]
        ssb, sst = next_bank()
        for half in range(2):
            S.add("sp", I("dma_start", out=fsv, in_=memr[:, half * 8:(half + 1) * 8, :]), writes=fst, dma_key="meml")
            for cp in range(0, 8, 2):
                state["sq"] += 1
                r = state["sq"] % 2
                sqt = S.tk("sq", r)
                S.add("act", I("activation", out=sq[:, r, :, 0:256], in_=fsv[:, cp:cp + 2, :], func=AF.Square), reads=fst, writes=[sqt])
                mm_group(ssb[:, 0:256], sst, [(ones[:, :], sq[:, r, j, 0:256]) for j in range(2)], reads=[sqt] + CT,
                         start=(half == 0 and cp == 0), stop=(half == 1 and cp == 6))
        rt = S.tk("rstd", 0)
        rsqrt_to(rstd[:, 0:256], ssb[:, 0:256], [sst] + CT, [rt], 1.0 / D_MODEL, 256)
        mnt = S.tk("wkvb", "memn")
        for half in range(2):
            S.add("sp", I("dma_start", out=fsv, in_=memr[:, half * 8:(half + 1) * 8, :]), writes=fst, dma_key="meml")
            for c in range(8):
                cc = half * 8 + c
                S.add("dve", I("scalar_tensor_tensor", out=memn[:, cc, :], in0=fsv[:, c, :], scalar=vecs[:, l, V_MEM + cc:V_MEM + cc + 1],
                                                                          in1=rstd[:, 0:256], op0=ALU.mult, op1=ALU.mult),
                      reads=fst + [rt] + CT, writes=[mnt])
        for s in range(8):
            wv, wt = load_slot(colslab(xa_wkv, l, s * 256), v16, "two")
            for j in range(2):
                b, bt = next_bank()
                mm_group(b[:, 0:256], bt, [(wv[:, kc, j * 128:(j + 1) * 128], memn[:, kc, :]) for kc in range(16)], reads=[wt, mnt])
                copy_out(kTm[:, s * 2 + j, :], b[:, 0:256], [bt], [S.tk("bx", 4, s * 2 + j)])
        for s in range(8):
            wv, wt = load_slot(colslab(xa_wkv, l, 2048 + s * 256), v16, "two")
            for mt in range(2):
                b, bt = next_bank()
                mm_group(b[:, 0:256], bt, [(memn[:, kc, mt * 128:(mt + 1) * 128], wv[:, kc, :]) for kc in range(16)], reads=[wt, mnt])
                copy_out(vm[:, mt, s * 256:(s + 1) * 256], b[:, 0:256], [bt], [S.tk("bx", 5, mt, s)])
        for s in range(8):
            wv, wt = load_slot(colslab(xa_wq, l, s * 256), v16, "two")
            for j in range(2):
                for tb in range(2):
                    b, bt = proj_fm(wv, wt, j, tb)
                    cj = s * 2 + j
                    copy_out(xq[:, cj, tb * 512:(tb + 1) * 512], b, [bt], [S.tk("bx", cj // 4, "q", cj, tb)])
        S.reset_unit("nT")
        sc = 512.0 ** -0.5
        for h in range(4):
            for tb in range(2):
                tsl = slice(tb * 512, (tb + 1) * 512)
                ptoks = []
                for mc in range(2):
                    b, bt = next_bank()
                    mm_group(b, bt, [(kTm[:, 4 * h + c, mc * 128:(mc + 1) * 128], xq[:, 4 * h + c, tsl]) for c in range(4)],
                             reads=[S.tk("bx", 4, 4 * h + c) for c in range(4)] + [S.tk("bx", h, "q", 4 * h + c, tb) for c in range(4)])
                    state["ptr"] += 1
                    pr = state["ptr"] % 4
                    ptt = S.tk("pt", pr)
                    S.add("act", I("activation", out=pt[:, pr, :], in_=b, func=AF.Exp, scale=sc), reads=[bt], writes=[ptt])
                    ptoks.append((pr, ptt))
                sb_, sbt = next_bank()
                mm_group(sb_, sbt, [(ones[:, :], pt[:, pr, :]) for pr, _ in ptoks], reads=[t for _, t in ptoks] + CT)
                state["fsr"] += 1
                r = state["fsr"] % 4
                ft = S.tk("fs", r)
                S.add("dve", I("reciprocal", out=fs[:, r, :], in_=sb_), reads=[sbt], writes=[ft])
                for c in range(4):
                    cj = 4 * h + c
                    b, bt = next_bank()
                    mm_group(b, bt, [(vm[:, mc, cj * 128:(cj + 1) * 128], pt[:, ptoks[mc][0], :]) for mc in range(2)],
                             reads=[S.tk("bx", 5, mc, cj // 2) for mc in range(2)] + [t for _, t in ptoks])
                    S.add("dve", I("tensor_tensor", out=nT[:, cj, tsl], in0=b, in1=fs[:, r, :], op=ALU.mult),
                          reads=[bt, ft], writes=[S.tk("nT", cj, tb)])
        for s in range(8):
            wv, wt = load_slot(colslab(xa_wo, l, s * 256), v16, "two")
            for j in range(2):
                for tb in range(2):
                    b, bt = proj_fm(wv, wt, j, tb)
                    h_accum(b, bt, s * 2 + j, tb, 1.0)

    xTr = xT.rearrange("(c p) t -> p c t", p=128)
    yTr = yT.rearrange("(c p) t -> p c t", p=128)
    outs = []
    for p in range(npass):
        for c4 in range(0, 16, 4):
            q = "sp" if (c4 // 4) % 2 == 0 else "act"
            for tb in range(2):
                S.add("sp", I("dma_start", out=hT[:, c4:c4 + 4, tb * 512:(tb + 1) * 512], in_=xTr[:, c4:c4 + 4, p * T + tb * 512: p * T + (tb + 1) * 512]),
                      writes=[S.tk("hT", c, tb) for c in range(c4, c4 + 4)], dma_key=f"xin{c4}_{tb}")
        for l in range(L):
            if "f" in phases:
                ffn(l, w_g1, w_u1, w_d1, V_FFN1)
            if any(c in phases for c in "abc"):
                norm_h_to_n(l, V_MIX)
            if "a" in phases:
                mixer_a(l, p)
            if "b" in phases:
                mixer_b(l, p)
            if "c" in phases:
                mixer_c(l, p)
            if "x" in phases:
                xattn(l, p)
            if "g" in phases:
                ffn(l, w_g2, w_u2, w_d2, V_FFN2)
        norm_fm(hT, h_tok, 16, D_MODEL, TBS, lambda c: vecs[:, 0, V_FIN + c:V_FIN + c + 1],
                lambda c, c0, n: hT[:, c, c0:c0 + n], h_tok)
        for c4 in range(0, 16, 4):
            for tb in range(2):
                o = S.add("sp", I("dma_start", out=yTr[:, c4:c4 + 4, p * T + tb * 512: p * T + (tb + 1) * 512], in_=hT[:, c4:c4 + 4, tb * 512:(tb + 1) * 512]),
                          reads=[S.tk("hT", c, tb) for c in range(c4, c4 + 4)], dma_key=f"yout{c4}_{tb}")
                outs.append(o)
    counts = S.emit(nc, final_waits=outs[-8:])
    return nc, counts


def _col(v):
    Ld, n = v.shape
    return np.ascontiguousarray(v.reshape(Ld, n // 128, 128).transpose(2, 0, 1))


def _swap_halves(w, width):
    K, N = w.shape
    half = width // 2
    w4 = w.reshape(K, N // width, 2, half)
    return w4[:, :, ::-1, :].reshape(K, N)


def prep_inputs(inp, depth=DEPTH, seq=SEQ):
    L = depth
    f32 = np.float32
    g = {k: np.asarray(v, dtype=f32) for k, v in inp.items()}
    d = {}
    d["w_g1"] = g["ffn1_w_gate"][:L]; d["w_u1"] = g["ffn1_w_up"][:L]; d["w_d1"] = g["ffn1_w_down"][:L]
    d["w_g2"] = g["ffn2_w_gate"][:L]; d["w_u2"] = g["ffn2_w_up"][:L]; d["w_d2"] = g["ffn2_w_down"][:L]
    wi = g["w_in"][:L]
    ext = np.empty((L, D_MODEL, 5120), f32)
    for l in range(L):
        w = wi[l]
        rq = w[:, 1024:1536]; rk = w[:, 1536:2048]; kr = w[:, 3840:3904]
        ext[l] = np.concatenate([
            w[:, 0:512], w[:, 512:1024],
            rq, _swap_halves(rq, 128), rk, _swap_halves(rk, 128),
            w[:, 2048:2560], w[:, 2560:3072], w[:, 3072:3584], w[:, 3584:3840],
            kr, kr, _swap_halves(kr, 64), _swap_halves(kr, 64)], axis=1)
    d["w_in"] = ext
    uq = g["w_uq"][:L].reshape(L, 512, 8, 192)
    nope = uq[:, :, :, :128].reshape(L, 512, 1024)
    ropeq = np.ascontiguousarray(uq[:, :, :, 128:]).reshape(L, 512, 512)
    ropes = np.stack([_swap_halves(ropeq[l], 64) for l in range(L)])
    d["w_uq"] = np.ascontiguousarray(np.concatenate([nope, ropeq, ropes], axis=2))
    ukv = g["w_ukv"][:L].reshape(L, 256, 8, 256)
    d["w_ukv"] = np.ascontiguousarray(np.concatenate([ukv[:, :, :, :128].reshape(L, 256, 1024), ukv[:, :, :, 128:].reshape(L, 256, 1024)], axis=2))
    d["w_out"] = g["w_out"][:L]
    d["xa_wq"] = g["xa_wq"][:L]; d["xa_wkv"] = g["xa_wkv"][:L]; d["xa_wo"] = g["xa_wo"][:L]
    vec = np.zeros((128, L, NV), f32)
    vec[:, :, V_FFN1:V_FFN1 + 16] = _col(g["ffn1_norm"][:L])
    vec[:, :, V_MIX:V_MIX + 16] = _col(g["mix_norm"][:L])
    vec[:, :, V_XA:V_XA + 16] = _col(g["xa_norm"][:L])
    vec[:, :, V_MEM:V_MEM + 16] = _col(g["mem_norm"][:L])
    vec[:, :, V_FFN2:V_FFN2 + 16] = _col(g["ffn2_norm"][:L])
    vec[:, :, V_QN:V_QN + 4] = _col(g["q_norm"][:L])
    vec[:, :, V_KVN:V_KVN + 2] = _col(g["kv_norm"][:L])
    vec[:, :, V_GN:V_GN + 4] = _col(g["ret_gn"][:L])
    vec[:, :, V_FIN:V_FIN + 16] = _col(np.broadcast_to(g["final_norm"][None], (L, D_MODEL)))
    d["vecs"] = vec
    d["sgu_rep"] = np.ascontiguousarray(np.broadcast_to(g["sgu_norm"][:L, None, :], (L, 128, 512)))
    d["sgu_wsT"] = np.ascontiguousarray(g["sgu_w_s"][:L].transpose(0, 3, 1, 2))
    d["sgu_b"] = np.ascontiguousarray(g["sgu_b"][:L].reshape(L, 1, 512))
    i = np.arange(128)
    d["c_ident"] = np.eye(128, dtype=f32)
    d["c_ones"] = np.ones((128, 128), f32)
    d["c_tri"] = (i[:, None] <= i[None, :]).astype(f32)
    gam = 1.0 - 2.0 ** (-5.0 - np.arange(4, dtype=np.float64))
    diff = (i[None, :] - i[:, None]).astype(np.float64)
    dm = np.where(diff[None] >= 0, gam[:, None, None] ** np.maximum(diff, 0)[None], 0.0) * (128 ** -0.5)
    d["c_dmask"] = np.ascontiguousarray(dm.transpose(1, 0, 2)).astype(f32)
    xi_ = gam[:, None] ** (i[None, :] + 1.0)
    d["c_xi"] = np.ascontiguousarray(np.broadcast_to(xi_[None], (128, 4, 128))).astype(f32)
    d["c_zeta"] = np.ascontiguousarray((gam[None, :] ** (127.0 - i[:, None])) * (128 ** -0.5)).astype(f32)
    pos = np.arange(seq, dtype=np.float32)

    def tables(dim, reps):
        half = dim // 2
        inv = (np.float32(10000.0) ** (-np.arange(half, dtype=np.float32) * np.float32(2.0) / np.float32(dim))).astype(np.float32)
        ang = (pos[None, :] * inv[:, None]).astype(np.float32)
        c = np.cos(ang).astype(f32); s = np.sin(ang).astype(f32)
        cosT = np.concatenate([c, c] * reps, axis=0)
        sinT = np.concatenate([-s, s] * reps, axis=0)
        return np.ascontiguousarray(cosT), np.ascontiguousarray(sinT)
    d["c_cosR"], d["c_sinR"] = tables(128, 1)
    d["c_cosM"], d["c_sinM"] = tables(64, 2)
    return d


_CACHE = {}


def kernel(**inputs):
    x = np.asarray(inputs["x"], dtype=np.float32)
    mem = np.asarray(inputs["mem"], dtype=np.float32)
    shared = prep_inputs(inputs)
    if "nc" not in _CACHE:
        _CACHE["nc"] = build()[0]
    nc = _CACHE["nc"]
    in_maps = []
    for core in range(8):
        b = core // 2
        m = dict(shared)
        m["xT"] = np.ascontiguousarray(x[b].T)
        m["memT"] = np.ascontiguousarray(mem[b].T)
        in_maps.append(m)
    res = run_bass_kernel_spmd(nc, in_maps, core_ids=list(range(8)))
    out = np.empty((BATCH, SEQ, D_MODEL), np.float32)
    for b in range(BATCH):
        out[b] = res.results[2 * b]["yT"].T
    return out
```

```python
import math
import numpy as np
import ml_dtypes
import concourse.bass as bass
import concourse.mybir as mybir
from concourse.bass_utils import run_bass_kernel_spmd

F32 = mybir.dt.float32
BF16 = mybir.dt.bfloat16
AF = mybir.ActivationFunctionType
ALU = mybir.AluOpType

D_MODEL = 2048
SEQ = 4096
BATCH = 4
DEPTH = 4
D_FF = 5632
T = 1024
EPS = 1e-6
NV = 106

ENGS = ["pe", "act", "dve", "pool", "sp"]
EPOCH = 20000


def I(method, **kw):
    return lambda e: getattr(e, method)(**kw)


class Buf:
    __slots__ = ("name", "last_w", "readers")

    def __init__(self, name, hazard=None):
        self.name = name
        self.last_w = None
        self.readers = dict(hazard) if hazard else {}


class Op:
    __slots__ = ("eng", "fn", "waits", "ord", "needs_inc", "sem_i", "val", "is_dma", "dkey")


class Sched:
    def __init__(self):
        self.ops = {e: [] for e in ENGS}
        self.seen = {e: {} for e in ENGS}
        self.dma_n = {}
        self.units = {}
        self.hazard = {}

    def tk(self, unit, *key):
        d = self.units.setdefault(unit, {})
        b = d.get(key)
        if b is None:
            b = Buf(f"{unit}{key}", self.hazard.get(unit))
            d[key] = b
        return b

    def reset_unit(self, unit):
        hz = dict(self.hazard.get(unit, {}))
        for b in self.units.get(unit, {}).values():
            cands = list(b.readers.values())
            if b.last_w is not None:
                cands.append(b.last_w)
            for op in cands:
                k = ("dma", op.dkey) if op.is_dma else op.eng
                if k not in hz or hz[k].ord < op.ord:
                    hz[k] = op
        self.hazard[unit] = hz
        self.units[unit] = {}

    def add(self, eng, fn, reads=(), writes=(), dma_key=None):
        op = Op()
        op.eng = eng
        op.fn = fn
        op.needs_inc = False
        op.is_dma = dma_key is not None
        op.dkey = dma_key
        op.sem_i = 0
        op.val = 0
        if op.is_dma:
            n = self.dma_n.get(dma_key, 0) + 1
            self.dma_n[dma_key] = n
            op.ord = n
        else:
            op.ord = len(self.ops[eng])
        deps = []
        for b in reads:
            if b.last_w is not None:
                d = b.last_w
                if d.is_dma or op.is_dma or not (d.eng == eng and eng == "pe"):
                    deps.append(d)
        for b in writes:
            cands = list(b.readers.values())
            if b.last_w is not None:
                cands.append(b.last_w)
            for d in cands:
                if d.is_dma or op.is_dma or d.eng != eng:
                    deps.append(d)
        waits = {}
        seen = self.seen[eng]
        for d in deps:
            if d is op:
                continue
            key = ("dma", d.dkey) if d.is_dma else ("eng", d.eng)
            if seen.get(key, -1) >= d.ord:
                continue
            if key not in waits or waits[key].ord < d.ord:
                waits[key] = d
        for key, d in waits.items():
            seen[key] = d.ord
            d.needs_inc = True
        op.waits = list(waits.values())
        rk = ("dma", dma_key) if op.is_dma else eng
        for b in reads:
            b.readers[rk] = op
        for b in writes:
            b.last_w = op
            b.readers = {}
        self.ops[eng].append(op)
        return op

    def emit(self, nc, final_waits=()):
        nsem = {}
        for e in ENGS:
            c = 0
            for op in self.ops[e]:
                if op.is_dma or not op.needs_inc:
                    continue
                c += 1
                op.sem_i = (c - 1) // EPOCH
                op.val = (c - 1) % EPOCH + 1
            nsem[e] = (c + EPOCH - 1) // EPOCH if c else 0
        sems = {}
        for e in ENGS:
            for i in range(max(nsem[e], 1)):
                sems[("eng", e, i)] = nc.alloc_semaphore(name=f"s_{e}_{i}")
        for k in self.dma_n:
            sems[("dma", k)] = nc.alloc_semaphore(name=f"d_{k}")

        def sem_of(d):
            if d.is_dma:
                return sems[("dma", d.dkey)], 16 * d.ord
            return sems[("eng", d.eng, d.sem_i)], d.val

        def run(e, engine):
            for op in self.ops[e]:
                for d in op.waits:
                    s, v = sem_of(d)
                    engine.wait_ge(s, v)
                ins = op.fn(engine)
                if op.is_dma:
                    ins.then_inc(sems[("dma", op.dkey)], 16)
                elif op.needs_inc:
                    ins.then_inc(sems[("eng", e, op.sem_i)], 1)
            if e == "sp":
                for d in final_waits:
                    s, v = sem_of(d)
                    engine.wait_ge(s, v)

        with nc.Block() as block:
            @block.tensor
            def _(eng):
                run("pe", eng)

            @block.scalar
            def _(eng):
                run("act", eng)

            @block.vector
            def _(eng):
                run("dve", eng)

            @block.gpsimd
            def _(eng):
                run("pool", eng)

            @block.sync
            def _(eng):
                run("sp", eng)
        return {e: len(self.ops[e]) for e in ENGS}


V_FFN1, V_MIX, V_XA, V_MEM, V_FFN2, V_QN, V_KVN, V_GN, V_FIN = 0, 16, 32, 48, 64, 80, 84, 86, 90


def build(depth=DEPTH, npass=SEQ // T, use_gelu_tanh_lut=True, phases="fabcxg"):
    nc = bass.Bass("TRN2", target_bir_lowering=False)
    S = Sched()
    L = depth
    NTOK = npass * T

    def din(name, shape, dt=F32):
        return nc.dram_tensor(name, list(shape), dt, kind="ExternalInput").ap()

    xT = din("xT", [D_MODEL, NTOK])
    memT = din("memT", [D_MODEL, 256])
    yT = nc.dram_tensor("yT", [D_MODEL, NTOK], F32, kind="ExternalOutput").ap()
    w_g1 = din("w_g1", [L, D_MODEL, D_FF]); w_u1 = din("w_u1", [L, D_MODEL, D_FF]); w_d1 = din("w_d1", [L, D_FF, D_MODEL])
    w_g2 = din("w_g2", [L, D_MODEL, D_FF]); w_u2 = din("w_u2", [L, D_MODEL, D_FF]); w_d2 = din("w_d2", [L, D_FF, D_MODEL])
    w_in = din("w_in", [L, D_MODEL, 5120])
    w_uq = din("w_uq", [L, 512, 2048])
    w_ukv = din("w_ukv", [L, 256, 2048])
    w_out = din("w_out", [L, D_MODEL, D_MODEL])
    xa_wq = din("xa_wq", [L, D_MODEL, D_MODEL])
    xa_wkv = din("xa_wkv", [L, D_MODEL, 4096])
    xa_wo = din("xa_wo", [L, D_MODEL, D_MODEL])
    vecs_d = din("vecs", [128, L, NV])
    sgu_rep_d = din("sgu_rep", [L, 128, 512])
    sgu_wsT_d = din("sgu_wsT", [L, 128, 4, 128])
    sgu_b_d = din("sgu_b", [L, 1, 512])
    c_ident = din("c_ident", [128, 128]); c_ones = din("c_ones", [128, 128]); c_tri = din("c_tri", [128, 128])
    c_dmask = din("c_dmask", [128, 4, 128]); c_xi = din("c_xi", [128, 4, 128]); c_zeta = din("c_zeta", [128, 4])
    c_cosR = din("c_cosR", [128, SEQ]); c_sinR = din("c_sinR", [128, SEQ])
    c_cosM = din("c_cosM", [128, SEQ]); c_sinM = din("c_sinM", [128, SEQ])
    lat_cache = nc.dram_tensor("lat_cache", [L, 128, 3, NTOK], BF16, kind="Internal").ap()
    st_cache = nc.dram_tensor("st_cache", [L, 128, 4, 128], F32, kind="Internal").ap()
    memkv_cache = nc.dram_tensor("memkv_cache", [L, 128, 2, 4096], BF16, kind="Internal").ap()

    def sb(name, shape, dt):
        return nc.alloc_sbuf_tensor("sb_" + name, list(shape), dt).ap()

    hT = sb("hT", [128, 16, T], F32)
    nT = sb("nT", [128, 16, T], BF16)
    bx = sb("bx", [128, 6, 4096], BF16)
    ring = sb("ring", [128, 2, 4096], BF16)
    wkvb = sb("wkvb", [128, 4096], BF16)
    pt = sb("pt", [128, 4, 512], BF16)
    tab = sb("tab", [128, 2, 512], F32)
    sq = sb("sq", [128, 2, 2, 512], BF16)
    rstd = sb("rstd", [128, T], F32)
    fs = sb("fs", [128, 4, 512], F32)
    vecs = sb("vecs", [128, L, NV], F32)
    ident = sb("ident", [128, 128], BF16); ones = sb("ones", [128, 128], BF16); tri = sb("tri", [128, 128], F32)
    trib = sb("trib", [128, 128], BF16)
    dmask = sb("dmask", [128, 4, 128], F32); xi = sb("xi", [128, 4, 128], F32); zeta = sb("zeta", [128, 4], F32)
    sgurep = sb("sgurep", [128, 512], F32)
    wsm = sb("wsm", [128, 4, 128], BF16)
    brow = sb("brow", [1, 512], BF16)
    st32 = sb("st32", [128, 4, 128], F32); st16 = sb("st16", [128, 4, 128], BF16)
    small = sb("small", [128, 16], F32)
    banks = [nc.alloc_psum_tensor(f"bank{i}", [128, 512], F32).ap() for i in range(8)]

    state = {"bank": 0, "slot": 0, "ev": 0, "sq": 0, "fsr": 0, "ptr": 0}
    bank_tok = [S.tk("bank", i) for i in range(8)]

    def next_bank(lo=0, hi=8):
        state["bank"] += 1
        i = lo + state["bank"] % (hi - lo)
        return banks[i], bank_tok[i]

    def units_all():
        return [("bx", i) for i in range(6)] + [("ring", 0), ("ring", 1)]

    def slot_ap(u):
        if u[0] == "wkvb":
            return wkvb[:, :]
        return (ring if u[0] == "ring" else bx)[:, u[1], :]

    def slot_tok(u):
        return S.tk(u[0], "slot", u[1])

    slotsets = {"all": units_all(), "two": [("ring", 0), ("ring", 1)], "three": [("ring", 0), ("ring", 1), ("wkvb", 0)]}

    def load_slot(src_ap, view_fn, mode):
        us = slotsets[mode]
        state["slot"] += 1
        u = us[state["slot"] % len(us)]
        ap = view_fn(slot_ap(u))
        tok = slot_tok(u)
        S.add("pool", I("dma_start", out=ap, in_=src_ap), writes=[tok], dma_key=f"w_{u[0]}{u[1]}")
        return ap, tok

    def v16(a):
        return a.rearrange("p (k n) -> p k n", k=16)

    def v4(a):
        return a.rearrange("p (k n) -> p k n", k=4)

    def v2(a):
        return a.rearrange("p (k n) -> p k n", k=2)

    def colslab(W, l, c0, ncols=256):
        return W[l].rearrange("(k p) n -> p k n", p=128)[:, :, c0:c0 + ncols]

    def rowslab(W, l, r0, nk, c0, ncols):
        return W[l][r0:r0 + nk * 128, c0:c0 + ncols].rearrange("(k p) n -> p k n", p=128)

    def evac_engine():
        state["ev"] += 1
        return "act" if state["ev"] % 2 else "dve"

    def copy_out(dst, src, reads, writes, eng=None, scale=None):
        eng = eng or evac_engine()
        if eng == "act":
            if scale is None:
                S.add("act", I("activation", out=dst, in_=src, func=AF.Copy), reads=reads, writes=writes)
            else:
                S.add("act", I("activation", out=dst, in_=src, func=AF.Copy, scale=scale), reads=reads, writes=writes)
        else:
            if scale is None:
                S.add("dve", I("tensor_copy", out=dst, in_=src), reads=reads, writes=writes)
            else:
                S.add("dve", I("tensor_scalar", out=dst, in0=src, scalar1=scale, scalar2=None, op0=ALU.mult), reads=reads, writes=writes)

    def mm_group(out_ap, out_tok, pairs, reads, start=True, stop=True):
        n = len(pairs)

        def fn(e):
            ins = None
            for i, (l_, r_) in enumerate(pairs):
                ins = e.matmul(out_ap, lhsT=l_, rhs=r_, start=(start and i == 0), stop=(stop and i == n - 1))
            return ins
        return S.add("pe", fn, reads=reads, writes=[out_tok])

    def rsqrt_to(dst, src, src_toks, dst_toks, scale, n):
        S.add("act", I("activation", out=dst, in_=src, func=AF.Sqrt, scale=scale, bias=eps_col[:, 0:1]),
              reads=src_toks, writes=dst_toks)
        S.add("dve", I("reciprocal", out=dst, in_=dst), reads=dst_toks, writes=dst_toks)

    def gelu_to(dst, src, reads, writes):
        n = src.shape[-1]
        if use_gelu_tanh_lut:
            S.add("act", I("activation", out=dst, in_=src, func=AF.Gelu_apprx_tanh), reads=reads, writes=writes)
            return
        state["fsr"] += 1
        r = state["fsr"] % 4
        tmp = fs[:, r, 0:n]
        tt = [S.tk("fs", r)]
        S.add("act", I("activation", out=tmp, in_=src, func=AF.Square), reads=reads, writes=tt)
        S.add("dve", I("tensor_scalar", out=tmp, in0=tmp, scalar1=0.044715, scalar2=1.0, op0=ALU.mult, op1=ALU.add), reads=tt, writes=tt)
        S.add("dve", I("tensor_tensor", out=tmp, in0=tmp, in1=src, op=ALU.mult), reads=tt + list(reads), writes=tt)
        S.add("act", I("activation", out=tmp, in_=tmp, func=AF.Sigmoid, scale=1.5957691216057308), reads=tt, writes=tt)
        S.add("dve", I("tensor_tensor", out=dst, in0=tmp, in1=src, op=ALU.mult), reads=tt + list(reads), writes=writes)

    ctok = S.tk("const", 0)
    eps_col = small[:, 0:1]

    def cload(dst, src, q="sp", key="c0"):
        S.add(q, I("dma_start", out=dst, in_=src), writes=[ctok], dma_key=key)

    cload(vecs, vecs_d, "sp", "c0")
    cload(tri, c_tri, "sp", "c0")
    cload(dmask, c_dmask, "sp", "c0")
    cload(xi, c_xi, "sp", "c0")
    cload(zeta, c_zeta, "sp", "c0")
    cload(ident, c_ident, "pool", "c1")
    cload(ones, c_ones, "pool", "c1")
    cload(trib, c_tri, "pool", "c1")
    S.add("dve", I("memset", ap=small[:, 0:1], constant=EPS), writes=[ctok])
    CT = [ctok]

    def norm_fm(src, src_tok_fn, nch, dtrue, ncols_list, gcol_fn, dst_fn, dst_tok_fn, in_place_f32=False):
        for (c0, n) in ncols_list:
            ssb, sst = next_bank()
            for cp in range(0, nch, 2):
                k = min(2, nch - cp)
                state["sq"] += 1
                r = state["sq"] % 2
                sqt = S.tk("sq", r)
                rd = [src_tok_fn(c, c0) for c in range(cp, cp + k)]
                S.add("act", I("activation", out=sq[:, r, 0:k, 0:n], in_=src[:, cp:cp + k, c0:c0 + n], func=AF.Square),
                      reads=rd, writes=[sqt])
                mm_group(ssb[:, 0:n], sst, [(ones[:, :], sq[:, r, j, 0:n]) for j in range(k)], reads=[sqt] + CT,
                         start=(cp == 0), stop=(cp + k >= nch))
            rt = S.tk("rstd", c0)
            rsqrt_to(rstd[:, c0:c0 + n], ssb[:, 0:n], [sst] + CT, [rt], 1.0 / dtrue, n)
            for c in range(nch):
                S.add("dve", I("scalar_tensor_tensor", out=dst_fn(c, c0, n), in0=src[:, c, c0:c0 + n], scalar=gcol_fn(c),
                                                                   in1=rstd[:, c0:c0 + n], op0=ALU.mult, op1=ALU.mult),
                      reads=[src_tok_fn(c, c0), rt] + CT, writes=[dst_tok_fn(c, c0)])

    TBS = [(0, 512), (512, 512)]

    def h_tok(c, c0):
        return S.tk("hT", c, c0 // 512)

    def n_tok(c, c0):
        return S.tk("nT", c, c0 // 512)

    def norm_h_to_n(l, vofs):
        S.reset_unit("nT")
        norm_fm(hT, h_tok, 16, D_MODEL, TBS, lambda c: vecs[:, l, vofs + c:vofs + c + 1],
                lambda c, c0, n: nT[:, c, c0:c0 + n], n_tok)

    def n_reads(tb):
        return [S.tk("nT", c, tb) for c in range(16)]

    def h_accum(bank, btok, oc, tb, scale, extra_reads=()):
        ht = S.tk("hT", oc, tb)
        dst = hT[:, oc, tb * 512:(tb + 1) * 512]
        S.add("dve", I("scalar_tensor_tensor", out=dst, in0=bank, scalar=scale, in1=dst, op0=ALU.mult, op1=ALU.add),
              reads=[btok, ht], writes=[ht])

    def ffn(l, Wg, Wu, Wd, vofs, restart=False):
        if restart:
            state["slot"] = -1
        norm_h_to_n(l, vofs)
        S.reset_unit("bx")
        S.reset_unit("pt")
        for s in range(D_FF // 256):
            gv, gt = load_slot(colslab(Wg, l, s * 256), v16, "all")
            uv, ut = load_slot(colslab(Wu, l, s * 256), v16, "all")
            dv, dt_ = load_slot(rowslab(Wd, l, s * 256, 2, 0, D_MODEL), v2, "all")
            for tb in range(2):
                tsl = slice(tb * 512, (tb + 1) * 512)
                state["ptr"] += 1
                pr = state["ptr"] % 2
                for j in range(2):
                    gb, gbt = next_bank()
                    mm_group(gb, gbt, [(gv[:, kc, j * 128:(j + 1) * 128], nT[:, kc, tsl]) for kc in range(16)], reads=[gt] + n_reads(tb))
                    ub, ubt = next_bank()
                    mm_group(ub, ubt, [(uv[:, kc, j * 128:(j + 1) * 128], nT[:, kc, tsl]) for kc in range(16)], reads=[ut] + n_reads(tb))
                    state["fsr"] += 1
                    r = state["fsr"] % 4
                    ft = S.tk("fs", r)
                    S.add("act", I("activation", out=fs[:, r, :], in_=gb, func=AF.Silu), reads=[gbt], writes=[ft])
                    at = S.tk("pt", pr, j)
                    S.add("dve", I("tensor_tensor", out=pt[:, pr * 2 + j, :], in0=fs[:, r, :], in1=ub, op=ALU.mult),
                          reads=[ft, ubt], writes=[at])
                for oc in range(16):
                    ob, obt = next_bank()
                    mm_group(ob, obt, [(dv[:, j, oc * 128:(oc + 1) * 128], pt[:, pr * 2 + j, :]) for j in range(2)],
                             reads=[dt_, S.tk("pt", pr, 0), S.tk("pt", pr, 1)])
                    h_accum(ob, obt, oc, tb, 0.5)

    def proj_fm(wv, wt, j, tb, src=None, src_reads=None, nk=16):
        b, bt = next_bank()
        tsl = slice(tb * 512, (tb + 1) * 512)
        src = nT if src is None else src
        rd = n_reads(tb) if src_reads is None else src_reads
        mm_group(b, bt, [(wv[:, kc, j * 128:(j + 1) * 128], src[:, kc, tsl]) for kc in range(nk)], reads=[wt] + rd)
        return b, bt

    def out_proj(l, W, r0, src_fn, src_reads_fn, mode="two"):
        for ch in range(2):
            wv, wt = load_slot(rowslab(W, l, r0, 4, ch * 1024, 1024), v4, mode)
            for tb in range(2):
                for o8 in range(8):
                    b, bt = next_bank()
                    mm_group(b, bt, [(wv[:, c, o8 * 128:(o8 + 1) * 128], src_fn(c, tb)) for c in range(4)],
                             reads=[wt] + src_reads_fn(tb))
                    h_accum(b, bt, ch * 8 + o8, tb, 1.0)

    X = [bx[:, i, :] for i in range(6)]
    MODE3 = "three"

    def load_tab(cosd, sind, pos0, tb):
        tt = S.tk("tab", 0)
        S.add("sp", I("dma_start", out=tab[:, 0, :], in_=cosd[:, pos0 + tb * 512: pos0 + (tb + 1) * 512]), writes=[tt], dma_key="tab")
        S.add("sp", I("dma_start", out=tab[:, 1, :], in_=sind[:, pos0 + tb * 512: pos0 + (tb + 1) * 512]), writes=[tt], dma_key="tab")
        return tt

    def rope_out(dst, a, at, b, bt, tt, wtoks):
        state["fsr"] += 1
        r0 = state["fsr"] % 4
        state["fsr"] += 1
        r1 = state["fsr"] % 4
        t0 = S.tk("fs", r0)
        t1 = S.tk("fs", r1)
        S.add("dve", I("tensor_tensor", out=fs[:, r0, :], in0=a, in1=tab[:, 0, :], op=ALU.mult), reads=[at, tt], writes=[t0])
        S.add("dve", I("tensor_tensor", out=fs[:, r1, :], in0=b, in1=tab[:, 1, :], op=ALU.mult), reads=[bt, tt], writes=[t1])
        S.add("dve", I("tensor_tensor", out=dst, in0=fs[:, r0, :], in1=fs[:, r1, :], op=ALU.add), reads=[t0, t1], writes=wtoks)

    def mixer_a(l, p):
        S.reset_unit("bx")
        S.reset_unit("pt")
        S.reset_unit("wkvb")
        uT = v4(X[0]); vt = X[1].rearrange("p (t n) -> p t n", t=8); yaT = v4(X[2])
        lt = S.tk("sgu_l", 0)
        S.add("sp", I("dma_start", out=sgurep, in_=sgu_rep_d[l]), writes=[lt], dma_key="sgul")
        S.add("sp", I("dma_start", out=fs[:, 3, :].rearrange("p (g t) -> p g t", g=4), in_=sgu_wsT_d[l]), writes=[S.tk("fs", 3)], dma_key="sgul2")
        S.add("pool", I("dma_start", out=brow, in_=sgu_b_d[l]), writes=[lt], dma_key="sgub")
        wt_ = S.tk("wsm", 0)
        for g in range(4):
            S.add("dve", I("tensor_tensor", out=wsm[:, g, :], in0=fs[:, 3, g * 128:(g + 1) * 128], in1=tri, op=ALU.mult),
                  reads=[S.tk("fs", 3)] + CT, writes=[wt_])
        for s in range(2):
            wv, wt = load_slot(colslab(w_in, l, s * 256), v16, MODE3)
            for j in range(2):
                for tb in range(2):
                    b, bt = proj_fm(wv, wt, j, tb)
                    gelu_to(uT[:, s * 2 + j, tb * 512:(tb + 1) * 512], b, [bt], [S.tk("bx", 0, s * 2 + j, tb)])
        wv2, wt2 = load_slot(colslab(w_in, l, 512), v16, MODE3)
        wv3, wt3 = load_slot(colslab(w_in, l, 768), v16, MODE3)
        for tt_ in range(8):
            b, bt = next_bank()
            mm_group(b[:, 0:256], bt, [(nT[:, kc, tt_ * 128:(tt_ + 1) * 128], wv2[:, kc, :]) for kc in range(16)], reads=[wt2] + n_reads(tt_ // 4))
            mm_group(b[:, 256:512], bt, [(nT[:, kc, tt_ * 128:(tt_ + 1) * 128], wv3[:, kc, :]) for kc in range(16)], reads=[wt3] + n_reads(tt_ // 4))
            r = tt_ % 2
            gt = S.tk("fs", r)
            gelu_to(fs[:, r, :], b, [bt], [gt])
            sst = S.tk("small", 1 + r)
            S.add("dve", I("memset", ap=small[:, 1 + r:2 + r], constant=0.0), writes=[sst])
            S.add("dve", I("scalar_tensor_tensor", out=fs[:, 2 + r, :], in0=fs[:, r, :], scalar=1.0, in1=fs[:, r, :], op0=ALU.mult, op1=ALU.mult,
                                                               accum_out=small[:, 1 + r:2 + r]), reads=[gt], writes=[sst, S.tk("fs", 2 + r)])
            rsqrt_to(small[:, 1 + r:2 + r], small[:, 1 + r:2 + r], [sst] + CT, [sst], 1.0 / 512, 1)
            S.add("dve", I("scalar_tensor_tensor", out=vt[:, tt_, :], in0=fs[:, r, :], scalar=small[:, 1 + r:2 + r], in1=sgurep,
                                                                        op0=ALU.mult, op1=ALU.mult),
                  reads=[gt, sst, lt], writes=[S.tk("bx", 1, tt_)])
        for g in range(4):
            for tb in range(2):
                b, bt = next_bank()
                for n4 in range(4):
                    n = tb * 4 + n4
                    mm_group(b[:, n4 * 128:(n4 + 1) * 128], bt,
                             [(vt[:, n, g * 128:(g + 1) * 128], wsm[:, g, :]), (ones[0:1, :], brow[0:1, g * 128:(g + 1) * 128])],
                             reads=[S.tk("bx", 1, n), wt_, lt] + CT)
                S.add("dve", I("tensor_tensor", out=yaT[:, g, tb * 512:(tb + 1) * 512], in0=b, in1=uT[:, g, tb * 512:(tb + 1) * 512], op=ALU.mult),
                      reads=[bt, S.tk("bx", 0, g, tb)], writes=[S.tk("bx", 2, g, tb)])
        out_proj(l, w_out, 0, lambda c, tb: yaT[:, c, tb * 512:(tb + 1) * 512], lambda tb: [S.tk("bx", 2, c, tb) for c in range(4)], MODE3)

    def mixer_b(l, p):
        S.reset_unit("bx")
        S.reset_unit("pt")
        qT = v4(X[0]); kT = v4(X[1])
        kz = X[2].rearrange("p (n h d) -> p n h d", n=8, h=4)
        vtm = X[3].rearrange("p (t n) -> p t n", t=8)
        sgT = v4(X[4]); yrT = v4(X[5])
        gam = [1.0 - 2.0 ** (-5.0 - h) for h in range(4)]
        S.reset_unit("st")
        stt = S.tk("st", "init")
        if p == 0:
            S.add("dve", I("memset", ap=st32[:], constant=0.0), writes=[stt])
            S.add("dve", I("memset", ap=st16[:], constant=0.0), writes=[stt])
        else:
            S.add("sp", I("dma_start", out=st32[:], in_=st_cache[l]), reads=[S.tk("stc", l)], writes=[stt], dma_key="stl")
            S.add("dve", I("tensor_copy", out=st16[:], in_=st32[:]), reads=[stt], writes=[stt])
        for which, dstT, un in ((0, qT, 0), (1, kT, 1)):
            c_a = 8 + which * 8
            for hp in range(2):
                wa, wat = load_slot(colslab(w_in, l, (c_a + hp * 2) * 128), v16, MODE3)
                wb, wbt = load_slot(colslab(w_in, l, (c_a + 4 + hp * 2) * 128), v16, MODE3)
                for tb in range(2):
                    tt = load_tab(c_cosR, c_sinR, p * T, tb)
                    for j in range(2):
                        a, at = proj_fm(wa, wat, j, tb)
                        b, bt = proj_fm(wb, wbt, j, tb)
                        h = hp * 2 + j
                        rope_out(dstT[:, h, tb * 512:(tb + 1) * 512], a, at, b, bt, tt, [S.tk("bx", un, h, tb)])
        for h in range(4):
            for n2 in range(0, 8, 4):
                b, bt = next_bank()
                bb = b.bitcast(BF16)
                for n4 in range(4):
                    n = n2 + n4
                    S.add("pe", I("transpose", out=bb[:, n4 * 128:(n4 + 1) * 128], in_=kT[:, h, n * 128:(n + 1) * 128], identity=ident),
                          reads=[S.tk("bx", 1, h, n // 4)] + CT, writes=[bt])
                for n4 in range(4):
                    n = n2 + n4
                    S.add("dve", I("tensor_scalar", out=kz[:, n, h, :], in0=bb[:, n4 * 128:(n4 + 1) * 128], scalar1=zeta[:, h:h + 1], scalar2=None, op0=ALU.mult),
                          reads=[bt] + CT, writes=[S.tk("bx", 2, n, h)])
        for s in range(2):
            wv, wt = load_slot(colslab(w_in, l, (24 + s * 2) * 128), v16, MODE3)
            for tt_ in range(8):
                b, bt = next_bank()
                mm_group(b[:, 0:256], bt, [(nT[:, kc, tt_ * 128:(tt_ + 1) * 128], wv[:, kc, :]) for kc in range(16)], reads=[wt] + n_reads(tt_ // 4))
                copy_out(vtm[:, tt_, s * 256:(s + 1) * 256], b[:, 0:256], [bt], [S.tk("bx", 3, tt_, s)])
        for s in range(2):
            wv, wt = load_slot(colslab(w_in, l, (28 + s * 2) * 128), v16, MODE3)
            for j in range(2):
                for tb in range(2):
                    b, bt = proj_fm(wv, wt, j, tb)
                    S.add("act", I("activation", out=sgT[:, s * 2 + j, tb * 512:(tb + 1) * 512], in_=b, func=AF.Silu),
                          reads=[bt], writes=[S.tk("bx", 4, s * 2 + j, tb)])
        for tb in range(2):
            ybanks = []
            for h in range(4):
                yb, ybt = next_bank(4, 8)
                ybanks.append((yb, ybt))
            for n4 in range(4):
                n = tb * 4 + n4
                nsl = slice(n * 128, (n + 1) * 128)
                for h in range(4):
                    yb, ybt = ybanks[h]
                    sb_, sbt = next_bank(0, 4)
                    mm_group(sb_[:, 0:128], sbt, [(kT[:, h, nsl], qT[:, h, nsl])], reads=[S.tk("bx", 1, h, tb), S.tk("bx", 0, h, tb)])
                    state["ptr"] += 1
                    pr = state["ptr"] % 8
                    sm = pt.rearrange("p a (b c) -> p (a b) c", b=4)
                    smt = S.tk("pt", "b16", pr)
                    S.add("dve", I("tensor_tensor", out=sm[:, pr, :], in0=sb_[:, 0:128], in1=dmask[:, h, :], op=ALU.mult),
                          reads=[sbt] + CT, writes=[smt])
                    qxt = S.tk("pt", "b16", 8 + pr)
                    S.add("pool", I("tensor_tensor", out=sm[:, 8 + pr, :], in0=qT[:, h, nsl], in1=xi[:, h, :], op=ALU.mult),
                          reads=[S.tk("bx", 0, h, tb)] + CT, writes=[qxt])
                    mm_group(yb[:, n4 * 128:(n4 + 1) * 128], ybt,
                             [(vtm[:, n, h * 128:(h + 1) * 128], sm[:, pr, :]), (st16[:, h, :], sm[:, 8 + pr, :])],
                             reads=[S.tk("bx", 3, n, h // 2), smt, qxt, S.tk("st", 16, h), stt])
                    kb, kbt = next_bank(0, 4)
                    mm_group(kb[:, 0:128], kbt, [(kz[:, n, h, :], vtm[:, n, h * 128:(h + 1) * 128])], reads=[S.tk("bx", 2, n, h), S.tk("bx", 3, n, h // 2)])
                    s32t = S.tk("st", 32, h)
                    S.add("dve", I("scalar_tensor_tensor", out=st32[:, h, :], in0=st32[:, h, :], scalar=gam[h] ** 128, in1=kb[:, 0:128], op0=ALU.mult, op1=ALU.add),
                          reads=[kbt, s32t, stt], writes=[s32t])
                    S.add("act", I("activation", out=st16[:, h, :], in_=st32[:, h, :], func=AF.Copy), reads=[s32t], writes=[S.tk("st", 16, h)])
            for h in range(4):
                yb, ybt = ybanks[h]
                state["sq"] += 1
                r = state["sq"] % 2
                sqt = S.tk("sq", r)
                S.add("act", I("activation", out=sq[:, r, 0, :], in_=yb, func=AF.Copy), reads=[ybt], writes=[sqt])
                S.add("act", I("activation", out=sq[:, r, 1, :], in_=yb, func=AF.Square), reads=[ybt], writes=[sqt])
                s1, s1t = next_bank(0, 4)
                mm_group(s1, s1t, [(ones[:, :], sq[:, r, 0, :])], reads=[sqt] + CT)
                s2, s2t = next_bank(0, 4)
                mm_group(s2, s2t, [(ones[:, :], sq[:, r, 1, :])], reads=[sqt] + CT)
                f0, f1, f2 = S.tk("fs", 0), S.tk("fs", 1), S.tk("fs", 2)
                S.add("dve", I("tensor_scalar", out=fs[:, 0, :], in0=s1, scalar1=1.0 / 128, scalar2=None, op0=ALU.mult), reads=[s1t], writes=[f0])
                S.add("dve", I("tensor_tensor", out=fs[:, 1, :], in0=fs[:, 0, :], in1=fs[:, 0, :], op=ALU.mult), reads=[f0], writes=[f1])
                S.add("dve", I("scalar_tensor_tensor", out=fs[:, 1, :], in0=s2, scalar=1.0 / 128, in1=fs[:, 1, :], op0=ALU.mult, op1=ALU.subtract),
                      reads=[s2t, f1], writes=[f1])
                rsqrt_to(fs[:, 1, :], fs[:, 1, :], [f1] + CT, [f1], 1.0, 512)
                S.add("dve", I("tensor_tensor", out=fs[:, 2, :], in0=yb, in1=fs[:, 0, :], op=ALU.subtract), reads=[ybt, f0], writes=[f2])
                S.add("dve", I("scalar_tensor_tensor", out=fs[:, 2, :], in0=fs[:, 2, :], scalar=vecs[:, l, V_GN + h:V_GN + h + 1], in1=fs[:, 1, :],
                                                                   op0=ALU.mult, op1=ALU.mult), reads=[f2, f1] + CT, writes=[f2])
                S.add("dve", I("tensor_tensor", out=yrT[:, h, tb * 512:(tb + 1) * 512], in0=fs[:, 2, :], in1=sgT[:, h, tb * 512:(tb + 1) * 512], op=ALU.mult),
                      reads=[f2, S.tk("bx", 4, h, tb)], writes=[S.tk("bx", 5, h, tb)])
        S.add("sp", I("dma_start", out=st_cache[l], in_=st32[:]), reads=[S.tk("st", 32, h) for h in range(4)] + [stt], writes=[S.tk("stc", l)], dma_key="sts")
        out_proj(l, w_out, 512, lambda c, tb: yrT[:, c, tb * 512:(tb + 1) * 512], lambda tb: [S.tk("bx", 5, c, tb) for c in range(4)], MODE3)

    def mixer_c(l, p):
        S.reset_unit("bx")
        S.reset_unit("pt")
        cqn = v4(X[0]); ycT = v4(X[0])
        latall = bx[:, 1:4, :]
        knT = X[4]
        vh = X[5].rearrange("p (t d) -> p t d", d=128)
        nkeys = (p + 1) * T
        pos0 = p * T
        S.reset_unit("wkvb")
        wkt = S.tk("wkvb", 0)
        S.add("pool", I("dma_start", out=v2(wkvb), in_=rowslab(w_ukv, l, 0, 2, 0, 2048)), writes=[wkt], dma_key="wukv")
        wkv2 = v2(wkvb)
        if p > 0:
            S.add("sp", I("dma_start", out=latall[:, :, 0:pos0], in_=lat_cache[l][:, :, 0:pos0]), reads=[S.tk("latc", l)],
                  writes=[S.tk("bx", "lat", "prior")], dma_key="latl")
        wq0, wq0t = load_slot(colslab(w_in, l, 32 * 128), v16, "two")
        wq1, wq1t = load_slot(colslab(w_in, l, 34 * 128), v16, "two")
        for tb in range(2):
            for j in range(4):
                wv, wt = (wq0, wq0t) if j < 2 else (wq1, wq1t)
                b, bt = proj_fm(wv, wt, j % 2, tb)
                copy_out(fs[:, j, :], b, [bt], [S.tk("fs", j)])
            norm_fm(fs, lambda c, c0: S.tk("fs", c), 4, 512, [(0, 512)], lambda c: vecs[:, l, V_QN + c:V_QN + c + 1],
                    lambda c, c0, n, tb=tb: cqn[:, c, tb * 512:(tb + 1) * 512], lambda c, c0, tb=tb: S.tk("bx", 0, c, tb))
        wkv_, wkvt = load_slot(colslab(w_in, l, 36 * 128), v16, "two")
        wkr, wkrt = load_slot(colslab(w_in, l, 38 * 128), v16, "two")
        for tb in range(2):
            for j in range(2):
                b, bt = proj_fm(wkv_, wkvt, j, tb)
                copy_out(fs[:, j, :], b, [bt], [S.tk("fs", j)])
            norm_fm(fs, lambda c, c0: S.tk("fs", c), 2, 256, [(0, 512)], lambda c: vecs[:, l, V_KVN + c:V_KVN + c + 1],
                    lambda c, c0, n, tb=tb: latall[:, c, pos0 + tb * 512: pos0 + (tb + 1) * 512], lambda c, c0, tb=tb: S.tk("bx", "lat", "own", c, tb))
            tt = load_tab(c_cosM, c_sinM, pos0, tb)
            a, at = proj_fm(wkr, wkrt, 0, tb)
            b, bt = proj_fm(wkr, wkrt, 1, tb)
            rope_out(latall[:, 2, pos0 + tb * 512: pos0 + (tb + 1) * 512], a, at, b, bt, tt, [S.tk("bx", "lat", "own", 2, tb)])
        own_lat = [S.tk("bx", "lat", "own", c, tb) for c in range(3) for tb in range(2)]
        if p + 1 < npass:
            S.add("sp", I("dma_start", out=lat_cache[l][:, :, pos0:pos0 + T], in_=latall[:, :, pos0:pos0 + T]), reads=own_lat,
                  writes=[S.tk("latc", l)], dma_key="lats")
        lat_reads = own_lat + ([S.tk("bx", "lat", "prior")] if p > 0 else [])
        S.reset_unit("nT")
        wn, wnt = load_slot(rowslab(w_uq, l, 0, 4, 0, 1024), v4, "two")
        wr, wrt = load_slot(rowslab(w_uq, l, 0, 4, 1024, 1024), v4, "two")
        for tb in range(2):
            cq_reads = [S.tk("bx", 0, c, tb) for c in range(4)]
            for hq in range(8):
                b, bt = proj_fm(wn, wnt, hq, tb, src=cqn, src_reads=cq_reads, nk=4)
                copy_out(nT[:, hq, tb * 512:(tb + 1) * 512], b, [bt], [S.tk("nT", hq, tb)])
            tt = load_tab(c_cosM, c_sinM, pos0, tb)
            for i in range(4):
                a, at = proj_fm(wr, wrt, i, tb, src=cqn, src_reads=cq_reads, nk=4)
                b, bt = proj_fm(wr, wrt, 4 + i, tb, src=cqn, src_reads=cq_reads, nk=4)
                rope_out(nT[:, 8 + i, tb * 512:(tb + 1) * 512], a, at, b, bt, tt, [S.tk("nT", 8 + i, tb)])
        sc = 192.0 ** -0.5
        for hg in range(2):
            for h4 in range(4):
                h = hg * 4 + h4
                half = h % 2
                pair = h // 2
                kt_ = S.tk("bx", 4, "k")
                vt_ = S.tk("bx", 5, "v")
                for kb in range(nkeys // 512):
                    b, bt = next_bank(0, 4)
                    mm_group(b, bt, [(wkv2[:, c, h * 128:(h + 1) * 128], latall[:, c, kb * 512:(kb + 1) * 512]) for c in range(2)], reads=[wkt] + lat_reads)
                    copy_out(knT[:, kb * 512:(kb + 1) * 512], b, [bt], [kt_])
                    b, bt = next_bank(0, 4)
                    for t4 in range(4):
                        t_ = kb * 4 + t4
                        mm_group(b[:, t4 * 128:(t4 + 1) * 128], bt,
                                 [(latall[:, c, t_ * 128:(t_ + 1) * 128], wkv2[:, c, 1024 + h * 128: 1024 + (h + 1) * 128]) for c in range(2)], reads=[wkt] + lat_reads)
                    copy_out(X[5][:, kb * 512:(kb + 1) * 512], b, [bt], [vt_])
                for qb in range(2):
                    state["acc"] = state.get("acc", 0) + 1
                    ob, obt = banks[4 + state["acc"] % 2], bank_tok[4 + state["acc"] % 2]
                    sb_, sbt = banks[6 + state["acc"] % 2], bank_tok[6 + state["acc"] % 2]
                    qsl0 = qb * 512
                    gmax = 8 * p + 4 * qb + 4
                    for g in range(gmax):
                        jd = g - (8 * p + 4 * qb)
                        c0 = 128 * jd if jd > 0 else 0
                        cols = slice(c0, 512)
                        qcols = slice(qsl0 + c0, qsl0 + 512)
                        ksl = slice(g * 128, (g + 1) * 128)
                        st_, stt_ = next_bank(0, 4)
                        mm_group(st_[:, cols], stt_,
                                 [(knT[:, ksl], nT[:, h, qcols]),
                                  (latall[64 * half:64 * half + 64, 2, ksl], nT[64 * half:64 * half + 64, 8 + pair, qcols])],
                                 reads=[kt_, S.tk("nT", h, qb), S.tk("nT", 8 + pair, qb)] + lat_reads)
                        state["ptr"] += 1
                        pr = state["ptr"] % 4
                        ptt = S.tk("pt", pr)
                        S.add("act", I("activation", out=pt[:, pr, cols], in_=st_[:, cols], func=AF.Exp, scale=sc),
                              reads=[stt_], writes=[ptt])
                        if jd >= 0:
                            S.add("dve", I("tensor_tensor", out=pt[:, pr, c0:c0 + 128], in0=pt[:, pr, c0:c0 + 128], in1=trib, op=ALU.mult),
                                  reads=[ptt] + CT, writes=[ptt])
                        mm_group(ob[:, cols], obt, [(vh[:, g, :], pt[:, pr, cols])], reads=[vt_, ptt], start=(g == 0), stop=(g == gmax - 1))
                        mm_group(sb_[:, cols], sbt, [(ones[:, :], pt[:, pr, cols])], reads=[ptt] + CT, start=(g == 0), stop=(g == gmax - 1))
                    state["fsr"] += 1
                    r = state["fsr"] % 4
                    ft = S.tk("fs", r)
                    S.add("dve", I("reciprocal", out=fs[:, r, :], in_=sb_), reads=[sbt], writes=[ft])
                    S.add("dve", I("tensor_tensor", out=ycT[:, h4, qb * 512:(qb + 1) * 512], in0=ob, in1=fs[:, r, :], op=ALU.mult),
                          reads=[obt, ft], writes=[S.tk("bx", 0, h4, qb)])
            out_proj(l, w_out, 1024 + hg * 512, lambda c, tb: ycT[:, c, tb * 512:(tb + 1) * 512], lambda tb: [S.tk("bx", 0, c, tb) for c in range(4)])

    def xattn(l, p):
        norm_h_to_n(l, V_XA)
        S.reset_unit("bx")
        S.reset_unit("pt")
        S.reset_unit("wkvb")
        memn = v16(wkvb)
        xq = bx[:, 0:4, :].rearrange("p u (c t) -> p (u c) t", c=4)
        kTm = X[4].rearrange("p (c m) -> p c m", m=256)
        vm = X[5].rearrange("p (t n) -> p t n", t=2)
        memr = memT.rearrange("(c p) m -> p c m", p=128)
        ktoks = [S.tk("bx", 4, j) for j in range(16)]
        vtoks = [S.tk("bx", 5, mt, s_) for mt in range(2) for s_ in range(8)]
        xmode = "two" if p == 0 else "three"
        if p > 0:
            S.add("sp", I("dma_start", out=X[4], in_=memkv_cache[l][:, 0, :]), reads=[S.tk("mkvc", l)], writes=ktoks, dma_key="mkvl0")
            S.add("sp", I("dma_start", out=X[5], in_=memkv_cache[l][:, 1, :]), reads=[S.tk("mkvc", l)], writes=vtoks, dma_key="mkvl1")
        else:
            mem_kv_compute(l, memn, kTm, vm, memr)
            if npass > 1:
                S.add("sp", I("dma_start", out=memkv_cache[l][:, 0, :], in_=X[4]), reads=ktoks, writes=[S.tk("mkvc", l)], dma_key="mkvs0")
                S.add("sp", I("dma_start", out=memkv_cache[l][:, 1, :], in_=X[5]), reads=ktoks + vtoks, writes=[S.tk("mkvc", l)], dma_key="mkvs1")
        xattn_rest(l, p, xq, kTm, vm, xmode)

    def mem_kv_compute(l, memn, kTm, vm, memr):
        fsv = fs.rearrange("p a b -> p (a b)").rearrange("p (c m) -> p c m", m=256)
        fst = [S.tk("fs", i) for i in range(4)]
        ssb, sst = next_bank()
        for half in range(2):
            S.add("sp", I("dma_start", out=fsv, in_=memr[:, half * 8:(half + 1) * 8, :]), writes=fst, dma_key="meml")
            for cp in range(0, 8, 2):
                state["sq"] += 1
                r = state["sq"] % 2
                sqt = S.tk("sq", r)
                S.add("act", I("activation", out=sq[:, r, :, 0:256], in_=fsv[:, cp:cp + 2, :], func=AF.Square), reads=fst, writes=[sqt])
                mm_group(ssb[:, 0:256], sst, [(ones[:, :], sq[:, r, j, 0:256]) for j in range(2)], reads=[sqt] + CT,
                         start=(half == 0 and cp == 0), stop=(half == 1 and cp == 6))
        rt = S.tk("rstd", 0)
        rsqrt_to(rstd[:, 0:256], ssb[:, 0:256], [sst] + CT, [rt], 1.0 / D_MODEL, 256)
        mnt = S.tk("wkvb", "memn")
        for half in range(2):
            S.add("sp", I("dma_start", out=fsv, in_=memr[:, half * 8:(half + 1) * 8, :]), writes=fst, dma_key="meml")
            for c in range(8):
                cc = half * 8 + c
                S.add("dve", I("scalar_tensor_tensor", out=memn[:, cc, :], in0=fsv[:, c, :], scalar=vecs[:, l, V_MEM + cc:V_MEM + cc + 1],
                                                                          in1=rstd[:, 0:256], op0=ALU.mult, op1=ALU.mult),
                      reads=fst + [rt] + CT, writes=[mnt])
        for s in range(8):
            wv, wt = load_slot(colslab(xa_wkv, l, s * 256), v16, "two")
            for j in range(2):
                b, bt = next_bank()
                mm_group(b[:, 0:256], bt, [(wv[:, kc, j * 128:(j + 1) * 128], memn[:, kc, :]) for kc in range(16)], reads=[wt, mnt])
                copy_out(kTm[:, s * 2 + j, :], b[:, 0:256], [bt], [S.tk("bx", 4, s * 2 + j)])
        for s in range(8):
            wv, wt = load_slot(colslab(xa_wkv, l, 2048 + s * 256), v16, "two")
            for mt in range(2):
                b, bt = next_bank()
                mm_group(b[:, 0:256], bt, [(memn[:, kc, mt * 128:(mt + 1) * 128], wv[:, kc, :]) for kc in range(16)], reads=[wt, mnt])
                copy_out(vm[:, mt, s * 256:(s + 1) * 256], b[:, 0:256], [bt], [S.tk("bx", 5, mt, s)])

    def xattn_rest(l, p, xq, kTm, vm, xmode):
        for s in range(8):
            wv, wt = load_slot(colslab(xa_wq, l, s * 256), v16, xmode)
            for j in range(2):
                for tb in range(2):
                    b, bt = proj_fm(wv, wt, j, tb)
                    cj = s * 2 + j
                    copy_out(xq[:, cj, tb * 512:(tb + 1) * 512], b, [bt], [S.tk("bx", cj // 4, "q", cj, tb)])
        S.reset_unit("nT")
        sc = 512.0 ** -0.5
        for h in range(4):
            for tb in range(2):
                tsl = slice(tb * 512, (tb + 1) * 512)
                ptoks = []
                for mc in range(2):
                    b, bt = next_bank()
                    mm_group(b, bt, [(kTm[:, 4 * h + c, mc * 128:(mc + 1) * 128], xq[:, 4 * h + c, tsl]) for c in range(4)],
                             reads=[S.tk("bx", 4, 4 * h + c) for c in range(4)] + [S.tk("bx", h, "q", 4 * h + c, tb) for c in range(4)])
                    state["ptr"] += 1
                    pr = state["ptr"] % 4
                    ptt = S.tk("pt", pr)
                    S.add("act", I("activation", out=pt[:, pr, :], in_=b, func=AF.Exp, scale=sc), reads=[bt], writes=[ptt])
                    ptoks.append((pr, ptt))
                sb_, sbt = next_bank()
                mm_group(sb_, sbt, [(ones[:, :], pt[:, pr, :]) for pr, _ in ptoks], reads=[t for _, t in ptoks] + CT)
                state["fsr"] += 1
                r = state["fsr"] % 4
                ft = S.tk("fs", r)
                S.add("dve", I("reciprocal", out=fs[:, r, :], in_=sb_), reads=[sbt], writes=[ft])
                for c in range(4):
                    cj = 4 * h + c
                    b, bt = next_bank()
                    mm_group(b, bt, [(vm[:, mc, cj * 128:(cj + 1) * 128], pt[:, ptoks[mc][0], :]) for mc in range(2)],
                             reads=[S.tk("bx", 5, mc, cj // 2) for mc in range(2)] + [t for _, t in ptoks])
                    S.add("dve", I("tensor_tensor", out=nT[:, cj, tsl], in0=b, in1=fs[:, r, :], op=ALU.mult),
                          reads=[bt, ft], writes=[S.tk("nT", cj, tb)])
        for s in range(8):
            wv, wt = load_slot(colslab(xa_wo, l, s * 256), v16, xmode)
            for j in range(2):
                for tb in range(2):
                    b, bt = proj_fm(wv, wt, j, tb)
                    h_accum(b, bt, s * 2 + j, tb, 1.0)

    xTr = xT.rearrange("(c p) t -> p c t", p=128)
    yTr = yT.rearrange("(c p) t -> p c t", p=128)
    outs = []
    for p in range(npass):
        for c4 in range(0, 16, 4):
            q = "sp" if (c4 // 4) % 2 == 0 else "act"
            for tb in range(2):
                S.add("sp", I("dma_start", out=hT[:, c4:c4 + 4, tb * 512:(tb + 1) * 512], in_=xTr[:, c4:c4 + 4, p * T + tb * 512: p * T + (tb + 1) * 512]),
                      writes=[S.tk("hT", c, tb) for c in range(c4, c4 + 4)], dma_key=f"xin{c4}_{tb}")
        for l in range(L):
            if "f" in phases:
                ffn(l, w_g1, w_u1, w_d1, V_FFN1)
            if any(c in phases for c in "abc"):
                norm_h_to_n(l, V_MIX)
            if "a" in phases:
                mixer_a(l, p)
            if "b" in phases:
                mixer_b(l, p)
            if "c" in phases:
                mixer_c(l, p)
            if "x" in phases:
                xattn(l, p)
            if "g" in phases:
                ffn(l, w_g2, w_u2, w_d2, V_FFN2, restart=True)
        norm_fm(hT, h_tok, 16, D_MODEL, TBS, lambda c: vecs[:, 0, V_FIN + c:V_FIN + c + 1],
                lambda c, c0, n: hT[:, c, c0:c0 + n], h_tok)
        for c4 in range(0, 16, 4):
            for tb in range(2):
                o = S.add("sp", I("dma_start", out=yTr[:, c4:c4 + 4, p * T + tb * 512: p * T + (tb + 1) * 512], in_=hT[:, c4:c4 + 4, tb * 512:(tb + 1) * 512]),
                          reads=[S.tk("hT", c, tb) for c in range(c4, c4 + 4)], dma_key=f"yout{c4}_{tb}")
                outs.append(o)
    counts = S.emit(nc, final_waits=outs[-8:])
    return nc, counts


def _col(v):
    Ld, n = v.shape
    return np.ascontiguousarray(v.reshape(Ld, n // 128, 128).transpose(2, 0, 1))


def _swap_halves(w, width):
    K, N = w.shape
    half = width // 2
    w4 = w.reshape(K, N // width, 2, half)
    return w4[:, :, ::-1, :].reshape(K, N)


def prep_inputs(inp, depth=DEPTH, seq=SEQ):
    L = depth
    f32 = np.float32
    g = {k: np.asarray(v, dtype=f32) for k, v in inp.items()}
    d = {}
    d["w_g1"] = g["ffn1_w_gate"][:L]; d["w_u1"] = g["ffn1_w_up"][:L]; d["w_d1"] = g["ffn1_w_down"][:L]
    d["w_g2"] = g["ffn2_w_gate"][:L]; d["w_u2"] = g["ffn2_w_up"][:L]; d["w_d2"] = g["ffn2_w_down"][:L]
    wi = g["w_in"][:L]
    ext = np.empty((L, D_MODEL, 5120), f32)
    for l in range(L):
        w = wi[l]
        rq = w[:, 1024:1536]; rk = w[:, 1536:2048]; kr = w[:, 3840:3904]
        ext[l] = np.concatenate([
            w[:, 0:512], w[:, 512:1024],
            rq, _swap_halves(rq, 128), rk, _swap_halves(rk, 128),
            w[:, 2048:2560], w[:, 2560:3072], w[:, 3072:3584], w[:, 3584:3840],
            kr, kr, _swap_halves(kr, 64), _swap_halves(kr, 64)], axis=1)
    d["w_in"] = ext
    uq = g["w_uq"][:L].reshape(L, 512, 8, 192)
    nope = uq[:, :, :, :128].reshape(L, 512, 1024)
    ropeq = np.ascontiguousarray(uq[:, :, :, 128:]).reshape(L, 512, 512)
    ropes = np.stack([_swap_halves(ropeq[l], 64) for l in range(L)])
    d["w_uq"] = np.ascontiguousarray(np.concatenate([nope, ropeq, ropes], axis=2))
    ukv = g["w_ukv"][:L].reshape(L, 256, 8, 256)
    d["w_ukv"] = np.ascontiguousarray(np.concatenate([ukv[:, :, :, :128].reshape(L, 256, 1024), ukv[:, :, :, 128:].reshape(L, 256, 1024)], axis=2))
    d["w_out"] = g["w_out"][:L]
    d["xa_wq"] = g["xa_wq"][:L]; d["xa_wkv"] = g["xa_wkv"][:L]; d["xa_wo"] = g["xa_wo"][:L]
    vec = np.zeros((128, L, NV), f32)
    vec[:, :, V_FFN1:V_FFN1 + 16] = _col(g["ffn1_norm"][:L])
    vec[:, :, V_MIX:V_MIX + 16] = _col(g["mix_norm"][:L])
    vec[:, :, V_XA:V_XA + 16] = _col(g["xa_norm"][:L])
    vec[:, :, V_MEM:V_MEM + 16] = _col(g["mem_norm"][:L])
    vec[:, :, V_FFN2:V_FFN2 + 16] = _col(g["ffn2_norm"][:L])
    vec[:, :, V_QN:V_QN + 4] = _col(g["q_norm"][:L])
    vec[:, :, V_KVN:V_KVN + 2] = _col(g["kv_norm"][:L])
    vec[:, :, V_GN:V_GN + 4] = _col(g["ret_gn"][:L])
    vec[:, :, V_FIN:V_FIN + 16] = _col(np.broadcast_to(g["final_norm"][None], (L, D_MODEL)))
    d["vecs"] = vec
    d["sgu_rep"] = np.ascontiguousarray(np.broadcast_to(g["sgu_norm"][:L, None, :], (L, 128, 512)))
    d["sgu_wsT"] = np.ascontiguousarray(g["sgu_w_s"][:L].transpose(0, 3, 1, 2))
    d["sgu_b"] = np.ascontiguousarray(g["sgu_b"][:L].reshape(L, 1, 512))
    i = np.arange(128)
    d["c_ident"] = np.eye(128, dtype=f32)
    d["c_ones"] = np.ones((128, 128), f32)
    d["c_tri"] = (i[:, None] <= i[None, :]).astype(f32)
    gam = 1.0 - 2.0 ** (-5.0 - np.arange(4, dtype=np.float64))
    diff = (i[None, :] - i[:, None]).astype(np.float64)
    dm = np.where(diff[None] >= 0, gam[:, None, None] ** np.maximum(diff, 0)[None], 0.0) * (128 ** -0.5)
    d["c_dmask"] = np.ascontiguousarray(dm.transpose(1, 0, 2)).astype(f32)
    xi_ = gam[:, None] ** (i[None, :] + 1.0)
    d["c_xi"] = np.ascontiguousarray(np.broadcast_to(xi_[None], (128, 4, 128))).astype(f32)
    d["c_zeta"] = np.ascontiguousarray((gam[None, :] ** (127.0 - i[:, None])) * (128 ** -0.5)).astype(f32)
    pos = np.arange(seq, dtype=np.float32)

    def tables(dim, reps):
        half = dim // 2
        inv = (np.float32(10000.0) ** (-np.arange(half, dtype=np.float32) * np.float32(2.0) / np.float32(dim))).astype(np.float32)
        ang = (pos[None, :] * inv[:, None]).astype(np.float32)
        c = np.cos(ang).astype(f32); s = np.sin(ang).astype(f32)
        cosT = np.concatenate([c, c] * reps, axis=0)
        sinT = np.concatenate([-s, s] * reps, axis=0)
        return np.ascontiguousarray(cosT), np.ascontiguousarray(sinT)
    d["c_cosR"], d["c_sinR"] = tables(128, 1)
    d["c_cosM"], d["c_sinM"] = tables(64, 2)
    return d


_CACHE = {}


def kernel(**inputs):
    x = np.asarray(inputs["x"], dtype=np.float32)
    mem = np.asarray(inputs["mem"], dtype=np.float32)
    shared = prep_inputs(inputs)
    if "nc" not in _CACHE:
        _CACHE["nc"] = build()[0]
    nc = _CACHE["nc"]
    in_maps = []
    for core in range(8):
        b = core // 2
        m = dict(shared)
        m["xT"] = np.ascontiguousarray(x[b].T)
        m["memT"] = np.ascontiguousarray(mem[b].T)
        in_maps.append(m)
    res = run_bass_kernel_spmd(nc, in_maps, core_ids=list(range(8)))
    out = np.empty((BATCH, SEQ, D_MODEL), np.float32)
    for b in range(BATCH):
        out[b] = res.results[2 * b]["yT"].T
    return out
```

```python
import math
import numpy as np
import ml_dtypes
import concourse.bass as bass
import concourse.mybir as mybir
from concourse.bass_utils import run_bass_kernel_spmd

F32 = mybir.dt.float32
BF16 = mybir.dt.bfloat16
AF = mybir.ActivationFunctionType
ALU = mybir.AluOpType

D_MODEL = 2048
SEQ = 4096
BATCH = 4
DEPTH = 4
D_FF = 5632
T = 1024
EPS = 1e-6
NV = 106

ENGS = ["pe", "act", "dve", "pool", "sp"]
EPOCH = 20000


def I(method, **kw):
    return lambda e: getattr(e, method)(**kw)


class Buf:
    __slots__ = ("name", "last_w", "readers")

    def __init__(self, name, hazard=None):
        self.name = name
        self.last_w = None
        self.readers = dict(hazard) if hazard else {}


class Op:
    __slots__ = ("eng", "fn", "waits", "ord", "needs_inc", "sem_i", "val", "is_dma", "dkey")


class Sched:
    def __init__(self):
        self.ops = {e: [] for e in ENGS}
        self.seen = {e: {} for e in ENGS}
        self.dma_n = {}
        self.units = {}
        self.hazard = {}

    def tk(self, unit, *key):
        d = self.units.setdefault(unit, {})
        b = d.get(key)
        if b is None:
            b = Buf(f"{unit}{key}", self.hazard.get(unit))
            d[key] = b
        return b

    def reset_unit(self, unit):
        hz = dict(self.hazard.get(unit, {}))
        for b in self.units.get(unit, {}).values():
            cands = list(b.readers.values())
            if b.last_w is not None:
                cands.append(b.last_w)
            for op in cands:
                k = ("dma", op.dkey) if op.is_dma else op.eng
                if k not in hz or hz[k].ord < op.ord:
                    hz[k] = op
        self.hazard[unit] = hz
        self.units[unit] = {}

    def add(self, eng, fn, reads=(), writes=(), dma_key=None):
        op = Op()
        op.eng = eng
        op.fn = fn
        op.needs_inc = False
        op.is_dma = dma_key is not None
        op.dkey = dma_key
        op.sem_i = 0
        op.val = 0
        if op.is_dma:
            n = self.dma_n.get(dma_key, 0) + 1
            self.dma_n[dma_key] = n
            op.ord = n
        else:
            op.ord = len(self.ops[eng])
        deps = []
        for b in reads:
            if b.last_w is not None:
                d = b.last_w
                if d.is_dma or op.is_dma or not (d.eng == eng and eng == "pe"):
                    deps.append(d)
        for b in writes:
            cands = list(b.readers.values())
            if b.last_w is not None:
                cands.append(b.last_w)
            for d in cands:
                if d.is_dma or op.is_dma or d.eng != eng:
                    deps.append(d)
        waits = {}
        seen = self.seen[eng]
        for d in deps:
            if d is op:
                continue
            key = ("dma", d.dkey) if d.is_dma else ("eng", d.eng)
            if seen.get(key, -1) >= d.ord:
                continue
            if key not in waits or waits[key].ord < d.ord:
                waits[key] = d
        for key, d in waits.items():
            seen[key] = d.ord
            d.needs_inc = True
        op.waits = list(waits.values())
        rk = ("dma", dma_key) if op.is_dma else eng
        for b in reads:
            b.readers[rk] = op
        for b in writes:
            b.last_w = op
            b.readers = {}
        self.ops[eng].append(op)
        return op

    def emit(self, nc, final_waits=()):
        nsem = {}
        for e in ENGS:
            c = 0
            for op in self.ops[e]:
                if op.is_dma or not op.needs_inc:
                    continue
                c += 1
                op.sem_i = (c - 1) // EPOCH
                op.val = (c - 1) % EPOCH + 1
            nsem[e] = (c + EPOCH - 1) // EPOCH if c else 0
        sems = {}
        for e in ENGS:
            for i in range(max(nsem[e], 1)):
                sems[("eng", e, i)] = nc.alloc_semaphore(name=f"s_{e}_{i}")
        for k in self.dma_n:
            sems[("dma", k)] = nc.alloc_semaphore(name=f"d_{k}")

        def sem_of(d):
            if d.is_dma:
                return sems[("dma", d.dkey)], 16 * d.ord
            return sems[("eng", d.eng, d.sem_i)], d.val

        def run(e, engine):
            for op in self.ops[e]:
                for d in op.waits:
                    s, v = sem_of(d)
                    engine.wait_ge(s, v)
                ins = op.fn(engine)
                if op.is_dma:
                    ins.then_inc(sems[("dma", op.dkey)], 16)
                elif op.needs_inc:
                    ins.then_inc(sems[("eng", e, op.sem_i)], 1)
            if e == "sp":
                for d in final_waits:
                    s, v = sem_of(d)
                    engine.wait_ge(s, v)

        with nc.Block() as block:
            @block.tensor
            def _(eng):
                run("pe", eng)

            @block.scalar
            def _(eng):
                run("act", eng)

            @block.vector
            def _(eng):
                run("dve", eng)

            @block.gpsimd
            def _(eng):
                run("pool", eng)

            @block.sync
            def _(eng):
                run("sp", eng)
        return {e: len(self.ops[e]) for e in ENGS}


V_FFN1, V_MIX, V_XA, V_MEM, V_FFN2, V_QN, V_KVN, V_GN, V_FIN = 0, 16, 32, 48, 64, 80, 84, 86, 90


def build(depth=DEPTH, npass=SEQ // T, use_gelu_tanh_lut=True, phases="fabcxg"):
    nc = bass.Bass("TRN2", target_bir_lowering=False)
    S = Sched()
    L = depth
    NTOK = npass * T

    def din(name, shape, dt=F32):
        return nc.dram_tensor(name, list(shape), dt, kind="ExternalInput").ap()

    xT = din("xT", [D_MODEL, NTOK])
    memT = din("memT", [D_MODEL, 256])
    yT = nc.dram_tensor("yT", [D_MODEL, NTOK], F32, kind="ExternalOutput").ap()
    w_g1 = din("w_g1", [L, D_MODEL, D_FF]); w_u1 = din("w_u1", [L, D_MODEL, D_FF]); w_d1 = din("w_d1", [L, D_FF, D_MODEL])
    w_g2 = din("w_g2", [L, D_MODEL, D_FF]); w_u2 = din("w_u2", [L, D_MODEL, D_FF]); w_d2 = din("w_d2", [L, D_FF, D_MODEL])
    w_in = din("w_in", [L, D_MODEL, 5120])
    w_uq = din("w_uq", [L, 512, 2048])
    w_ukv = din("w_ukv", [L, 256, 2048])
    w_out = din("w_out", [L, D_MODEL, D_MODEL])
    xa_wq = din("xa_wq", [L, D_MODEL, D_MODEL])
    xa_wkv = din("xa_wkv", [L, D_MODEL, 4096])
    xa_wo = din("xa_wo", [L, D_MODEL, D_MODEL])
    vecs_d = din("vecs", [128, L, NV])
    sgu_rep_d = din("sgu_rep", [L, 128, 512])
    sgu_wsT_d = din("sgu_wsT", [L, 128, 4, 128])
    sgu_b_d = din("sgu_b", [L, 1, 512])
    c_ident = din("c_ident", [128, 128]); c_ones = din("c_ones", [128, 128]); c_tri = din("c_tri", [128, 128])
    c_dmask = din("c_dmask", [128, 4, 128]); c_xi = din("c_xi", [128, 4, 128]); c_zeta = din("c_zeta", [128, 4])
    c_cosR = din("c_cosR", [128, SEQ]); c_sinR = din("c_sinR", [128, SEQ])
    c_cosM = din("c_cosM", [128, SEQ]); c_sinM = din("c_sinM", [128, SEQ])
    lat_cache = nc.dram_tensor("lat_cache", [L, 128, 3, NTOK], BF16, kind="Internal").ap()
    st_cache = nc.dram_tensor("st_cache", [L, 128, 4, 128], F32, kind="Internal").ap()
    memkv_cache = nc.dram_tensor("memkv_cache", [L, 128, 2, 4096], BF16, kind="Internal").ap()

    def sb(name, shape, dt):
        return nc.alloc_sbuf_tensor("sb_" + name, list(shape), dt).ap()

    hT = sb("hT", [128, 16, T], F32)
    nT = sb("nT", [128, 16, T], BF16)
    bx = sb("bx", [128, 6, 4096], BF16)
    ring = sb("ring", [128, 2, 4096], BF16)
    wkvb = sb("wkvb", [128, 4096], BF16)
    pt = sb("pt", [128, 4, 512], BF16)
    tab = sb("tab", [128, 2, 512], F32)
    sq = sb("sq", [128, 2, 2, 512], BF16)
    rstd = sb("rstd", [128, T], F32)
    fs = sb("fs", [128, 4, 512], F32)
    vecs = sb("vecs", [128, L, NV], F32)
    ident = sb("ident", [128, 128], BF16); ones = sb("ones", [128, 128], BF16); tri = sb("tri", [128, 128], F32)
    trib = sb("trib", [128, 128], BF16)
    dmask = sb("dmask", [128, 4, 128], F32); xi = sb("xi", [128, 4, 128], F32); zeta = sb("zeta", [128, 4], F32)
    sgurep = sb("sgurep", [128, 512], F32)
    wsm = sb("wsm", [128, 4, 128], BF16)
    brow = sb("brow", [1, 512], BF16)
    st32 = sb("st32", [128, 4, 128], F32); st16 = sb("st16", [128, 4, 128], BF16)
    small = sb("small", [128, 16], F32)
    banks = [nc.alloc_psum_tensor(f"bank{i}", [128, 512], F32).ap() for i in range(8)]

    state = {"bank": 0, "slot": 0, "ev": 0, "sq": 0, "fsr": 0, "ptr": 0}
    bank_tok = [S.tk("bank", i) for i in range(8)]

    def next_bank(lo=0, hi=8):
        state["bank"] += 1
        i = lo + state["bank"] % (hi - lo)
        return banks[i], bank_tok[i]

    def units_all():
        return [("bx", i) for i in range(6)] + [("ring", 0), ("ring", 1)]

    def slot_ap(u):
        if u[0] == "wkvb":
            return wkvb[:, :]
        return (ring if u[0] == "ring" else bx)[:, u[1], :]

    def slot_tok(u):
        return S.tk(u[0], "slot", u[1])

    slotsets = {"all": units_all(), "two": [("ring", 0), ("ring", 1)], "three": [("ring", 0), ("ring", 1), ("wkvb", 0)]}

    def load_slot(src_ap, view_fn, mode):
        us = slotsets[mode]
        state["slot"] += 1
        u = us[state["slot"] % len(us)]
        ap = view_fn(slot_ap(u))
        tok = slot_tok(u)
        S.add("pool", I("dma_start", out=ap, in_=src_ap), writes=[tok], dma_key=f"w_{u[0]}{u[1]}")
        return ap, tok

    def v16(a):
        return a.rearrange("p (k n) -> p k n", k=16)

    def v4(a):
        return a.rearrange("p (k n) -> p k n", k=4)

    def v2(a):
        return a.rearrange("p (k n) -> p k n", k=2)

    def colslab(W, l, c0, ncols=256):
        return W[l].rearrange("(k p) n -> p k n", p=128)[:, :, c0:c0 + ncols]

    def rowslab(W, l, r0, nk, c0, ncols):
        return W[l][r0:r0 + nk * 128, c0:c0 + ncols].rearrange("(k p) n -> p k n", p=128)

    def evac_engine():
        state["ev"] += 1
        return "act" if state["ev"] % 2 else "dve"

    def copy_out(dst, src, reads, writes, eng=None, scale=None):
        eng = eng or evac_engine()
        if eng == "act":
            if scale is None:
                S.add("act", I("activation", out=dst, in_=src, func=AF.Copy), reads=reads, writes=writes)
            else:
                S.add("act", I("activation", out=dst, in_=src, func=AF.Copy, scale=scale), reads=reads, writes=writes)
        else:
            if scale is None:
                S.add("dve", I("tensor_copy", out=dst, in_=src), reads=reads, writes=writes)
            else:
                S.add("dve", I("tensor_scalar", out=dst, in0=src, scalar1=scale, scalar2=None, op0=ALU.mult), reads=reads, writes=writes)

    def mm_group(out_ap, out_tok, pairs, reads, start=True, stop=True):
        n = len(pairs)

        def fn(e):
            ins = None
            for i, (l_, r_) in enumerate(pairs):
                ins = e.matmul(out_ap, lhsT=l_, rhs=r_, start=(start and i == 0), stop=(stop and i == n - 1))
            return ins
        return S.add("pe", fn, reads=reads, writes=[out_tok])

    def rsqrt_to(dst, src, src_toks, dst_toks, scale, n):
        S.add("act", I("activation", out=dst, in_=src, func=AF.Sqrt, scale=scale, bias=eps_col[:, 0:1]),
              reads=src_toks, writes=dst_toks)
        S.add("dve", I("reciprocal", out=dst, in_=dst), reads=dst_toks, writes=dst_toks)

    def gelu_to(dst, src, reads, writes):
        n = src.shape[-1]
        if use_gelu_tanh_lut:
            S.add("act", I("activation", out=dst, in_=src, func=AF.Gelu_apprx_tanh), reads=reads, writes=writes)
            return
        state["fsr"] += 1
        r = state["fsr"] % 4
        tmp = fs[:, r, 0:n]
        tt = [S.tk("fs", r)]
        S.add("act", I("activation", out=tmp, in_=src, func=AF.Square), reads=reads, writes=tt)
        S.add("dve", I("tensor_scalar", out=tmp, in0=tmp, scalar1=0.044715, scalar2=1.0, op0=ALU.mult, op1=ALU.add), reads=tt, writes=tt)
        S.add("dve", I("tensor_tensor", out=tmp, in0=tmp, in1=src, op=ALU.mult), reads=tt + list(reads), writes=tt)
        S.add("act", I("activation", out=tmp, in_=tmp, func=AF.Sigmoid, scale=1.5957691216057308), reads=tt, writes=tt)
        S.add("dve", I("tensor_tensor", out=dst, in0=tmp, in1=src, op=ALU.mult), reads=tt + list(reads), writes=writes)

    ctok = S.tk("const", 0)
    eps_col = small[:, 0:1]

    def cload(dst, src, q="sp", key="c0"):
        S.add(q, I("dma_start", out=dst, in_=src), writes=[ctok], dma_key=key)

    cload(vecs, vecs_d, "sp", "c0")
    cload(tri, c_tri, "sp", "c0")
    cload(dmask, c_dmask, "sp", "c0")
    cload(xi, c_xi, "sp", "c0")
    cload(zeta, c_zeta, "sp", "c0")
    cload(ident, c_ident, "pool", "c1")
    cload(ones, c_ones, "pool", "c1")
    cload(trib, c_tri, "pool", "c1")
    S.add("dve", I("memset", ap=small[:, 0:1], constant=EPS), writes=[ctok])
    CT = [ctok]

    def norm_fm(src, src_tok_fn, nch, dtrue, ncols_list, gcol_fn, dst_fn, dst_tok_fn, in_place_f32=False):
        for (c0, n) in ncols_list:
            ssb, sst = next_bank()
            for cp in range(0, nch, 2):
                k = min(2, nch - cp)
                state["sq"] += 1
                r = state["sq"] % 2
                sqt = S.tk("sq", r)
                rd = [src_tok_fn(c, c0) for c in range(cp, cp + k)]
                S.add("act", I("activation", out=sq[:, r, 0:k, 0:n], in_=src[:, cp:cp + k, c0:c0 + n], func=AF.Square),
                      reads=rd, writes=[sqt])
                mm_group(ssb[:, 0:n], sst, [(ones[:, :], sq[:, r, j, 0:n]) for j in range(k)], reads=[sqt] + CT,
                         start=(cp == 0), stop=(cp + k >= nch))
            rt = S.tk("rstd", c0)
            rsqrt_to(rstd[:, c0:c0 + n], ssb[:, 0:n], [sst] + CT, [rt], 1.0 / dtrue, n)
            for c in range(nch):
                S.add("dve", I("scalar_tensor_tensor", out=dst_fn(c, c0, n), in0=src[:, c, c0:c0 + n], scalar=gcol_fn(c),
                                                                   in1=rstd[:, c0:c0 + n], op0=ALU.mult, op1=ALU.mult),
                      reads=[src_tok_fn(c, c0), rt] + CT, writes=[dst_tok_fn(c, c0)])

    TBS = [(0, 512), (512, 512)]

    def h_tok(c, c0):
        return S.tk("hT", c, c0 // 512)

    def n_tok(c, c0):
        return S.tk("nT", c, c0 // 512)

    def norm_h_to_n(l, vofs):
        S.reset_unit("nT")
        norm_fm(hT, h_tok, 16, D_MODEL, TBS, lambda c: vecs[:, l, vofs + c:vofs + c + 1],
                lambda c, c0, n: nT[:, c, c0:c0 + n], n_tok)

    def n_reads(tb):
        return [S.tk("nT", c, tb) for c in range(16)]

    def h_accum(bank, btok, oc, tb, scale, extra_reads=()):
        ht = S.tk("hT", oc, tb)
        dst = hT[:, oc, tb * 512:(tb + 1) * 512]
        S.add("dve", I("scalar_tensor_tensor", out=dst, in0=bank, scalar=scale, in1=dst, op0=ALU.mult, op1=ALU.add),
              reads=[btok, ht], writes=[ht])

    def ffn(l, Wg, Wu, Wd, vofs, restart=False):
        if restart:
            state["slot"] = -1
        norm_h_to_n(l, vofs)
        S.reset_unit("bx")
        S.reset_unit("pt")
        for s in range(D_FF // 256):
            gv, gt = load_slot(colslab(Wg, l, s * 256), v16, "all")
            uv, ut = load_slot(colslab(Wu, l, s * 256), v16, "all")
            dv, dt_ = load_slot(rowslab(Wd, l, s * 256, 2, 0, D_MODEL), v2, "all")
            for tb in range(2):
                tsl = slice(tb * 512, (tb + 1) * 512)
                state["ptr"] += 1
                pr = state["ptr"] % 2
                for j in range(2):
                    gb, gbt = next_bank()
                    mm_group(gb, gbt, [(gv[:, kc, j * 128:(j + 1) * 128], nT[:, kc, tsl]) for kc in range(16)], reads=[gt] + n_reads(tb))
                    ub, ubt = next_bank()
                    mm_group(ub, ubt, [(uv[:, kc, j * 128:(j + 1) * 128], nT[:, kc, tsl]) for kc in range(16)], reads=[ut] + n_reads(tb))
                    state["fsr"] += 1
                    r = state["fsr"] % 4
                    ft = S.tk("fs", r)
                    S.add("act", I("activation", out=fs[:, r, :], in_=gb, func=AF.Silu), reads=[gbt], writes=[ft])
                    at = S.tk("pt", pr, j)
                    S.add("dve", I("tensor_tensor", out=pt[:, pr * 2 + j, :], in0=fs[:, r, :], in1=ub, op=ALU.mult),
                          reads=[ft, ubt], writes=[at])
                for oc in range(16):
                    ob, obt = next_bank()
                    mm_group(ob, obt, [(dv[:, j, oc * 128:(oc + 1) * 128], pt[:, pr * 2 + j, :]) for j in range(2)],
                             reads=[dt_, S.tk("pt", pr, 0), S.tk("pt", pr, 1)])
                    h_accum(ob, obt, oc, tb, 0.5)

    def proj_fm(wv, wt, j, tb, src=None, src_reads=None, nk=16):
        b, bt = next_bank()
        tsl = slice(tb * 512, (tb + 1) * 512)
        src = nT if src is None else src
        rd = n_reads(tb) if src_reads is None else src_reads
        mm_group(b, bt, [(wv[:, kc, j * 128:(j + 1) * 128], src[:, kc, tsl]) for kc in range(nk)], reads=[wt] + rd)
        return b, bt

    def out_proj(l, W, r0, src_fn, src_reads_fn, mode="two"):
        for ch in range(2):
            wv, wt = load_slot(rowslab(W, l, r0, 4, ch * 1024, 1024), v4, mode)
            for tb in range(2):
                for o8 in range(8):
                    b, bt = next_bank()
                    mm_group(b, bt, [(wv[:, c, o8 * 128:(o8 + 1) * 128], src_fn(c, tb)) for c in range(4)],
                             reads=[wt] + src_reads_fn(tb))
                    h_accum(b, bt, ch * 8 + o8, tb, 1.0)

    X = [bx[:, i, :] for i in range(6)]
    MODE3 = "three"

    def load_tab(cosd, sind, pos0, tb):
        tt = S.tk("tab", 0)
        S.add("sp", I("dma_start", out=tab[:, 0, :], in_=cosd[:, pos0 + tb * 512: pos0 + (tb + 1) * 512]), writes=[tt], dma_key="tab")
        S.add("sp", I("dma_start", out=tab[:, 1, :], in_=sind[:, pos0 + tb * 512: pos0 + (tb + 1) * 512]), writes=[tt], dma_key="tab")
        return tt

    def rope_out(dst, a, at, b, bt, tt, wtoks):
        state["fsr"] += 1
        r0 = state["fsr"] % 4
        state["fsr"] += 1
        r1 = state["fsr"] % 4
        t0 = S.tk("fs", r0)
        t1 = S.tk("fs", r1)
        S.add("dve", I("tensor_tensor", out=fs[:, r0, :], in0=a, in1=tab[:, 0, :], op=ALU.mult), reads=[at, tt], writes=[t0])
        S.add("dve", I("tensor_tensor", out=fs[:, r1, :], in0=b, in1=tab[:, 1, :], op=ALU.mult), reads=[bt, tt], writes=[t1])
        S.add("dve", I("tensor_tensor", out=dst, in0=fs[:, r0, :], in1=fs[:, r1, :], op=ALU.add), reads=[t0, t1], writes=wtoks)

    def mixer_a(l, p):
        S.reset_unit("bx")
        S.reset_unit("pt")
        S.reset_unit("wkvb")
        uT = v4(X[0]); vt = X[1].rearrange("p (t n) -> p t n", t=8); yaT = v4(X[2])
        lt = S.tk("sgu_l", 0)
        S.add("sp", I("dma_start", out=sgurep, in_=sgu_rep_d[l]), writes=[lt], dma_key="sgul")
        S.add("sp", I("dma_start", out=fs[:, 3, :].rearrange("p (g t) -> p g t", g=4), in_=sgu_wsT_d[l]), writes=[S.tk("fs", 3)], dma_key="sgul2")
        S.add("pool", I("dma_start", out=brow, in_=sgu_b_d[l]), writes=[lt], dma_key="sgub")
        wt_ = S.tk("wsm", 0)
        for g in range(4):
            S.add("dve", I("tensor_tensor", out=wsm[:, g, :], in0=fs[:, 3, g * 128:(g + 1) * 128], in1=tri, op=ALU.mult),
                  reads=[S.tk("fs", 3)] + CT, writes=[wt_])
        for s in range(2):
            wv, wt = load_slot(colslab(w_in, l, s * 256), v16, MODE3)
            for j in range(2):
                for tb in range(2):
                    b, bt = proj_fm(wv, wt, j, tb)
                    gelu_to(uT[:, s * 2 + j, tb * 512:(tb + 1) * 512], b, [bt], [S.tk("bx", 0, s * 2 + j, tb)])
        wv2, wt2 = load_slot(colslab(w_in, l, 512), v16, MODE3)
        wv3, wt3 = load_slot(colslab(w_in, l, 768), v16, MODE3)
        for tt_ in range(8):
            b, bt = next_bank()
            mm_group(b[:, 0:256], bt, [(nT[:, kc, tt_ * 128:(tt_ + 1) * 128], wv2[:, kc, :]) for kc in range(16)], reads=[wt2] + n_reads(tt_ // 4))
            mm_group(b[:, 256:512], bt, [(nT[:, kc, tt_ * 128:(tt_ + 1) * 128], wv3[:, kc, :]) for kc in range(16)], reads=[wt3] + n_reads(tt_ // 4))
            r = tt_ % 2
            gt = S.tk("fs", r)
            gelu_to(fs[:, r, :], b, [bt], [gt])
            sst = S.tk("small", 1 + r)
            S.add("dve", I("memset", ap=small[:, 1 + r:2 + r], constant=0.0), writes=[sst])
            S.add("dve", I("scalar_tensor_tensor", out=fs[:, 2 + r, :], in0=fs[:, r, :], scalar=1.0, in1=fs[:, r, :], op0=ALU.mult, op1=ALU.mult,
                                                               accum_out=small[:, 1 + r:2 + r]), reads=[gt], writes=[sst, S.tk("fs", 2 + r)])
            rsqrt_to(small[:, 1 + r:2 + r], small[:, 1 + r:2 + r], [sst] + CT, [sst], 1.0 / 512, 1)
            S.add("dve", I("scalar_tensor_tensor", out=vt[:, tt_, :], in0=fs[:, r, :], scalar=small[:, 1 + r:2 + r], in1=sgurep,
                                                                        op0=ALU.mult, op1=ALU.mult),
                  reads=[gt, sst, lt], writes=[S.tk("bx", 1, tt_)])
        for g in range(4):
            for tb in range(2):
                b, bt = next_bank()
                for n4 in range(4):
                    n = tb * 4 + n4
                    mm_group(b[:, n4 * 128:(n4 + 1) * 128], bt,
                             [(vt[:, n, g * 128:(g + 1) * 128], wsm[:, g, :]), (ones[0:1, :], brow[0:1, g * 128:(g + 1) * 128])],
                             reads=[S.tk("bx", 1, n), wt_, lt] + CT)
                S.add("dve", I("tensor_tensor", out=yaT[:, g, tb * 512:(tb + 1) * 512], in0=b, in1=uT[:, g, tb * 512:(tb + 1) * 512], op=ALU.mult),
                      reads=[bt, S.tk("bx", 0, g, tb)], writes=[S.tk("bx", 2, g, tb)])
        out_proj(l, w_out, 0, lambda c, tb: yaT[:, c, tb * 512:(tb + 1) * 512], lambda tb: [S.tk("bx", 2, c, tb) for c in range(4)], MODE3)

    def mixer_b(l, p):
        S.reset_unit("bx")
        S.reset_unit("pt")
        qT = v4(X[0]); kT = v4(X[1])
        kz = X[2].rearrange("p (n h d) -> p n h d", n=8, h=4)
        vtm = X[3].rearrange("p (t n) -> p t n", t=8)
        sgT = v4(X[4]); yrT = v4(X[5])
        gam = [1.0 - 2.0 ** (-5.0 - h) for h in range(4)]
        S.reset_unit("st")
        stt = S.tk("st", "init")
        if p == 0:
            S.add("dve", I("memset", ap=st32[:], constant=0.0), writes=[stt])
            S.add("dve", I("memset", ap=st16[:], constant=0.0), writes=[stt])
        else:
            S.add("sp", I("dma_start", out=st32[:], in_=st_cache[l]), reads=[S.tk("stc", l)], writes=[stt], dma_key="stl")
            S.add("dve", I("tensor_copy", out=st16[:], in_=st32[:]), reads=[stt], writes=[stt])
        for which, dstT, un in ((0, qT, 0), (1, kT, 1)):
            c_a = 8 + which * 8
            for hp in range(2):
                wa, wat = load_slot(colslab(w_in, l, (c_a + hp * 2) * 128), v16, MODE3)
                wb, wbt = load_slot(colslab(w_in, l, (c_a + 4 + hp * 2) * 128), v16, MODE3)
                for tb in range(2):
                    tt = load_tab(c_cosR, c_sinR, p * T, tb)
                    for j in range(2):
                        a, at = proj_fm(wa, wat, j, tb)
                        b, bt = proj_fm(wb, wbt, j, tb)
                        h = hp * 2 + j
                        rope_out(dstT[:, h, tb * 512:(tb + 1) * 512], a, at, b, bt, tt, [S.tk("bx", un, h, tb)])
        for h in range(4):
            for n2 in range(0, 8, 4):
                b, bt = next_bank()
                bb = b.bitcast(BF16)
                for n4 in range(4):
                    n = n2 + n4
                    S.add("pe", I("transpose", out=bb[:, n4 * 128:(n4 + 1) * 128], in_=kT[:, h, n * 128:(n + 1) * 128], identity=ident),
                          reads=[S.tk("bx", 1, h, n // 4)] + CT, writes=[bt])
                for n4 in range(4):
                    n = n2 + n4
                    S.add("dve", I("tensor_scalar", out=kz[:, n, h, :], in0=bb[:, n4 * 128:(n4 + 1) * 128], scalar1=zeta[:, h:h + 1], scalar2=None, op0=ALU.mult),
                          reads=[bt] + CT, writes=[S.tk("bx", 2, n, h)])
        for s in range(2):
            wv, wt = load_slot(colslab(w_in, l, (24 + s * 2) * 128), v16, MODE3)
            for tt_ in range(8):
                b, bt = next_bank()
                mm_group(b[:, 0:256], bt, [(nT[:, kc, tt_ * 128:(tt_ + 1) * 128], wv[:, kc, :]) for kc in range(16)], reads=[wt] + n_reads(tt_ // 4))
                copy_out(vtm[:, tt_, s * 256:(s + 1) * 256], b[:, 0:256], [bt], [S.tk("bx", 3, tt_, s)])
        for s in range(2):
            wv, wt = load_slot(colslab(w_in, l, (28 + s * 2) * 128), v16, MODE3)
            for j in range(2):
                for tb in range(2):
                    b, bt = proj_fm(wv, wt, j, tb)
                    S.add("act", I("activation", out=sgT[:, s * 2 + j, tb * 512:(tb + 1) * 512], in_=b, func=AF.Silu),
                          reads=[bt], writes=[S.tk("bx", 4, s * 2 + j, tb)])
        for tb in range(2):
            ybanks = []
            for h in range(4):
                yb, ybt = next_bank(4, 8)
                ybanks.append((yb, ybt))
            for n4 in range(4):
                n = tb * 4 + n4
                nsl = slice(n * 128, (n + 1) * 128)
                for h in range(4):
                    yb, ybt = ybanks[h]
                    sb_, sbt = next_bank(0, 4)
                    mm_group(sb_[:, 0:128], sbt, [(kT[:, h, nsl], qT[:, h, nsl])], reads=[S.tk("bx", 1, h, tb), S.tk("bx", 0, h, tb)])
                    state["ptr"] += 1
                    pr = state["ptr"] % 8
                    sm = pt.rearrange("p a (b c) -> p (a b) c", b=4)
                    smt = S.tk("pt", "b16", pr)
                    S.add("dve", I("tensor_tensor", out=sm[:, pr, :], in0=sb_[:, 0:128], in1=dmask[:, h, :], op=ALU.mult),
                          reads=[sbt] + CT, writes=[smt])
                    qxt = S.tk("pt", "b16", 8 + pr)
                    S.add("pool", I("tensor_tensor", out=sm[:, 8 + pr, :], in0=qT[:, h, nsl], in1=xi[:, h, :], op=ALU.mult),
                          reads=[S.tk("bx", 0, h, tb)] + CT, writes=[qxt])
                    mm_group(yb[:, n4 * 128:(n4 + 1) * 128], ybt,
                             [(vtm[:, n, h * 128:(h + 1) * 128], sm[:, pr, :]), (st16[:, h, :], sm[:, 8 + pr, :])],
                             reads=[S.tk("bx", 3, n, h // 2), smt, qxt, S.tk("st", 16, h), stt])
                    kb, kbt = next_bank(0, 4)
                    mm_group(kb[:, 0:128], kbt, [(kz[:, n, h, :], vtm[:, n, h * 128:(h + 1) * 128])], reads=[S.tk("bx", 2, n, h), S.tk("bx", 3, n, h // 2)])
                    s32t = S.tk("st", 32, h)
                    S.add("dve", I("scalar_tensor_tensor", out=st32[:, h, :], in0=st32[:, h, :], scalar=gam[h] ** 128, in1=kb[:, 0:128], op0=ALU.mult, op1=ALU.add),
                          reads=[kbt, s32t, stt], writes=[s32t])
                    S.add("act", I("activation", out=st16[:, h, :], in_=st32[:, h, :], func=AF.Copy), reads=[s32t], writes=[S.tk("st", 16, h)])
            for h in range(4):
                yb, ybt = ybanks[h]
                state["sq"] += 1
                r = state["sq"] % 2
                sqt = S.tk("sq", r)
                S.add("act", I("activation", out=sq[:, r, 0, :], in_=yb, func=AF.Copy), reads=[ybt], writes=[sqt])
                S.add("act", I("activation", out=sq[:, r, 1, :], in_=yb, func=AF.Square), reads=[ybt], writes=[sqt])
                s1, s1t = next_bank(0, 4)
                mm_group(s1, s1t, [(ones[:, :], sq[:, r, 0, :])], reads=[sqt] + CT)
                s2, s2t = next_bank(0, 4)
                mm_group(s2, s2t, [(ones[:, :], sq[:, r, 1, :])], reads=[sqt] + CT)
                f0, f1, f2 = S.tk("fs", 0), S.tk("fs", 1), S.tk("fs", 2)
                S.add("dve", I("tensor_scalar", out=fs[:, 0, :], in0=s1, scalar1=1.0 / 128, scalar2=None, op0=ALU.mult), reads=[s1t], writes=[f0])
                S.add("dve", I("tensor_tensor", out=fs[:, 1, :], in0=fs[:, 0, :], in1=fs[:, 0, :], op=ALU.mult), reads=[f0], writes=[f1])
                S.add("dve", I("scalar_tensor_tensor", out=fs[:, 1, :], in0=s2, scalar=1.0 / 128, in1=fs[:, 1, :], op0=ALU.mult, op1=ALU.subtract),
                      reads=[s2t, f1], writes=[f1])
                rsqrt_to(fs[:, 1, :], fs[:, 1, :], [f1] + CT, [f1], 1.0, 512)
                S.add("dve", I("tensor_tensor", out=fs[:, 2, :], in0=yb, in1=fs[:, 0, :], op=ALU.subtract), reads=[ybt, f0], writes=[f2])
                S.add("dve", I("scalar_tensor_tensor", out=fs[:, 2, :], in0=fs[:, 2, :], scalar=vecs[:, l, V_GN + h:V_GN + h + 1], in1=fs[:, 1, :],
                                                                   op0=ALU.mult, op1=ALU.mult), reads=[f2, f1] + CT, writes=[f2])
                S.add("dve", I("tensor_tensor", out=yrT[:, h, tb * 512:(tb + 1) * 512], in0=fs[:, 2, :], in1=sgT[:, h, tb * 512:(tb + 1) * 512], op=ALU.mult),
                      reads=[f2, S.tk("bx", 4, h, tb)], writes=[S.tk("bx", 5, h, tb)])
        S.add("sp", I("dma_start", out=st_cache[l], in_=st32[:]), reads=[S.tk("st", 32, h) for h in range(4)] + [stt], writes=[S.tk("stc", l)], dma_key="sts")
        out_proj(l, w_out, 512, lambda c, tb: yrT[:, c, tb * 512:(tb + 1) * 512], lambda tb: [S.tk("bx", 5, c, tb) for c in range(4)], MODE3)

    def mixer_c(l, p):
        S.reset_unit("bx")
        S.reset_unit("pt")
        cqn = v4(X[0]); ycT = v4(X[0])
        latall = bx[:, 1:4, :]
        knT = X[4]
        vh = X[5].rearrange("p (t d) -> p t d", d=128)
        nkeys = (p + 1) * T
        pos0 = p * T
        S.reset_unit("wkvb")
        wkt = S.tk("wkvb", 0)
        S.add("pool", I("dma_start", out=v2(wkvb), in_=rowslab(w_ukv, l, 0, 2, 0, 2048)), writes=[wkt], dma_key="wukv")
        wkv2 = v2(wkvb)
        if p > 0:
            S.add("sp", I("dma_start", out=latall[:, :, 0:pos0], in_=lat_cache[l][:, :, 0:pos0]), reads=[S.tk("latc", l)],
                  writes=[S.tk("bx", "lat", "prior")], dma_key="latl")
        wq0, wq0t = load_slot(colslab(w_in, l, 32 * 128), v16, "two")
        wq1, wq1t = load_slot(colslab(w_in, l, 34 * 128), v16, "two")
        for tb in range(2):
            for j in range(4):
                wv, wt = (wq0, wq0t) if j < 2 else (wq1, wq1t)
                b, bt = proj_fm(wv, wt, j % 2, tb)
                copy_out(fs[:, j, :], b, [bt], [S.tk("fs", j)])
            norm_fm(fs, lambda c, c0: S.tk("fs", c), 4, 512, [(0, 512)], lambda c: vecs[:, l, V_QN + c:V_QN + c + 1],
                    lambda c, c0, n, tb=tb: cqn[:, c, tb * 512:(tb + 1) * 512], lambda c, c0, tb=tb: S.tk("bx", 0, c, tb))
        wkv_, wkvt = load_slot(colslab(w_in, l, 36 * 128), v16, "two")
        wkr, wkrt = load_slot(colslab(w_in, l, 38 * 128), v16, "two")
        for tb in range(2):
            for j in range(2):
                b, bt = proj_fm(wkv_, wkvt, j, tb)
                copy_out(fs[:, j, :], b, [bt], [S.tk("fs", j)])
            norm_fm(fs, lambda c, c0: S.tk("fs", c), 2, 256, [(0, 512)], lambda c: vecs[:, l, V_KVN + c:V_KVN + c + 1],
                    lambda c, c0, n, tb=tb: latall[:, c, pos0 + tb * 512: pos0 + (tb + 1) * 512], lambda c, c0, tb=tb: S.tk("bx", "lat", "own", c, tb))
            tt = load_tab(c_cosM, c_sinM, pos0, tb)
            a, at = proj_fm(wkr, wkrt, 0, tb)
            b, bt = proj_fm(wkr, wkrt, 1, tb)
            rope_out(latall[:, 2, pos0 + tb * 512: pos0 + (tb + 1) * 512], a, at, b, bt, tt, [S.tk("bx", "lat", "own", 2, tb)])
        own_lat = [S.tk("bx", "lat", "own", c, tb) for c in range(3) for tb in range(2)]
        if p + 1 < npass:
            S.add("sp", I("dma_start", out=lat_cache[l][:, :, pos0:pos0 + T], in_=latall[:, :, pos0:pos0 + T]), reads=own_lat,
                  writes=[S.tk("latc", l)], dma_key="lats")
        lat_reads = own_lat + ([S.tk("bx", "lat", "prior")] if p > 0 else [])
        S.reset_unit("nT")
        wn, wnt = load_slot(rowslab(w_uq, l, 0, 4, 0, 1024), v4, "two")
        wr, wrt = load_slot(rowslab(w_uq, l, 0, 4, 1024, 1024), v4, "two")
        for tb in range(2):
            cq_reads = [S.tk("bx", 0, c, tb) for c in range(4)]
            for hq in range(8):
                b, bt = proj_fm(wn, wnt, hq, tb, src=cqn, src_reads=cq_reads, nk=4)
                copy_out(nT[:, hq, tb * 512:(tb + 1) * 512], b, [bt], [S.tk("nT", hq, tb)])
            tt = load_tab(c_cosM, c_sinM, pos0, tb)
            for i in range(4):
                a, at = proj_fm(wr, wrt, i, tb, src=cqn, src_reads=cq_reads, nk=4)
                b, bt = proj_fm(wr, wrt, 4 + i, tb, src=cqn, src_reads=cq_reads, nk=4)
                rope_out(nT[:, 8 + i, tb * 512:(tb + 1) * 512], a, at, b, bt, tt, [S.tk("nT", 8 + i, tb)])
        sc = 192.0 ** -0.5
        for hg in range(2):
            for h4 in range(4):
                h = hg * 4 + h4
                half = h % 2
                pair = h // 2
                kt_ = S.tk("bx", 4, "k")
                vt_ = S.tk("bx", 5, "v")
                for kb in range(nkeys // 512):
                    b, bt = next_bank(0, 4)
                    mm_group(b, bt, [(wkv2[:, c, h * 128:(h + 1) * 128], latall[:, c, kb * 512:(kb + 1) * 512]) for c in range(2)], reads=[wkt] + lat_reads)
                    copy_out(knT[:, kb * 512:(kb + 1) * 512], b, [bt], [kt_])
                    b, bt = next_bank(0, 4)
                    for t4 in range(4):
                        t_ = kb * 4 + t4
                        mm_group(b[:, t4 * 128:(t4 + 1) * 128], bt,
                                 [(latall[:, c, t_ * 128:(t_ + 1) * 128], wkv2[:, c, 1024 + h * 128: 1024 + (h + 1) * 128]) for c in range(2)], reads=[wkt] + lat_reads)
                    copy_out(X[5][:, kb * 512:(kb + 1) * 512], b, [bt], [vt_])
                for qb in range(2):
                    state["acc"] = state.get("acc", 0) + 1
                    ob, obt = banks[4 + state["acc"] % 2], bank_tok[4 + state["acc"] % 2]
                    sb_, sbt = banks[6 + state["acc"] % 2], bank_tok[6 + state["acc"] % 2]
                    qsl0 = qb * 512
                    gmax = 8 * p + 4 * qb + 4
                    for g in range(gmax):
                        jd = g - (8 * p + 4 * qb)
                        c0 = 128 * jd if jd > 0 else 0
                        cols = slice(c0, 512)
                        qcols = slice(qsl0 + c0, qsl0 + 512)
                        ksl = slice(g * 128, (g + 1) * 128)
                        st_, stt_ = next_bank(0, 4)
                        mm_group(st_[:, cols], stt_,
                                 [(knT[:, ksl], nT[:, h, qcols]),
                                  (latall[64 * half:64 * half + 64, 2, ksl], nT[64 * half:64 * half + 64, 8 + pair, qcols])],
                                 reads=[kt_, S.tk("nT", h, qb), S.tk("nT", 8 + pair, qb)] + lat_reads)
                        state["ptr"] += 1
                        pr = state["ptr"] % 4
                        ptt = S.tk("pt", pr)
                        S.add("act", I("activation", out=pt[:, pr, cols], in_=st_[:, cols], func=AF.Exp, scale=sc),
                              reads=[stt_], writes=[ptt])
                        if jd >= 0:
                            S.add("dve", I("tensor_tensor", out=pt[:, pr, c0:c0 + 128], in0=pt[:, pr, c0:c0 + 128], in1=trib, op=ALU.mult),
                                  reads=[ptt] + CT, writes=[ptt])
                        mm_group(ob[:, cols], obt, [(vh[:, g, :], pt[:, pr, cols])], reads=[vt_, ptt], start=(g == 0), stop=(g == gmax - 1))
                        mm_group(sb_[:, cols], sbt, [(ones[:, :], pt[:, pr, cols])], reads=[ptt] + CT, start=(g == 0), stop=(g == gmax - 1))
                    state["fsr"] += 1
                    r = state["fsr"] % 4
                    ft = S.tk("fs", r)
                    S.add("dve", I("reciprocal", out=fs[:, r, :], in_=sb_), reads=[sbt], writes=[ft])
                    S.add("dve", I("tensor_tensor", out=ycT[:, h4, qb * 512:(qb + 1) * 512], in0=ob, in1=fs[:, r, :], op=ALU.mult),
                          reads=[obt, ft], writes=[S.tk("bx", 0, h4, qb)])
            out_proj(l, w_out, 1024 + hg * 512, lambda c, tb: ycT[:, c, tb * 512:(tb + 1) * 512], lambda tb: [S.tk("bx", 0, c, tb) for c in range(4)])

    def xattn(l, p):
        norm_h_to_n(l, V_XA)
        S.reset_unit("bx")
        S.reset_unit("pt")
        S.reset_unit("wkvb")
        memn = v16(wkvb)
        xq = bx[:, 0:4, :].rearrange("p u (c t) -> p (u c) t", c=4)
        kTm = X[4].rearrange("p (c m) -> p c m", m=256)
        vm = X[5].rearrange("p (t n) -> p t n", t=2)
        memr = memT.rearrange("(c p) m -> p c m", p=128)
        ktoks = [S.tk("bx", 4, j) for j in range(16)]
        vtoks = [S.tk("bx", 5, mt, s_) for mt in range(2) for s_ in range(8)]
        xmode = "two" if p == 0 else "three"
        if p > 0:
            S.add("sp", I("dma_start", out=X[4], in_=memkv_cache[l][:, 0, :]), reads=[S.tk("mkvc", l)], writes=ktoks, dma_key="mkvl0")
            S.add("sp", I("dma_start", out=X[5], in_=memkv_cache[l][:, 1, :]), reads=[S.tk("mkvc", l)], writes=vtoks, dma_key="mkvl1")
        else:
            mem_kv_compute(l, memn, kTm, vm, memr)
            if npass > 1:
                S.add("sp", I("dma_start", out=memkv_cache[l][:, 0, :], in_=X[4]), reads=ktoks, writes=[S.tk("mkvc", l)], dma_key="mkvs0")
                S.add("sp", I("dma_start", out=memkv_cache[l][:, 1, :], in_=X[5]), reads=ktoks + vtoks, writes=[S.tk("mkvc", l)], dma_key="mkvs1")
        xattn_rest(l, p, xq, kTm, vm, xmode)

    def mem_kv_compute(l, memn, kTm, vm, memr):
        fsv = fs.rearrange("p a b -> p (a b)").rearrange("p (c m) -> p c m", m=256)
        fst = [S.tk("fs", i) for i in range(4)]
        ssb, sst = next_bank()
        for half in range(2):
            S.add("sp", I("dma_start", out=fsv, in_=memr[:, half * 8:(half + 1) * 8, :]), writes=fst, dma_key="meml")
            for cp in range(0, 8, 2):
                state["sq"] += 1
                r = state["sq"] % 2
                sqt = S.tk("sq", r)
                S.add("act", I("activation", out=sq[:, r, :, 0:256], in_=fsv[:, cp:cp + 2, :], func=AF.Square), reads=fst, writes=[sqt])
                mm_group(ssb[:, 0:256], sst, [(ones[:, :], sq[:, r, j, 0:256]) for j in range(2)], reads=[sqt] + CT,
                         start=(half == 0 and cp == 0), stop=(half == 1 and cp == 6))
        rt = S.tk("rstd", 0)
        rsqrt_to(rstd[:, 0:256], ssb[:, 0:256], [sst] + CT, [rt], 1.0 / D_MODEL, 256)
        mnt = S.tk("wkvb", "memn")
        for half in range(2):
            S.add("sp", I("dma_start", out=fsv, in_=memr[:, half * 8:(half + 1) * 8, :]), writes=fst, dma_key="meml")
            for c in range(8):
                cc = half * 8 + c
                S.add("dve", I("scalar_tensor_tensor", out=memn[:, cc, :], in0=fsv[:, c, :], scalar=vecs[:, l, V_MEM + cc:V_MEM + cc + 1],
                                                                          in1=rstd[:, 0:256], op0=ALU.mult, op1=ALU.mult),
                      reads=fst + [rt] + CT, writes=[mnt])
        for s in range(8):
            wv, wt = load_slot(colslab(xa_wkv, l, s * 256), v16, "two")
            for j in range(2):
                b, bt = next_bank()
                mm_group(b[:, 0:256], bt, [(wv[:, kc, j * 128:(j + 1) * 128], memn[:, kc, :]) for kc in range(16)], reads=[wt, mnt])
                copy_out(kTm[:, s * 2 + j, :], b[:, 0:256], [bt], [S.tk("bx", 4, s * 2 + j)])
        for s in range(8):
            wv, wt = load_slot(colslab(xa_wkv, l, 2048 + s * 256), v16, "two")
            for mt in range(2):
                b, bt = next_bank()
                mm_group(b[:, 0:256], bt, [(memn[:, kc, mt * 128:(mt + 1) * 128], wv[:, kc, :]) for kc in range(16)], reads=[wt, mnt])
                copy_out(vm[:, mt, s * 256:(s + 1) * 256], b[:, 0:256], [bt], [S.tk("bx", 5, mt, s)])

    def xattn_rest(l, p, xq, kTm, vm, xmode):
        for s in range(8):
            wv, wt = load_slot(colslab(xa_wq, l, s * 256), v16, xmode)
            for j in range(2):
                for tb in range(2):
                    b, bt = proj_fm(wv, wt, j, tb)
                    cj = s * 2 + j
                    copy_out(xq[:, cj, tb * 512:(tb + 1) * 512], b, [bt], [S.tk("bx", cj // 4, "q", cj, tb)])
        S.reset_unit("nT")
        sc = 512.0 ** -0.5
        for h in range(4):
            for tb in range(2):
                tsl = slice(tb * 512, (tb + 1) * 512)
                ptoks = []
                for mc in range(2):
                    b, bt = next_bank()
                    mm_group(b, bt, [(kTm[:, 4 * h + c, mc * 128:(mc + 1) * 128], xq[:, 4 * h + c, tsl]) for c in range(4)],
                             reads=[S.tk("bx", 4, 4 * h + c) for c in range(4)] + [S.tk("bx", h, "q", 4 * h + c, tb) for c in range(4)])
                    state["ptr"] += 1
                    pr = state["ptr"] % 4
                    ptt = S.tk("pt", pr)
                    S.add("act", I("activation", out=pt[:, pr, :], in_=b, func=AF.Exp, scale=sc), reads=[bt], writes=[ptt])
                    ptoks.append((pr, ptt))
                sb_, sbt = next_bank()
                mm_group(sb_, sbt, [(ones[:, :], pt[:, pr, :]) for pr, _ in ptoks], reads=[t for _, t in ptoks] + CT)
                state["fsr"] += 1
                r = state["fsr"] % 4
                ft = S.tk("fs", r)
                S.add("dve", I("reciprocal", out=fs[:, r, :], in_=sb_), reads=[sbt], writes=[ft])
                for c in range(4):
                    cj = 4 * h + c
                    b, bt = next_bank()
                    mm_group(b, bt, [(vm[:, mc, cj * 128:(cj + 1) * 128], pt[:, ptoks[mc][0], :]) for mc in range(2)],
                             reads=[S.tk("bx", 5, mc, cj // 2) for mc in range(2)] + [t for _, t in ptoks])
                    S.add("dve", I("tensor_tensor", out=nT[:, cj, tsl], in0=b, in1=fs[:, r, :], op=ALU.mult),
                          reads=[bt, ft], writes=[S.tk("nT", cj, tb)])
        for s in range(8):
            wv, wt = load_slot(colslab(xa_wo, l, s * 256), v16, xmode)
            for j in range(2):
                for tb in range(2):
                    b, bt = proj_fm(wv, wt, j, tb)
                    h_accum(b, bt, s * 2 + j, tb, 1.0)

    xTr = xT.rearrange("(c p) t -> p c t", p=128)
    yTr = yT.rearrange("(c p) t -> p c t", p=128)
    outs = []
    for p in range(npass):
        for c4 in range(0, 16, 4):
            q = "sp" if (c4 // 4) % 2 == 0 else "act"
            for tb in range(2):
                S.add("sp", I("dma_start", out=hT[:, c4:c4 + 4, tb * 512:(tb + 1) * 512], in_=xTr[:, c4:c4 + 4, p * T + tb * 512: p * T + (tb + 1) * 512]),
                      writes=[S.tk("hT", c, tb) for c in range(c4, c4 + 4)], dma_key=f"xin{c4}_{tb}")
        for l in range(L):
            if "f" in phases:
                ffn(l, w_g1, w_u1, w_d1, V_FFN1)
            if any(c in phases for c in "abc"):
                norm_h_to_n(l, V_MIX)
            if "a" in phases:
                mixer_a(l, p)
            if "b" in phases:
                mixer_b(l, p)
            if "c" in phases:
                mixer_c(l, p)
            if "x" in phases:
                xattn(l, p)
            if "g" in phases:
                ffn(l, w_g2, w_u2, w_d2, V_FFN2, restart=True)
        norm_fm(hT, h_tok, 16, D_MODEL, TBS, lambda c: vecs[:, 0, V_FIN + c:V_FIN + c + 1],
                lambda c, c0, n: hT[:, c, c0:c0 + n], h_tok)
        for c4 in range(0, 16, 4):
            for tb in range(2):
                o = S.add("sp", I("dma_start", out=yTr[:, c4:c4 + 4, p * T + tb * 512: p * T + (tb + 1) * 512], in_=hT[:, c4:c4 + 4, tb * 512:(tb + 1) * 512]),
                          reads=[S.tk("hT", c, tb) for c in range(c4, c4 + 4)], dma_key=f"yout{c4}_{tb}")
                outs.append(o)
    counts = S.emit(nc, final_waits=outs[-8:])
    return nc, counts


def _col(v):
    Ld, n = v.shape
    return np.ascontiguousarray(v.reshape(Ld, n // 128, 128).transpose(2, 0, 1))


def _swap_halves(w, width):
    K, N = w.shape
    half = width // 2
    w4 = w.reshape(K, N // width, 2, half)
    return w4[:, :, ::-1, :].reshape(K, N)


def prep_inputs(inp, depth=DEPTH, seq=SEQ):
    L = depth
    f32 = np.float32
    g = {k: np.asarray(v, dtype=f32) for k, v in inp.items()}
    d = {}
    d["w_g1"] = g["ffn1_w_gate"][:L]; d["w_u1"] = g["ffn1_w_up"][:L]; d["w_d1"] = g["ffn1_w_down"][:L]
    d["w_g2"] = g["ffn2_w_gate"][:L]; d["w_u2"] = g["ffn2_w_up"][:L]; d["w_d2"] = g["ffn2_w_down"][:L]
    wi = g["w_in"][:L]
    ext = np.empty((L, D_MODEL, 5120), f32)
    for l in range(L):
        w = wi[l]
        rq = w[:, 1024:1536]; rk = w[:, 1536:2048]; kr = w[:, 3840:3904]
        ext[l] = np.concatenate([
            w[:, 0:512], w[:, 512:1024],
            rq, _swap_halves(rq, 128), rk, _swap_halves(rk, 128),
            w[:, 2048:2560], w[:, 2560:3072], w[:, 3072:3584], w[:, 3584:3840],
            kr, kr, _swap_halves(kr, 64), _swap_halves(kr, 64)], axis=1)
    d["w_in"] = ext
    uq = g["w_uq"][:L].reshape(L, 512, 8, 192)
    nope = uq[:, :, :, :128].reshape(L, 512, 1024)
    ropeq = np.ascontiguousarray(uq[:, :, :, 128:]).reshape(L, 512, 512)
    ropes = np.stack([_swap_halves(ropeq[l], 64) for l in range(L)])
    d["w_uq"] = np.ascontiguousarray(np.concatenate([nope, ropeq, ropes], axis=2))
    ukv = g["w_ukv"][:L].reshape(L, 256, 8, 256)
    d["w_ukv"] = np.ascontiguousarray(np.concatenate([ukv[:, :, :, :128].reshape(L, 256, 1024), ukv[:, :, :, 128:].reshape(L, 256, 1024)], axis=2))
    d["w_out"] = g["w_out"][:L]
    d["xa_wq"] = g["xa_wq"][:L]; d["xa_wkv"] = g["xa_wkv"][:L]; d["xa_wo"] = g["xa_wo"][:L]
    vec = np.zeros((128, L, NV), f32)
    vec[:, :, V_FFN1:V_FFN1 + 16] = _col(g["ffn1_norm"][:L])
    vec[:, :, V_MIX:V_MIX + 16] = _col(g["mix_norm"][:L])
    vec[:, :, V_XA:V_XA + 16] = _col(g["xa_norm"][:L])
    vec[:, :, V_MEM:V_MEM + 16] = _col(g["mem_norm"][:L])
    vec[:, :, V_FFN2:V_FFN2 + 16] = _col(g["ffn2_norm"][:L])
    vec[:, :, V_QN:V_QN + 4] = _col(g["q_norm"][:L])
    vec[:, :, V_KVN:V_KVN + 2] = _col(g["kv_norm"][:L])
    vec[:, :, V_GN:V_GN + 4] = _col(g["ret_gn"][:L])
    vec[:, :, V_FIN:V_FIN + 16] = _col(np.broadcast_to(g["final_norm"][None], (L, D_MODEL)))
    d["vecs"] = vec
    d["sgu_rep"] = np.ascontiguousarray(np.broadcast_to(g["sgu_norm"][:L, None, :], (L, 128, 512)))
    d["sgu_wsT"] = np.ascontiguousarray(g["sgu_w_s"][:L].transpose(0, 3, 1, 2))
    d["sgu_b"] = np.ascontiguousarray(g["sgu_b"][:L].reshape(L, 1, 512))
    i = np.arange(128)
    d["c_ident"] = np.eye(128, dtype=f32)
    d["c_ones"] = np.ones((128, 128), f32)
    d["c_tri"] = (i[:, None] <= i[None, :]).astype(f32)
    gam = 1.0 - 2.0 ** (-5.0 - np.arange(4, dtype=np.float64))
    diff = (i[None, :] - i[:, None]).astype(np.float64)
    dm = np.where(diff[None] >= 0, gam[:, None, None] ** np.maximum(diff, 0)[None], 0.0) * (128 ** -0.5)
    d["c_dmask"] = np.ascontiguousarray(dm.transpose(1, 0, 2)).astype(f32)
    xi_ = gam[:, None] ** (i[None, :] + 1.0)
    d["c_xi"] = np.ascontiguousarray(np.broadcast_to(xi_[None], (128, 4, 128))).astype(f32)
    d["c_zeta"] = np.ascontiguousarray((gam[None, :] ** (127.0 - i[:, None])) * (128 ** -0.5)).astype(f32)
    pos = np.arange(seq, dtype=np.float32)

    def tables(dim, reps):
        half = dim // 2
        inv = (np.float32(10000.0) ** (-np.arange(half, dtype=np.float32) * np.float32(2.0) / np.float32(dim))).astype(np.float32)
        ang = (pos[None, :] * inv[:, None]).astype(np.float32)
        c = np.cos(ang).astype(f32); s = np.sin(ang).astype(f32)
        cosT = np.concatenate([c, c] * reps, axis=0)
        sinT = np.concatenate([-s, s] * reps, axis=0)
        return np.ascontiguousarray(cosT), np.ascontiguousarray(sinT)
    d["c_cosR"], d["c_sinR"] = tables(128, 1)
    d["c_cosM"], d["c_sinM"] = tables(64, 2)
    return d


_CACHE = {}


def kernel(**inputs):
    x = np.asarray(inputs["x"], dtype=np.float32)
    mem = np.asarray(inputs["mem"], dtype=np.float32)
    shared = prep_inputs(inputs)
    if "nc" not in _CACHE:
        _CACHE["nc"] = build()[0]
    nc = _CACHE["nc"]
    zeros = {k: np.zeros_like(v) for k, v in shared.items()}
    zeros["xT"] = np.zeros((D_MODEL, SEQ), np.float32)
    zeros["memT"] = np.zeros((D_MODEL, 256), np.float32)
    in_maps = []
    for core in range(8):
        if core % 2:
            in_maps.append(zeros)
            continue
        b = core // 2
        m = dict(shared)
        m["xT"] = np.ascontiguousarray(x[b].T)
        m["memT"] = np.ascontiguousarray(mem[b].T)
        in_maps.append(m)
    res = run_bass_kernel_spmd(nc, in_maps, core_ids=list(range(8)))
    out = np.empty((BATCH, SEQ, D_MODEL), np.float32)
    for b in range(BATCH):
        out[b] = res.results[2 * b]["yT"].T
    return out
```

```python
import math
import numpy as np
import ml_dtypes
import concourse.bass as bass
import concourse.mybir as mybir
from concourse.bass_utils import run_bass_kernel_spmd

F32 = mybir.dt.float32
BF16 = mybir.dt.bfloat16
AF = mybir.ActivationFunctionType
ALU = mybir.AluOpType

D_MODEL = 2048
SEQ = 4096
BATCH = 4
DEPTH = 4
D_FF = 5632
T = 1024
EPS = 1e-6
NV = 106

ENGS = ["pe", "act", "dve", "pool", "sp"]
EPOCH = 20000


def I(method, **kw):
    return lambda e: getattr(e, method)(**kw)


class Buf:
    __slots__ = ("name", "last_w", "readers")

    def __init__(self, name, hazard=None):
        self.name = name
        self.last_w = None
        self.readers = dict(hazard) if hazard else {}


class Op:
    __slots__ = ("eng", "fn", "waits", "ord", "needs_inc", "sem_i", "val", "is_dma", "dkey")


class Sched:
    def __init__(self):
        self.ops = {e: [] for e in ENGS}
        self.seen = {e: {} for e in ENGS}
        self.dma_n = {}
        self.units = {}
        self.hazard = {}

    def tk(self, unit, *key):
        d = self.units.setdefault(unit, {})
        b = d.get(key)
        if b is None:
            b = Buf(f"{unit}{key}", self.hazard.get(unit))
            d[key] = b
        return b

    def reset_unit(self, unit):
        hz = dict(self.hazard.get(unit, {}))
        for b in self.units.get(unit, {}).values():
            cands = list(b.readers.values())
            if b.last_w is not None:
                cands.append(b.last_w)
            for op in cands:
                k = ("dma", op.dkey) if op.is_dma else op.eng
                if k not in hz or hz[k].ord < op.ord:
                    hz[k] = op
        self.hazard[unit] = hz
        self.units[unit] = {}

    def add(self, eng, fn, reads=(), writes=(), dma_key=None):
        op = Op()
        op.eng = eng
        op.fn = fn
        op.needs_inc = False
        op.is_dma = dma_key is not None
        op.dkey = dma_key
        op.sem_i = 0
        op.val = 0
        if op.is_dma:
            n = self.dma_n.get(dma_key, 0) + 1
            self.dma_n[dma_key] = n
            op.ord = n
        else:
            op.ord = len(self.ops[eng])
        deps = []
        for b in reads:
            if b.last_w is not None:
                d = b.last_w
                if d.is_dma or op.is_dma or not (d.eng == eng and eng == "pe"):
                    deps.append(d)
        for b in writes:
            cands = list(b.readers.values())
            if b.last_w is not None:
                cands.append(b.last_w)
            for d in cands:
                if d.is_dma or op.is_dma or d.eng != eng:
                    deps.append(d)
        waits = {}
        seen = self.seen[eng]
        for d in deps:
            if d is op:
                continue
            key = ("dma", d.dkey) if d.is_dma else ("eng", d.eng)
            if seen.get(key, -1) >= d.ord:
                continue
            if key not in waits or waits[key].ord < d.ord:
                waits[key] = d
        for key, d in waits.items():
            seen[key] = d.ord
            d.needs_inc = True
        op.waits = list(waits.values())
        rk = ("dma", dma_key) if op.is_dma else eng
        for b in reads:
            b.readers[rk] = op
        for b in writes:
            b.last_w = op
            b.readers = {}
        self.ops[eng].append(op)
        return op

    def emit(self, nc, final_waits=()):
        nsem = {}
        for e in ENGS:
            c = 0
            for op in self.ops[e]:
                if op.is_dma or not op.needs_inc:
                    continue
                c += 1
                op.sem_i = (c - 1) // EPOCH
                op.val = (c - 1) % EPOCH + 1
            nsem[e] = (c + EPOCH - 1) // EPOCH if c else 0
        sems = {}
        for e in ENGS:
            for i in range(max(nsem[e], 1)):
                sems[("eng", e, i)] = nc.alloc_semaphore(name=f"s_{e}_{i}")
        for k in self.dma_n:
            sems[("dma", k)] = nc.alloc_semaphore(name=f"d_{k}")

        def sem_of(d):
            if d.is_dma:
                return sems[("dma", d.dkey)], 16 * d.ord
            return sems[("eng", d.eng, d.sem_i)], d.val

        def run(e, engine):
            for op in self.ops[e]:
                for d in op.waits:
                    s, v = sem_of(d)
                    engine.wait_ge(s, v)
                ins = op.fn(engine)
                if op.is_dma:
                    ins.then_inc(sems[("dma", op.dkey)], 16)
                elif op.needs_inc:
                    ins.then_inc(sems[("eng", e, op.sem_i)], 1)
            if e == "sp":
                for d in final_waits:
                    s, v = sem_of(d)
                    engine.wait_ge(s, v)

        with nc.Block() as block:
            @block.tensor
            def _(eng):
                run("pe", eng)

            @block.scalar
            def _(eng):
                run("act", eng)

            @block.vector
            def _(eng):
                run("dve", eng)

            @block.gpsimd
            def _(eng):
                run("pool", eng)

            @block.sync
            def _(eng):
                run("sp", eng)
        return {e: len(self.ops[e]) for e in ENGS}


V_FFN1, V_MIX, V_XA, V_MEM, V_FFN2, V_QN, V_KVN, V_GN, V_FIN = 0, 16, 32, 48, 64, 80, 84, 86, 90


def build(depth=DEPTH, npass=SEQ // T, use_gelu_tanh_lut=True, phases="fabcxg"):
    nc = bass.Bass("TRN2", target_bir_lowering=False)
    S = Sched()
    L = depth
    NTOK = npass * T

    def din(name, shape, dt=F32):
        return nc.dram_tensor(name, list(shape), dt, kind="ExternalInput").ap()

    xT = din("xT", [D_MODEL, NTOK])
    memT = din("memT", [D_MODEL, 256])
    yT = nc.dram_tensor("yT", [D_MODEL, NTOK], F32, kind="ExternalOutput").ap()
    w_g1 = din("w_g1", [L, D_MODEL, D_FF]); w_u1 = din("w_u1", [L, D_MODEL, D_FF]); w_d1 = din("w_d1", [L, D_FF, D_MODEL])
    w_g2 = din("w_g2", [L, D_MODEL, D_FF]); w_u2 = din("w_u2", [L, D_MODEL, D_FF]); w_d2 = din("w_d2", [L, D_FF, D_MODEL])
    w_in = din("w_in", [L, D_MODEL, 5120])
    w_uq = din("w_uq", [L, 512, 2048])
    w_ukv = din("w_ukv", [L, 256, 2048])
    w_out = din("w_out", [L, D_MODEL, D_MODEL])
    xa_wq = din("xa_wq", [L, D_MODEL, D_MODEL])
    xa_wkv = din("xa_wkv", [L, D_MODEL, 4096])
    xa_wo = din("xa_wo", [L, D_MODEL, D_MODEL])
    vecs_d = din("vecs", [128, L, NV])
    sgu_rep_d = din("sgu_rep", [L, 128, 512])
    sgu_wsT_d = din("sgu_wsT", [L, 128, 4, 128])
    sgu_b_d = din("sgu_b", [L, 1, 512])
    c_ident = din("c_ident", [128, 128]); c_ones = din("c_ones", [128, 128]); c_tri = din("c_tri", [128, 128])
    c_dmask = din("c_dmask", [128, 4, 128]); c_xi = din("c_xi", [128, 4, 128]); c_zeta = din("c_zeta", [128, 4])
    c_cosR = din("c_cosR", [128, SEQ]); c_sinR = din("c_sinR", [128, SEQ])
    c_cosM = din("c_cosM", [128, SEQ]); c_sinM = din("c_sinM", [128, SEQ])
    lat_cache = nc.dram_tensor("lat_cache", [L, 128, 3, NTOK], BF16, kind="Internal").ap()
    st_cache = nc.dram_tensor("st_cache", [L, 128, 4, 128], F32, kind="Internal").ap()
    memkv_cache = nc.dram_tensor("memkv_cache", [L, 128, 2, 4096], BF16, kind="Internal").ap()

    def sb(name, shape, dt):
        return nc.alloc_sbuf_tensor("sb_" + name, list(shape), dt).ap()

    hT = sb("hT", [128, 16, T], F32)
    nT = sb("nT", [128, 16, T], BF16)
    bx = sb("bx", [128, 6, 4096], BF16)
    ring = sb("ring", [128, 2, 4096], BF16)
    wkvb = sb("wkvb", [128, 4096], BF16)
    pt = sb("pt", [128, 4, 512], BF16)
    tab = sb("tab", [128, 2, 512], F32)
    sq = sb("sq", [128, 2, 2, 512], BF16)
    rstd = sb("rstd", [128, T], F32)
    fs = sb("fs", [128, 4, 512], F32)
    vecs = sb("vecs", [128, L, NV], F32)
    ident = sb("ident", [128, 128], BF16); ones = sb("ones", [128, 128], BF16); tri = sb("tri", [128, 128], F32)
    trib = sb("trib", [128, 128], BF16)
    dmask = sb("dmask", [128, 4, 128], F32); xi = sb("xi", [128, 4, 128], F32); zeta = sb("zeta", [128, 4], F32)
    sgurep = sb("sgurep", [128, 512], F32)
    wsm = sb("wsm", [128, 4, 128], BF16)
    brow = sb("brow", [1, 512], BF16)
    st32 = sb("st32", [128, 4, 128], F32); st16 = sb("st16", [128, 4, 128], BF16)
    small = sb("small", [128, 16], F32)
    banks = [nc.alloc_psum_tensor(f"bank{i}", [128, 512], F32).ap() for i in range(8)]

    state = {"bank": 0, "slot": 0, "ev": 0, "sq": 0, "fsr": 0, "ptr": 0}
    bank_tok = [S.tk("bank", i) for i in range(8)]

    def next_bank(lo=0, hi=8):
        state["bank"] += 1
        i = lo + state["bank"] % (hi - lo)
        return banks[i], bank_tok[i]

    def units_all():
        return [("bx", i) for i in range(6)] + [("ring", 0), ("ring", 1)]

    def slot_ap(u):
        if u[0] == "wkvb":
            return wkvb[:, :]
        return (ring if u[0] == "ring" else bx)[:, u[1], :]

    def slot_tok(u):
        return S.tk(u[0], "slot", u[1])

    slotsets = {"all": units_all(), "two": [("ring", 0), ("ring", 1)], "three": [("ring", 0), ("ring", 1), ("wkvb", 0)]}

    def load_slot(src_ap, view_fn, mode):
        us = slotsets[mode]
        state["slot"] += 1
        u = us[state["slot"] % len(us)]
        ap = view_fn(slot_ap(u))
        tok = slot_tok(u)
        S.add("pool", I("dma_start", out=ap, in_=src_ap), writes=[tok], dma_key=f"w_{u[0]}{u[1]}")
        return ap, tok

    def v16(a):
        return a.rearrange("p (k n) -> p k n", k=16)

    def v4(a):
        return a.rearrange("p (k n) -> p k n", k=4)

    def v2(a):
        return a.rearrange("p (k n) -> p k n", k=2)

    def colslab(W, l, c0, ncols=256):
        return W[l].rearrange("(k p) n -> p k n", p=128)[:, :, c0:c0 + ncols]

    def rowslab(W, l, r0, nk, c0, ncols):
        return W[l][r0:r0 + nk * 128, c0:c0 + ncols].rearrange("(k p) n -> p k n", p=128)

    def evac_engine():
        state["ev"] += 1
        return "act" if state["ev"] % 2 else "dve"

    def copy_out(dst, src, reads, writes, eng=None, scale=None):
        eng = eng or evac_engine()
        if eng == "act":
            if scale is None:
                S.add("act", I("activation", out=dst, in_=src, func=AF.Copy), reads=reads, writes=writes)
            else:
                S.add("act", I("activation", out=dst, in_=src, func=AF.Copy, scale=scale), reads=reads, writes=writes)
        else:
            if scale is None:
                S.add("dve", I("tensor_copy", out=dst, in_=src), reads=reads, writes=writes)
            else:
                S.add("dve", I("tensor_scalar", out=dst, in0=src, scalar1=scale, scalar2=None, op0=ALU.mult), reads=reads, writes=writes)

    def mm_group(out_ap, out_tok, pairs, reads, start=True, stop=True):
        n = len(pairs)

        def fn(e):
            ins = None
            for i, (l_, r_) in enumerate(pairs):
                ins = e.matmul(out_ap, lhsT=l_, rhs=r_, start=(start and i == 0), stop=(stop and i == n - 1))
            return ins
        return S.add("pe", fn, reads=reads, writes=[out_tok])

    def rsqrt_to(dst, src, src_toks, dst_toks, scale, n):
        S.add("act", I("activation", out=dst, in_=src, func=AF.Sqrt, scale=scale, bias=eps_col[:, 0:1]),
              reads=src_toks, writes=dst_toks)
        S.add("dve", I("reciprocal", out=dst, in_=dst), reads=dst_toks, writes=dst_toks)

    def gelu_to(dst, src, reads, writes):
        n = src.shape[-1]
        if use_gelu_tanh_lut:
            S.add("act", I("activation", out=dst, in_=src, func=AF.Gelu_apprx_tanh), reads=reads, writes=writes)
            return
        state["fsr"] += 1
        r = state["fsr"] % 4
        tmp = fs[:, r, 0:n]
        tt = [S.tk("fs", r)]
        S.add("act", I("activation", out=tmp, in_=src, func=AF.Square), reads=reads, writes=tt)
        S.add("dve", I("tensor_scalar", out=tmp, in0=tmp, scalar1=0.044715, scalar2=1.0, op0=ALU.mult, op1=ALU.add), reads=tt, writes=tt)
        S.add("dve", I("tensor_tensor", out=tmp, in0=tmp, in1=src, op=ALU.mult), reads=tt + list(reads), writes=tt)
        S.add("act", I("activation", out=tmp, in_=tmp, func=AF.Sigmoid, scale=1.5957691216057308), reads=tt, writes=tt)
        S.add("dve", I("tensor_tensor", out=dst, in0=tmp, in1=src, op=ALU.mult), reads=tt + list(reads), writes=writes)

    ctok = S.tk("const", 0)
    eps_col = small[:, 0:1]

    def cload(dst, src, q="sp", key="c0"):
        S.add(q, I("dma_start", out=dst, in_=src), writes=[ctok], dma_key=key)

    cload(vecs, vecs_d, "sp", "c0")
    cload(tri, c_tri, "sp", "c0")
    cload(dmask, c_dmask, "sp", "c0")
    cload(xi, c_xi, "sp", "c0")
    cload(zeta, c_zeta, "sp", "c0")
    cload(ident, c_ident, "pool", "c1")
    cload(ones, c_ones, "pool", "c1")
    cload(trib, c_tri, "pool", "c1")
    S.add("dve", I("memset", ap=small[:, 0:1], constant=EPS), writes=[ctok])
    CT = [ctok]

    def norm_fm(src, src_tok_fn, nch, dtrue, ncols_list, gcol_fn, dst_fn, dst_tok_fn, in_place_f32=False):
        for (c0, n) in ncols_list:
            ssb, sst = next_bank()
            for cp in range(0, nch, 2):
                k = min(2, nch - cp)
                state["sq"] += 1
                r = state["sq"] % 2
                sqt = S.tk("sq", r)
                rd = [src_tok_fn(c, c0) for c in range(cp, cp + k)]
                S.add("act", I("activation", out=sq[:, r, 0:k, 0:n], in_=src[:, cp:cp + k, c0:c0 + n], func=AF.Square),
                      reads=rd, writes=[sqt])
                mm_group(ssb[:, 0:n], sst, [(ones[:, :], sq[:, r, j, 0:n]) for j in range(k)], reads=[sqt] + CT,
                         start=(cp == 0), stop=(cp + k >= nch))
            rt = S.tk("rstd", c0)
            rsqrt_to(rstd[:, c0:c0 + n], ssb[:, 0:n], [sst] + CT, [rt], 1.0 / dtrue, n)
            for c in range(nch):
                S.add("dve", I("scalar_tensor_tensor", out=dst_fn(c, c0, n), in0=src[:, c, c0:c0 + n], scalar=gcol_fn(c),
                                                                   in1=rstd[:, c0:c0 + n], op0=ALU.mult, op1=ALU.mult),
                      reads=[src_tok_fn(c, c0), rt] + CT, writes=[dst_tok_fn(c, c0)])

    TBS = [(0, 512), (512, 512)]

    def h_tok(c, c0):
        return S.tk("hT", c, c0 // 512)

    def n_tok(c, c0):
        return S.tk("nT", c, c0 // 512)

    def norm_h_to_n(l, vofs):
        S.reset_unit("nT")
        norm_fm(hT, h_tok, 16, D_MODEL, TBS, lambda c: vecs[:, l, vofs + c:vofs + c + 1],
                lambda c, c0, n: nT[:, c, c0:c0 + n], n_tok)

    def n_reads(tb):
        return [S.tk("nT", c, tb) for c in range(16)]

    def h_accum(bank, btok, oc, tb, scale, extra_reads=()):
        ht = S.tk("hT", oc, tb)
        dst = hT[:, oc, tb * 512:(tb + 1) * 512]
        S.add("dve", I("scalar_tensor_tensor", out=dst, in0=bank, scalar=scale, in1=dst, op0=ALU.mult, op1=ALU.add),
              reads=[btok, ht], writes=[ht])

    def ffn(l, Wg, Wu, Wd, vofs, restart=False):
        if restart:
            state["slot"] = -1
        norm_h_to_n(l, vofs)
        S.reset_unit("bx")
        S.reset_unit("pt")
        for s in range(D_FF // 256):
            gv, gt = load_slot(colslab(Wg, l, s * 256), v16, "all")
            uv, ut = load_slot(colslab(Wu, l, s * 256), v16, "all")
            dv, dt_ = load_slot(rowslab(Wd, l, s * 256, 2, 0, D_MODEL), v2, "all")
            for tb in range(2):
                tsl = slice(tb * 512, (tb + 1) * 512)
                state["ptr"] += 1
                pr = state["ptr"] % 2
                for j in range(2):
                    gb, gbt = next_bank()
                    mm_group(gb, gbt, [(gv[:, kc, j * 128:(j + 1) * 128], nT[:, kc, tsl]) for kc in range(16)], reads=[gt] + n_reads(tb))
                    ub, ubt = next_bank()
                    mm_group(ub, ubt, [(uv[:, kc, j * 128:(j + 1) * 128], nT[:, kc, tsl]) for kc in range(16)], reads=[ut] + n_reads(tb))
                    state["fsr"] += 1
                    r = state["fsr"] % 4
                    ft = S.tk("fs", r)
                    S.add("act", I("activation", out=fs[:, r, :], in_=gb, func=AF.Silu), reads=[gbt], writes=[ft])
                    at = S.tk("pt", pr, j)
                    S.add("dve", I("tensor_tensor", out=pt[:, pr * 2 + j, :], in0=fs[:, r, :], in1=ub, op=ALU.mult),
                          reads=[ft, ubt], writes=[at])
                for oc in range(16):
                    ob, obt = next_bank()
                    mm_group(ob, obt, [(dv[:, j, oc * 128:(oc + 1) * 128], pt[:, pr * 2 + j, :]) for j in range(2)],
                             reads=[dt_, S.tk("pt", pr, 0), S.tk("pt", pr, 1)])
                    h_accum(ob, obt, oc, tb, 0.5)

    def proj_fm(wv, wt, j, tb, src=None, src_reads=None, nk=16):
        b, bt = next_bank()
        tsl = slice(tb * 512, (tb + 1) * 512)
        src = nT if src is None else src
        rd = n_reads(tb) if src_reads is None else src_reads
        mm_group(b, bt, [(wv[:, kc, j * 128:(j + 1) * 128], src[:, kc, tsl]) for kc in range(nk)], reads=[wt] + rd)
        return b, bt

    def out_proj(l, W, r0, src_fn, src_reads_fn, mode="two"):
        for ch in range(2):
            wv, wt = load_slot(rowslab(W, l, r0, 4, ch * 1024, 1024), v4, mode)
            for tb in range(2):
                for o8 in range(8):
                    b, bt = next_bank()
                    mm_group(b, bt, [(wv[:, c, o8 * 128:(o8 + 1) * 128], src_fn(c, tb)) for c in range(4)],
                             reads=[wt] + src_reads_fn(tb))
                    h_accum(b, bt, ch * 8 + o8, tb, 1.0)

    X = [bx[:, i, :] for i in range(6)]
    MODE3 = "three"

    def load_tab(cosd, sind, pos0, tb):
        tt = S.tk("tab", 0)
        S.add("sp", I("dma_start", out=tab[:, 0, :], in_=cosd[:, pos0 + tb * 512: pos0 + (tb + 1) * 512]), writes=[tt], dma_key="tab")
        S.add("sp", I("dma_start", out=tab[:, 1, :], in_=sind[:, pos0 + tb * 512: pos0 + (tb + 1) * 512]), writes=[tt], dma_key="tab")
        return tt

    def rope_out(dst, a, at, b, bt, tt, wtoks):
        state["fsr"] += 1
        r0 = state["fsr"] % 4
        state["fsr"] += 1
        r1 = state["fsr"] % 4
        t0 = S.tk("fs", r0)
        t1 = S.tk("fs", r1)
        S.add("dve", I("tensor_tensor", out=fs[:, r0, :], in0=a, in1=tab[:, 0, :], op=ALU.mult), reads=[at, tt], writes=[t0])
        S.add("dve", I("tensor_tensor", out=fs[:, r1, :], in0=b, in1=tab[:, 1, :], op=ALU.mult), reads=[bt, tt], writes=[t1])
        S.add("dve", I("tensor_tensor", out=dst, in0=fs[:, r0, :], in1=fs[:, r1, :], op=ALU.add), reads=[t0, t1], writes=wtoks)

    def mixer_a(l, p):
        S.reset_unit("bx")
        S.reset_unit("pt")
        S.reset_unit("wkvb")
        uT = v4(X[0]); vt = X[1].rearrange("p (t n) -> p t n", t=8); yaT = v4(X[2])
        lt = S.tk("sgu_l", 0)
        S.add("sp", I("dma_start", out=sgurep, in_=sgu_rep_d[l]), writes=[lt], dma_key="sgul")
        S.add("sp", I("dma_start", out=fs[:, 3, :].rearrange("p (g t) -> p g t", g=4), in_=sgu_wsT_d[l]), writes=[S.tk("fs", 3)], dma_key="sgul2")
        S.add("pool", I("dma_start", out=brow, in_=sgu_b_d[l]), writes=[lt], dma_key="sgub")
        wt_ = S.tk("wsm", 0)
        for g in range(4):
            S.add("dve", I("tensor_tensor", out=wsm[:, g, :], in0=fs[:, 3, g * 128:(g + 1) * 128], in1=tri, op=ALU.mult),
                  reads=[S.tk("fs", 3)] + CT, writes=[wt_])
        for s in range(2):
            wv, wt = load_slot(colslab(w_in, l, s * 256), v16, MODE3)
            for j in range(2):
                for tb in range(2):
                    b, bt = proj_fm(wv, wt, j, tb)
                    gelu_to(uT[:, s * 2 + j, tb * 512:(tb + 1) * 512], b, [bt], [S.tk("bx", 0, s * 2 + j, tb)])
        wv2, wt2 = load_slot(colslab(w_in, l, 512), v16, MODE3)
        wv3, wt3 = load_slot(colslab(w_in, l, 768), v16, MODE3)
        for tt_ in range(8):
            b, bt = next_bank()
            mm_group(b[:, 0:256], bt, [(nT[:, kc, tt_ * 128:(tt_ + 1) * 128], wv2[:, kc, :]) for kc in range(16)], reads=[wt2] + n_reads(tt_ // 4))
            mm_group(b[:, 256:512], bt, [(nT[:, kc, tt_ * 128:(tt_ + 1) * 128], wv3[:, kc, :]) for kc in range(16)], reads=[wt3] + n_reads(tt_ // 4))
            r = tt_ % 2
            gt = S.tk("fs", r)
            gelu_to(fs[:, r, :], b, [bt], [gt])
            sst = S.tk("small", 1 + r)
            S.add("dve", I("memset", ap=small[:, 1 + r:2 + r], constant=0.0), writes=[sst])
            S.add("dve", I("scalar_tensor_tensor", out=fs[:, 2 + r, :], in0=fs[:, r, :], scalar=1.0, in1=fs[:, r, :], op0=ALU.mult, op1=ALU.mult,
                                                               accum_out=small[:, 1 + r:2 + r]), reads=[gt], writes=[sst, S.tk("fs", 2 + r)])
            rsqrt_to(small[:, 1 + r:2 + r], small[:, 1 + r:2 + r], [sst] + CT, [sst], 1.0 / 512, 1)
            S.add("dve", I("scalar_tensor_tensor", out=vt[:, tt_, :], in0=fs[:, r, :], scalar=small[:, 1 + r:2 + r], in1=sgurep,
                                                                        op0=ALU.mult, op1=ALU.mult),
                  reads=[gt, sst, lt], writes=[S.tk("bx", 1, tt_)])
        for g in range(4):
            for tb in range(2):
                b, bt = next_bank()
                for n4 in range(4):
                    n = tb * 4 + n4
                    mm_group(b[:, n4 * 128:(n4 + 1) * 128], bt,
                             [(vt[:, n, g * 128:(g + 1) * 128], wsm[:, g, :]), (ones[0:1, :], brow[0:1, g * 128:(g + 1) * 128])],
                             reads=[S.tk("bx", 1, n), wt_, lt] + CT)
                S.add("dve", I("tensor_tensor", out=yaT[:, g, tb * 512:(tb + 1) * 512], in0=b, in1=uT[:, g, tb * 512:(tb + 1) * 512], op=ALU.mult),
                      reads=[bt, S.tk("bx", 0, g, tb)], writes=[S.tk("bx", 2, g, tb)])
        out_proj(l, w_out, 0, lambda c, tb: yaT[:, c, tb * 512:(tb + 1) * 512], lambda tb: [S.tk("bx", 2, c, tb) for c in range(4)], MODE3)

    def mixer_b(l, p):
        S.reset_unit("bx")
        S.reset_unit("pt")
        qT = v4(X[0]); kT = v4(X[1])
        kz = X[2].rearrange("p (n h d) -> p n h d", n=8, h=4)
        vtm = X[3].rearrange("p (t n) -> p t n", t=8)
        sgT = v4(X[4]); yrT = v4(X[5])
        gam = [1.0 - 2.0 ** (-5.0 - h) for h in range(4)]
        S.reset_unit("st")
        stt = S.tk("st", "init")
        if p == 0:
            S.add("dve", I("memset", ap=st32[:], constant=0.0), writes=[stt])
            S.add("dve", I("memset", ap=st16[:], constant=0.0), writes=[stt])
        else:
            S.add("sp", I("dma_start", out=st32[:], in_=st_cache[l]), reads=[S.tk("stc", l)], writes=[stt], dma_key="stl")
            S.add("dve", I("tensor_copy", out=st16[:], in_=st32[:]), reads=[stt], writes=[stt])
        for which, dstT, un in ((0, qT, 0), (1, kT, 1)):
            c_a = 8 + which * 8
            for hp in range(2):
                wa, wat = load_slot(colslab(w_in, l, (c_a + hp * 2) * 128), v16, MODE3)
                wb, wbt = load_slot(colslab(w_in, l, (c_a + 4 + hp * 2) * 128), v16, MODE3)
                for tb in range(2):
                    tt = load_tab(c_cosR, c_sinR, p * T, tb)
                    for j in range(2):
                        a, at = proj_fm(wa, wat, j, tb)
                        b, bt = proj_fm(wb, wbt, j, tb)
                        h = hp * 2 + j
                        rope_out(dstT[:, h, tb * 512:(tb + 1) * 512], a, at, b, bt, tt, [S.tk("bx", un, h, tb)])
        for h in range(4):
            for n2 in range(0, 8, 4):
                b, bt = next_bank()
                bb = b.bitcast(BF16)
                for n4 in range(4):
                    n = n2 + n4
                    S.add("pe", I("transpose", out=bb[:, n4 * 128:(n4 + 1) * 128], in_=kT[:, h, n * 128:(n + 1) * 128], identity=ident),
                          reads=[S.tk("bx", 1, h, n // 4)] + CT, writes=[bt])
                for n4 in range(4):
                    n = n2 + n4
                    S.add("dve", I("tensor_scalar", out=kz[:, n, h, :], in0=bb[:, n4 * 128:(n4 + 1) * 128], scalar1=zeta[:, h:h + 1], scalar2=None, op0=ALU.mult),
                          reads=[bt] + CT, writes=[S.tk("bx", 2, n, h)])
        for s in range(2):
            wv, wt = load_slot(colslab(w_in, l, (24 + s * 2) * 128), v16, MODE3)
            for tt_ in range(8):
                b, bt = next_bank()
                mm_group(b[:, 0:256], bt, [(nT[:, kc, tt_ * 128:(tt_ + 1) * 128], wv[:, kc, :]) for kc in range(16)], reads=[wt] + n_reads(tt_ // 4))
                copy_out(vtm[:, tt_, s * 256:(s + 1) * 256], b[:, 0:256], [bt], [S.tk("bx", 3, tt_, s)])
        for s in range(2):
            wv, wt = load_slot(colslab(w_in, l, (28 + s * 2) * 128), v16, MODE3)
            for j in range(2):
                for tb in range(2):
                    b, bt = proj_fm(wv, wt, j, tb)
                    S.add("act", I("activation", out=sgT[:, s * 2 + j, tb * 512:(tb + 1) * 512], in_=b, func=AF.Silu),
                          reads=[bt], writes=[S.tk("bx", 4, s * 2 + j, tb)])
        for tb in range(2):
            ybanks = []
            for h in range(4):
                yb, ybt = next_bank(4, 8)
                ybanks.append((yb, ybt))
            for n4 in range(4):
                n = tb * 4 + n4
                nsl = slice(n * 128, (n + 1) * 128)
                for h in range(4):
                    yb, ybt = ybanks[h]
                    sb_, sbt = next_bank(0, 4)
                    mm_group(sb_[:, 0:128], sbt, [(kT[:, h, nsl], qT[:, h, nsl])], reads=[S.tk("bx", 1, h, tb), S.tk("bx", 0, h, tb)])
                    state["ptr"] += 1
                    pr = state["ptr"] % 8
                    sm = pt.rearrange("p a (b c) -> p (a b) c", b=4)
                    smt = S.tk("pt", "b16", pr)
                    S.add("dve", I("tensor_tensor", out=sm[:, pr, :], in0=sb_[:, 0:128], in1=dmask[:, h, :], op=ALU.mult),
                          reads=[sbt] + CT, writes=[smt])
                    qxt = S.tk("pt", "b16", 8 + pr)
                    S.add("pool", I("tensor_tensor", out=sm[:, 8 + pr, :], in0=qT[:, h, nsl], in1=xi[:, h, :], op=ALU.mult),
                          reads=[S.tk("bx", 0, h, tb)] + CT, writes=[qxt])
                    mm_group(yb[:, n4 * 128:(n4 + 1) * 128], ybt,
                             [(vtm[:, n, h * 128:(h + 1) * 128], sm[:, pr, :]), (st16[:, h, :], sm[:, 8 + pr, :])],
                             reads=[S.tk("bx", 3, n, h // 2), smt, qxt, S.tk("st", 16, h), stt])
                    kb, kbt = next_bank(0, 4)
                    mm_group(kb[:, 0:128], kbt, [(kz[:, n, h, :], vtm[:, n, h * 128:(h + 1) * 128])], reads=[S.tk("bx", 2, n, h), S.tk("bx", 3, n, h // 2)])
                    s32t = S.tk("st", 32, h)
                    S.add("dve", I("scalar_tensor_tensor", out=st32[:, h, :], in0=st32[:, h, :], scalar=gam[h] ** 128, in1=kb[:, 0:128], op0=ALU.mult, op1=ALU.add),
                          reads=[kbt, s32t, stt], writes=[s32t])
                    S.add("act", I("activation", out=st16[:, h, :], in_=st32[:, h, :], func=AF.Copy), reads=[s32t], writes=[S.tk("st", 16, h)])
            for h in range(4):
                yb, ybt = ybanks[h]
                state["sq"] += 1
                r = state["sq"] % 2
                sqt = S.tk("sq", r)
                S.add("act", I("activation", out=sq[:, r, 0, :], in_=yb, func=AF.Copy), reads=[ybt], writes=[sqt])
                S.add("act", I("activation", out=sq[:, r, 1, :], in_=yb, func=AF.Square), reads=[ybt], writes=[sqt])
                s1, s1t = next_bank(0, 4)
                mm_group(s1, s1t, [(ones[:, :], sq[:, r, 0, :])], reads=[sqt] + CT)
                s2, s2t = next_bank(0, 4)
                mm_group(s2, s2t, [(ones[:, :], sq[:, r, 1, :])], reads=[sqt] + CT)
                f0, f1, f2 = S.tk("fs", 0), S.tk("fs", 1), S.tk("fs", 2)
                S.add("dve", I("tensor_scalar", out=fs[:, 0, :], in0=s1, scalar1=1.0 / 128, scalar2=None, op0=ALU.mult), reads=[s1t], writes=[f0])
                S.add("dve", I("tensor_tensor", out=fs[:, 1, :], in0=fs[:, 0, :], in1=fs[:, 0, :], op=ALU.mult), reads=[f0], writes=[f1])
                S.add("dve", I("scalar_tensor_tensor", out=fs[:, 1, :], in0=s2, scalar=1.0 / 128, in1=fs[:, 1, :], op0=ALU.mult, op1=ALU.subtract),
                      reads=[s2t, f1], writes=[f1])
                rsqrt_to(fs[:, 1, :], fs[:, 1, :], [f1] + CT, [f1], 1.0, 512)
                S.add("dve", I("tensor_tensor", out=fs[:, 2, :], in0=yb, in1=fs[:, 0, :], op=ALU.subtract), reads=[ybt, f0], writes=[f2])
                S.add("dve", I("scalar_tensor_tensor", out=fs[:, 2, :], in0=fs[:, 2, :], scalar=vecs[:, l, V_GN + h:V_GN + h + 1], in1=fs[:, 1, :],
                                                                   op0=ALU.mult, op1=ALU.mult), reads=[f2, f1] + CT, writes=[f2])
                S.add("dve", I("tensor_tensor", out=yrT[:, h, tb * 512:(tb + 1) * 512], in0=fs[:, 2, :], in1=sgT[:, h, tb * 512:(tb + 1) * 512], op=ALU.mult),
                      reads=[f2, S.tk("bx", 4, h, tb)], writes=[S.tk("bx", 5, h, tb)])
        S.add("sp", I("dma_start", out=st_cache[l], in_=st32[:]), reads=[S.tk("st", 32, h) for h in range(4)] + [stt], writes=[S.tk("stc", l)], dma_key="sts")
        out_proj(l, w_out, 512, lambda c, tb: yrT[:, c, tb * 512:(tb + 1) * 512], lambda tb: [S.tk("bx", 5, c, tb) for c in range(4)], MODE3)

    def mixer_c(l, p):
        S.reset_unit("bx")
        S.reset_unit("pt")
        cqn = v4(X[0]); ycT = v4(X[0])
        latall = bx[:, 1:4, :]
        knT = X[4]
        vh = X[5].rearrange("p (t d) -> p t d", d=128)
        nkeys = (p + 1) * T
        pos0 = p * T
        S.reset_unit("wkvb")
        wkt = S.tk("wkvb", 0)
        S.add("pool", I("dma_start", out=v2(wkvb), in_=rowslab(w_ukv, l, 0, 2, 0, 2048)), writes=[wkt], dma_key="wukv")
        wkv2 = v2(wkvb)
        if p > 0:
            S.add("sp", I("dma_start", out=latall[:, :, 0:pos0], in_=lat_cache[l][:, :, 0:pos0]), reads=[S.tk("latc", l)],
                  writes=[S.tk("bx", "lat", "prior")], dma_key="latl")
        wq0, wq0t = load_slot(colslab(w_in, l, 32 * 128), v16, "two")
        wq1, wq1t = load_slot(colslab(w_in, l, 34 * 128), v16, "two")
        for tb in range(2):
            for j in range(4):
                wv, wt = (wq0, wq0t) if j < 2 else (wq1, wq1t)
                b, bt = proj_fm(wv, wt, j % 2, tb)
                copy_out(fs[:, j, :], b, [bt], [S.tk("fs", j)])
            norm_fm(fs, lambda c, c0: S.tk("fs", c), 4, 512, [(0, 512)], lambda c: vecs[:, l, V_QN + c:V_QN + c + 1],
                    lambda c, c0, n, tb=tb: cqn[:, c, tb * 512:(tb + 1) * 512], lambda c, c0, tb=tb: S.tk("bx", 0, c, tb))
        wkv_, wkvt = load_slot(colslab(w_in, l, 36 * 128), v16, "two")
        wkr, wkrt = load_slot(colslab(w_in, l, 38 * 128), v16, "two")
        for tb in range(2):
            for j in range(2):
                b, bt = proj_fm(wkv_, wkvt, j, tb)
                copy_out(fs[:, j, :], b, [bt], [S.tk("fs", j)])
            norm_fm(fs, lambda c, c0: S.tk("fs", c), 2, 256, [(0, 512)], lambda c: vecs[:, l, V_KVN + c:V_KVN + c + 1],
                    lambda c, c0, n, tb=tb: latall[:, c, pos0 + tb * 512: pos0 + (tb + 1) * 512], lambda c, c0, tb=tb: S.tk("bx", "lat", "own", c, tb))
            tt = load_tab(c_cosM, c_sinM, pos0, tb)
            a, at = proj_fm(wkr, wkrt, 0, tb)
            b, bt = proj_fm(wkr, wkrt, 1, tb)
            rope_out(latall[:, 2, pos0 + tb * 512: pos0 + (tb + 1) * 512], a, at, b, bt, tt, [S.tk("bx", "lat", "own", 2, tb)])
        own_lat = [S.tk("bx", "lat", "own", c, tb) for c in range(3) for tb in range(2)]
        if p + 1 < npass:
            S.add("sp", I("dma_start", out=lat_cache[l][:, :, pos0:pos0 + T], in_=latall[:, :, pos0:pos0 + T]), reads=own_lat,
                  writes=[S.tk("latc", l)], dma_key="lats")
        lat_reads = own_lat + ([S.tk("bx", "lat", "prior")] if p > 0 else [])
        S.reset_unit("nT")
        wn, wnt = load_slot(rowslab(w_uq, l, 0, 4, 0, 1024), v4, "two")
        wr, wrt = load_slot(rowslab(w_uq, l, 0, 4, 1024, 1024), v4, "two")
        for tb in range(2):
            cq_reads = [S.tk("bx", 0, c, tb) for c in range(4)]
            for hq in range(8):
                b, bt = proj_fm(wn, wnt, hq, tb, src=cqn, src_reads=cq_reads, nk=4)
                copy_out(nT[:, hq, tb * 512:(tb + 1) * 512], b, [bt], [S.tk("nT", hq, tb)])
            tt = load_tab(c_cosM, c_sinM, pos0, tb)
            for i in range(4):
                a, at = proj_fm(wr, wrt, i, tb, src=cqn, src_reads=cq_reads, nk=4)
                b, bt = proj_fm(wr, wrt, 4 + i, tb, src=cqn, src_reads=cq_reads, nk=4)
                rope_out(nT[:, 8 + i, tb * 512:(tb + 1) * 512], a, at, b, bt, tt, [S.tk("nT", 8 + i, tb)])
        sc = 192.0 ** -0.5
        for hg in range(2):
            for h4 in range(4):
                h = hg * 4 + h4
                half = h % 2
                pair = h // 2
                kt_ = S.tk("bx", 4, "k")
                vt_ = S.tk("bx", 5, "v")
                for kb in range(nkeys // 512):
                    b, bt = next_bank(0, 4)
                    mm_group(b, bt, [(wkv2[:, c, h * 128:(h + 1) * 128], latall[:, c, kb * 512:(kb + 1) * 512]) for c in range(2)], reads=[wkt] + lat_reads)
                    copy_out(knT[:, kb * 512:(kb + 1) * 512], b, [bt], [kt_])
                    b, bt = next_bank(0, 4)
                    for t4 in range(4):
                        t_ = kb * 4 + t4
                        mm_group(b[:, t4 * 128:(t4 + 1) * 128], bt,
                                 [(latall[:, c, t_ * 128:(t_ + 1) * 128], wkv2[:, c, 1024 + h * 128: 1024 + (h + 1) * 128]) for c in range(2)], reads=[wkt] + lat_reads)
                    copy_out(X[5][:, kb * 512:(kb + 1) * 512], b, [bt], [vt_])
                for qb in range(2):
                    state["acc"] = state.get("acc", 0) + 1
                    ob, obt = banks[4 + state["acc"] % 2], bank_tok[4 + state["acc"] % 2]
                    sb_, sbt = banks[6 + state["acc"] % 2], bank_tok[6 + state["acc"] % 2]
                    qsl0 = qb * 512
                    gmax = 8 * p + 4 * qb + 4
                    for g in range(gmax):
                        jd = g - (8 * p + 4 * qb)
                        c0 = 128 * jd if jd > 0 else 0
                        cols = slice(c0, 512)
                        qcols = slice(qsl0 + c0, qsl0 + 512)
                        ksl = slice(g * 128, (g + 1) * 128)
                        st_, stt_ = next_bank(0, 4)
                        mm_group(st_[:, cols], stt_,
                                 [(knT[:, ksl], nT[:, h, qcols]),
                                  (latall[64 * half:64 * half + 64, 2, ksl], nT[64 * half:64 * half + 64, 8 + pair, qcols])],
                                 reads=[kt_, S.tk("nT", h, qb), S.tk("nT", 8 + pair, qb)] + lat_reads)
                        state["ptr"] += 1
                        pr = state["ptr"] % 4
                        ptt = S.tk("pt", pr)
                        S.add("act", I("activation", out=pt[:, pr, cols], in_=st_[:, cols], func=AF.Exp, scale=sc),
                              reads=[stt_], writes=[ptt])
                        if jd >= 0:
                            S.add("dve", I("tensor_tensor", out=pt[:, pr, c0:c0 + 128], in0=pt[:, pr, c0:c0 + 128], in1=trib, op=ALU.mult),
                                  reads=[ptt] + CT, writes=[ptt])
                        mm_group(ob[:, cols], obt, [(vh[:, g, :], pt[:, pr, cols])], reads=[vt_, ptt], start=(g == 0), stop=(g == gmax - 1))
                        mm_group(sb_[:, cols], sbt, [(ones[:, :], pt[:, pr, cols])], reads=[ptt] + CT, start=(g == 0), stop=(g == gmax - 1))
                    state["fsr"] += 1
                    r = state["fsr"] % 4
                    ft = S.tk("fs", r)
                    S.add("dve", I("reciprocal", out=fs[:, r, :], in_=sb_), reads=[sbt], writes=[ft])
                    S.add("dve", I("tensor_tensor", out=ycT[:, h4, qb * 512:(qb + 1) * 512], in0=ob, in1=fs[:, r, :], op=ALU.mult),
                          reads=[obt, ft], writes=[S.tk("bx", 0, h4, qb)])
            out_proj(l, w_out, 1024 + hg * 512, lambda c, tb: ycT[:, c, tb * 512:(tb + 1) * 512], lambda tb: [S.tk("bx", 0, c, tb) for c in range(4)])

    def xattn(l, p):
        norm_h_to_n(l, V_XA)
        S.reset_unit("bx")
        S.reset_unit("pt")
        S.reset_unit("wkvb")
        memn = v16(wkvb)
        xq = bx[:, 0:4, :].rearrange("p u (c t) -> p (u c) t", c=4)
        kTm = X[4].rearrange("p (c m) -> p c m", m=256)
        vm = X[5].rearrange("p (t n) -> p t n", t=2)
        memr = memT.rearrange("(c p) m -> p c m", p=128)
        ktoks = [S.tk("bx", 4, j) for j in range(16)]
        vtoks = [S.tk("bx", 5, mt, s_) for mt in range(2) for s_ in range(8)]
        xmode = "two" if p == 0 else "three"
        if p > 0:
            S.add("sp", I("dma_start", out=X[4], in_=memkv_cache[l][:, 0, :]), reads=[S.tk("mkvc", l)], writes=ktoks, dma_key="mkvl0")
            S.add("sp", I("dma_start", out=X[5], in_=memkv_cache[l][:, 1, :]), reads=[S.tk("mkvc", l)], writes=vtoks, dma_key="mkvl1")
        else:
            mem_kv_compute(l, memn, kTm, vm, memr)
            if npass > 1:
                S.add("sp", I("dma_start", out=memkv_cache[l][:, 0, :], in_=X[4]), reads=ktoks, writes=[S.tk("mkvc", l)], dma_key="mkvs0")
                S.add("sp", I("dma_start", out=memkv_cache[l][:, 1, :], in_=X[5]), reads=ktoks + vtoks, writes=[S.tk("mkvc", l)], dma_key="mkvs1")
        xattn_rest(l, p, xq, kTm, vm, xmode)

    def mem_kv_compute(l, memn, kTm, vm, memr):
        fsv = fs.rearrange("p a b -> p (a b)").rearrange("p (c m) -> p c m", m=256)
        fst = [S.tk("fs", i) for i in range(4)]
        ssb, sst = next_bank()
        for half in range(2):
            S.add("sp", I("dma_start", out=fsv, in_=memr[:, half * 8:(half + 1) * 8, :]), writes=fst, dma_key="meml")
            for cp in range(0, 8, 2):
                state["sq"] += 1
                r = state["sq"] % 2
                sqt = S.tk("sq", r)
                S.add("act", I("activation", out=sq[:, r, :, 0:256], in_=fsv[:, cp:cp + 2, :], func=AF.Square), reads=fst, writes=[sqt])
                mm_group(ssb[:, 0:256], sst, [(ones[:, :], sq[:, r, j, 0:256]) for j in range(2)], reads=[sqt] + CT,
                         start=(half == 0 and cp == 0), stop=(half == 1 and cp == 6))
        rt = S.tk("rstd", 0)
        rsqrt_to(rstd[:, 0:256], ssb[:, 0:256], [sst] + CT, [rt], 1.0 / D_MODEL, 256)
        mnt = S.tk("wkvb", "memn")
        for half in range(2):
            S.add("sp", I("dma_start", out=fsv, in_=memr[:, half * 8:(half + 1) * 8, :]), writes=fst, dma_key="meml")
            for c in range(8):
                cc = half * 8 + c
                S.add("dve", I("scalar_tensor_tensor", out=memn[:, cc, :], in0=fsv[:, c, :], scalar=vecs[:, l, V_MEM + cc:V_MEM + cc + 1],
                                                                          in1=rstd[:, 0:256], op0=ALU.mult, op1=ALU.mult),
                      reads=fst + [rt] + CT, writes=[mnt])
        for s in range(8):
            wv, wt = load_slot(colslab(xa_wkv, l, s * 256), v16, "two")
            for j in range(2):
                b, bt = next_bank()
                mm_group(b[:, 0:256], bt, [(wv[:, kc, j * 128:(j + 1) * 128], memn[:, kc, :]) for kc in range(16)], reads=[wt, mnt])
                copy_out(kTm[:, s * 2 + j, :], b[:, 0:256], [bt], [S.tk("bx", 4, s * 2 + j)])
        for s in range(8):
            wv, wt = load_slot(colslab(xa_wkv, l, 2048 + s * 256), v16, "two")
            for mt in range(2):
                b, bt = next_bank()
                mm_group(b[:, 0:256], bt, [(memn[:, kc, mt * 128:(mt + 1) * 128], wv[:, kc, :]) for kc in range(16)], reads=[wt, mnt])
                copy_out(vm[:, mt, s * 256:(s + 1) * 256], b[:, 0:256], [bt], [S.tk("bx", 5, mt, s)])

    def xattn_rest(l, p, xq, kTm, vm, xmode):
        for s in range(8):
            wv, wt = load_slot(colslab(xa_wq, l, s * 256), v16, xmode)
            for j in range(2):
                for tb in range(2):
                    b, bt = proj_fm(wv, wt, j, tb)
                    cj = s * 2 + j
                    copy_out(xq[:, cj, tb * 512:(tb + 1) * 512], b, [bt], [S.tk("bx", cj // 4, "q", cj, tb)])
        S.reset_unit("nT")
        sc = 512.0 ** -0.5
        for h in range(4):
            for tb in range(2):
                tsl = slice(tb * 512, (tb + 1) * 512)
                ptoks = []
                for mc in range(2):
                    b, bt = next_bank()
                    mm_group(b, bt, [(kTm[:, 4 * h + c, mc * 128:(mc + 1) * 128], xq[:, 4 * h + c, tsl]) for c in range(4)],
                             reads=[S.tk("bx", 4, 4 * h + c) for c in range(4)] + [S.tk("bx", h, "q", 4 * h + c, tb) for c in range(4)])
                    state["ptr"] += 1
                    pr = state["ptr"] % 4
                    ptt = S.tk("pt", pr)
                    S.add("act", I("activation", out=pt[:, pr, :], in_=b, func=AF.Exp, scale=sc), reads=[bt], writes=[ptt])
                    ptoks.append((pr, ptt))
                sb_, sbt = next_bank()
                mm_group(sb_, sbt, [(ones[:, :], pt[:, pr, :]) for pr, _ in ptoks], reads=[t for _, t in ptoks] + CT)
                state["fsr"] += 1
                r = state["fsr"] % 4
                ft = S.tk("fs", r)
                S.add("dve", I("reciprocal", out=fs[:, r, :], in_=sb_), reads=[sbt], writes=[ft])
                for c in range(4):
                    cj = 4 * h + c
                    b, bt = next_bank()
                    mm_group(b, bt, [(vm[:, mc, cj * 128:(cj + 1) * 128], pt[:, ptoks[mc][0], :]) for mc in range(2)],
                             reads=[S.tk("bx", 5, mc, cj // 2) for mc in range(2)] + [t for _, t in ptoks])
                    S.add("dve", I("tensor_tensor", out=nT[:, cj, tsl], in0=b, in1=fs[:, r, :], op=ALU.mult),
                          reads=[bt, ft], writes=[S.tk("nT", cj, tb)])
        for s in range(8):
            wv, wt = load_slot(colslab(xa_wo, l, s * 256), v16, xmode)
            for j in range(2):
                for tb in range(2):
                    b, bt = proj_fm(wv, wt, j, tb)
                    h_accum(b, bt, s * 2 + j, tb, 1.0)

    xTr = xT.rearrange("(c p) t -> p c t", p=128)
    yTr = yT.rearrange("(c p) t -> p c t", p=128)
    outs = []
    for p in range(npass):
        for c4 in range(0, 16, 4):
            q = "sp" if (c4 // 4) % 2 == 0 else "act"
            for tb in range(2):
                S.add("sp", I("dma_start", out=hT[:, c4:c4 + 4, tb * 512:(tb + 1) * 512], in_=xTr[:, c4:c4 + 4, p * T + tb * 512: p * T + (tb + 1) * 512]),
                      writes=[S.tk("hT", c, tb) for c in range(c4, c4 + 4)], dma_key=f"xin{c4}_{tb}")
        for l in range(L):
            if "f" in phases:
                ffn(l, w_g1, w_u1, w_d1, V_FFN1)
            if any(c in phases for c in "abc"):
                norm_h_to_n(l, V_MIX)
            if "a" in phases:
                mixer_a(l, p)
            if "b" in phases:
                mixer_b(l, p)
            if "c" in phases:
                mixer_c(l, p)
            if "x" in phases:
                xattn(l, p)
            if "g" in phases:
                ffn(l, w_g2, w_u2, w_d2, V_FFN2, restart=True)
        norm_fm(hT, h_tok, 16, D_MODEL, TBS, lambda c: vecs[:, 0, V_FIN + c:V_FIN + c + 1],
                lambda c, c0, n: hT[:, c, c0:c0 + n], h_tok)
        for c4 in range(0, 16, 4):
            for tb in range(2):
                o = S.add("sp", I("dma_start", out=yTr[:, c4:c4 + 4, p * T + tb * 512: p * T + (tb + 1) * 512], in_=hT[:, c4:c4 + 4, tb * 512:(tb + 1) * 512]),
                          reads=[S.tk("hT", c, tb) for c in range(c4, c4 + 4)], dma_key=f"yout{c4}_{tb}")
                outs.append(o)
    counts = S.emit(nc, final_waits=outs[-8:])
    return nc, counts


def _col(v):
    Ld, n = v.shape
    return np.ascontiguousarray(v.reshape(Ld, n // 128, 128).transpose(2, 0, 1))


def _swap_halves(w, width):
    K, N = w.shape
    half = width // 2
    w4 = w.reshape(K, N // width, 2, half)
    return w4[:, :, ::-1, :].reshape(K, N)


def prep_inputs(inp, depth=DEPTH, seq=SEQ):
    L = depth
    f32 = np.float32
    g = {k: np.asarray(v, dtype=f32) for k, v in inp.items()}
    d = {}
    d["w_g1"] = g["ffn1_w_gate"][:L]; d["w_u1"] = g["ffn1_w_up"][:L]; d["w_d1"] = g["ffn1_w_down"][:L]
    d["w_g2"] = g["ffn2_w_gate"][:L]; d["w_u2"] = g["ffn2_w_up"][:L]; d["w_d2"] = g["ffn2_w_down"][:L]
    wi = g["w_in"][:L]
    ext = np.empty((L, D_MODEL, 5120), f32)
    for l in range(L):
        w = wi[l]
        rq = w[:, 1024:1536]; rk = w[:, 1536:2048]; kr = w[:, 3840:3904]
        ext[l] = np.concatenate([
            w[:, 0:512], w[:, 512:1024],
            rq, _swap_halves(rq, 128), rk, _swap_halves(rk, 128),
            w[:, 2048:2560], w[:, 2560:3072], w[:, 3072:3584], w[:, 3584:3840],
            kr, kr, _swap_halves(kr, 64), _swap_halves(kr, 64)], axis=1)
    d["w_in"] = ext
    uq = g["w_uq"][:L].reshape(L, 512, 8, 192)
    nope = uq[:, :, :, :128].reshape(L, 512, 1024)
    ropeq = np.ascontiguousarray(uq[:, :, :, 128:]).reshape(L, 512, 512)
    ropes = np.stack([_swap_halves(ropeq[l], 64) for l in range(L)])
    d["w_uq"] = np.ascontiguousarray(np.concatenate([nope, ropeq, ropes], axis=2))
    ukv = g["w_ukv"][:L].reshape(L, 256, 8, 256)
    d["w_ukv"] = np.ascontiguousarray(np.concatenate([ukv[:, :, :, :128].reshape(L, 256, 1024), ukv[:, :, :, 128:].reshape(L, 256, 1024)], axis=2))
    d["w_out"] = g["w_out"][:L]
    d["xa_wq"] = g["xa_wq"][:L]; d["xa_wkv"] = g["xa_wkv"][:L]; d["xa_wo"] = g["xa_wo"][:L]
    vec = np.zeros((128, L, NV), f32)
    vec[:, :, V_FFN1:V_FFN1 + 16] = _col(g["ffn1_norm"][:L])
    vec[:, :, V_MIX:V_MIX + 16] = _col(g["mix_norm"][:L])
    vec[:, :, V_XA:V_XA + 16] = _col(g["xa_norm"][:L])
    vec[:, :, V_MEM:V_MEM + 16] = _col(g["mem_norm"][:L])
    vec[:, :, V_FFN2:V_FFN2 + 16] = _col(g["ffn2_norm"][:L])
    vec[:, :, V_QN:V_QN + 4] = _col(g["q_norm"][:L])
    vec[:, :, V_KVN:V_KVN + 2] = _col(g["kv_norm"][:L])
    vec[:, :, V_GN:V_GN + 4] = _col(g["ret_gn"][:L])
    vec[:, :, V_FIN:V_FIN + 16] = _col(np.broadcast_to(g["final_norm"][None], (L, D_MODEL)))
    d["vecs"] = vec
    d["sgu_rep"] = np.ascontiguousarray(np.broadcast_to(g["sgu_norm"][:L, None, :], (L, 128, 512)))
    d["sgu_wsT"] = np.ascontiguousarray(g["sgu_w_s"][:L].transpose(0, 3, 1, 2))
    d["sgu_b"] = np.ascontiguousarray(g["sgu_b"][:L].reshape(L, 1, 512))
    i = np.arange(128)
    d["c_ident"] = np.eye(128, dtype=f32)
    d["c_ones"] = np.ones((128, 128), f32)
    d["c_tri"] = (i[:, None] <= i[None, :]).astype(f32)
    gam = 1.0 - 2.0 ** (-5.0 - np.arange(4, dtype=np.float64))
    diff = (i[None, :] - i[:, None]).astype(np.float64)
    dm = np.where(diff[None] >= 0, gam[:, None, None] ** np.maximum(diff, 0)[None], 0.0) * (128 ** -0.5)
    d["c_dmask"] = np.ascontiguousarray(dm.transpose(1, 0, 2)).astype(f32)
    xi_ = gam[:, None] ** (i[None, :] + 1.0)
    d["c_xi"] = np.ascontiguousarray(np.broadcast_to(xi_[None], (128, 4, 128))).astype(f32)
    d["c_zeta"] = np.ascontiguousarray((gam[None, :] ** (127.0 - i[:, None])) * (128 ** -0.5)).astype(f32)
    pos = np.arange(seq, dtype=np.float32)

    def tables(dim, reps):
        half = dim // 2
        inv = (np.float32(10000.0) ** (-np.arange(half, dtype=np.float32) * np.float32(2.0) / np.float32(dim))).astype(np.float32)
        ang = (pos[None, :] * inv[:, None]).astype(np.float32)
        c = np.cos(ang).astype(f32); s = np.sin(ang).astype(f32)
        cosT = np.concatenate([c, c] * reps, axis=0)
        sinT = np.concatenate([-s, s] * reps, axis=0)
        return np.ascontiguousarray(cosT), np.ascontiguousarray(sinT)
    d["c_cosR"], d["c_sinR"] = tables(128, 1)
    d["c_cosM"], d["c_sinM"] = tables(64, 2)
    return d


_CACHE = {}


def kernel(**inputs):
    x = np.asarray(inputs["x"], dtype=np.float32)
    mem = np.asarray(inputs["mem"], dtype=np.float32)
    shared = prep_inputs(inputs)
    if "nc" not in _CACHE:
        _CACHE["nc"] = build()[0]
    nc = _CACHE["nc"]
    zx = np.zeros((D_MODEL, SEQ), np.float32)
    zm = np.zeros((D_MODEL, 256), np.float32)
    in_maps = []
    for core in range(8):
        m = dict(shared)
        if core % 2:
            m["xT"] = zx
            m["memT"] = zm
        else:
            b = core // 2
            m["xT"] = np.ascontiguousarray(x[b].T)
            m["memT"] = np.ascontiguousarray(mem[b].T)
        in_maps.append(m)
    res = run_bass_kernel_spmd(nc, in_maps, core_ids=list(range(8)))
    out = np.empty((BATCH, SEQ, D_MODEL), np.float32)
    for b in range(BATCH):
        out[b] = res.results[2 * b]["yT"].T
    return out
```
